# Optimizing a Trainium2 kernel written in Bass

```python
import jax, jax.numpy as jnp
from jax import lax
import numpy as np

D_MODEL = 1024
BATCH = 16
SEQ = 2048
DEPTH = 1

SB_HEADS = 8
SB_HEAD_DIM = 64
MLA_HEADS = 8
MLA_NOPE_DIM = 64
MLA_ROPE_DIM = 32
MLA_V_DIM = 64
Q_LORA_RANK = 384
KV_LORA_RANK = 256
D_FF = 2816
CONV_WIDTH = 3
BLOCK_Q = 128
ROPE_BASE = 10000.0
EPS = 1e-6

SB_WIDTH = SB_HEADS * SB_HEAD_DIM
MLA_WIDTH = MLA_HEADS * MLA_V_DIM
MIX_WIDTH = SB_WIDTH + MLA_WIDTH
IN_COLS = 3 * SB_WIDTH + Q_LORA_RANK + KV_LORA_RANK + MLA_ROPE_DIM
MLA_QK_DIM = MLA_NOPE_DIM + MLA_ROPE_DIM

kernel_name = "hymba_style_stickbreak_mla_convffn"


def rmsnorm(x, g):
    xf = x.astype(jnp.float32)
    y = xf * lax.rsqrt(jnp.mean(xf * xf, axis=-1, keepdims=True) + EPS)
    return (y * g.astype(jnp.float32)).astype(x.dtype)


def rope_tables(positions, dim):
    half = dim // 2
    inv_freq = 1.0 / (ROPE_BASE ** (jnp.arange(half, dtype=jnp.float32) * (2.0 / dim)))
    ang = positions.astype(jnp.float32)[..., None] * inv_freq
    return jnp.cos(ang), jnp.sin(ang)


def apply_rope(x, cos, sin):
    half = x.shape[-1] // 2
    xf = x.astype(jnp.float32)
    x1, x2 = xf[..., :half], xf[..., half:]
    out = jnp.concatenate([x1 * cos - x2 * sin, x2 * cos + x1 * sin], axis=-1)
    return out.astype(x.dtype)


def stick_breaking_attention(q, k, v):
    S, Dh = q.shape[1], q.shape[-1]
    scale = Dh ** -0.5
    outs = []
    for i in range(S // BLOCK_Q):
        q0 = i * BLOCK_Q
        kv_len = q0 + BLOCK_Q
        qb = q[:, q0:kv_len]
        kb = k[:, :kv_len]
        vb = v[:, :kv_len]
        z = jnp.einsum('bqhd,bkhd->bhqk', qb, kb, preferred_element_type=jnp.float32) * scale
        t_idx = q0 + jnp.arange(BLOCK_Q)[:, None]
        s_idx = jnp.arange(kv_len)[None, :]
        visible = s_idx < t_idx
        log_beta = jax.nn.log_sigmoid(z)
        log_keep = jnp.where(visible, jax.nn.log_sigmoid(-z), 0.0)
        tail = lax.cumsum(log_keep, axis=3, reverse=True) - log_keep
        a = jnp.where(visible, jnp.exp(log_beta + tail), 0.0)
        outs.append(jnp.einsum('bhqk,bkhd->bqhd', a.astype(v.dtype), vb))
    return jnp.concatenate(outs, axis=1)


def mla_attention(q_nope, q_rope, k_nope, k_rope, v):
    S = q_nope.shape[1]
    scale = MLA_QK_DIM ** -0.5
    outs = []
    for i in range(S // BLOCK_Q):
        q0 = i * BLOCK_Q
        kv_len = q0 + BLOCK_Q
        s = (jnp.einsum('bqhd,bkhd->bhqk', q_nope[:, q0:kv_len], k_nope[:, :kv_len],
                        preferred_element_type=jnp.float32)
             + jnp.einsum('bqhr,bkr->bhqk', q_rope[:, q0:kv_len], k_rope[:, :kv_len],
                          preferred_element_type=jnp.float32)) * scale
        t_idx = q0 + jnp.arange(BLOCK_Q)[:, None]
        s_idx = jnp.arange(kv_len)[None, :]
        s = jnp.where(s_idx <= t_idx, s, -jnp.inf)
        p = jax.nn.softmax(s, axis=-1)
        outs.append(jnp.einsum('bhqk,bkhd->bqhd', p.astype(v.dtype), v[:, :kv_len]))
    return jnp.concatenate(outs, axis=1)


def causal_depthwise_conv(h, w, b):
    C = h.shape[-1]
    y = lax.conv_general_dilated(
        h, w[:, None, :].astype(h.dtype), window_strides=(1,),
        padding=[(CONV_WIDTH - 1, 0)],
        dimension_numbers=('NWC', 'WIO', 'NWC'),
        feature_group_count=C)
    return y + b.astype(h.dtype)


def setup_inputs(seed: int = 0) -> dict:
    key = jax.random.key(seed)
    ks = jax.random.split(key, 20)
    f32 = jnp.float32

    def nrm(k, shape, fan_in):
        return jax.random.normal(k, shape, f32) * (fan_in ** -0.5)

    def gain(k, shape):
        return 1.0 + 0.02 * jax.random.normal(k, shape, f32)

    x = jax.random.normal(ks[0], (BATCH, SEQ, D_MODEL), f32)
    offset = jax.random.randint(ks[1], (BATCH, 1), 0, 1024, dtype=jnp.int32)
    positions = (jnp.arange(SEQ, dtype=jnp.int32)[None, :] + offset).astype(jnp.int32)
    return {
        "x": x,
        "positions": positions,
        "g_mix": gain(ks[2], (DEPTH, D_MODEL)),
        "w_in": nrm(ks[3], (DEPTH, D_MODEL, IN_COLS), D_MODEL),
        "g_cq": gain(ks[4], (DEPTH, Q_LORA_RANK)),
        "w_uq": nrm(ks[5], (DEPTH, Q_LORA_RANK, MLA_HEADS * MLA_QK_DIM), Q_LORA_RANK),
        "g_ckv": gain(ks[6], (DEPTH, KV_LORA_RANK)),
        "w_ukv": nrm(ks[7], (DEPTH, KV_LORA_RANK, MLA_HEADS * (MLA_NOPE_DIM + MLA_V_DIM)), KV_LORA_RANK),
        "g_sb_out": gain(ks[8], (DEPTH, SB_WIDTH)),
        "g_mla_out": gain(ks[9], (DEPTH, MLA_WIDTH)),
        "w_out": nrm(ks[10], (DEPTH, MIX_WIDTH, D_MODEL), MIX_WIDTH),
        "g_ffn": gain(ks[11], (DEPTH, D_MODEL)),
        "w_up": nrm(ks[12], (DEPTH, D_MODEL, 2 * D_FF), D_MODEL),
        "conv_w": nrm(ks[13], (DEPTH, CONV_WIDTH, 2 * D_FF), CONV_WIDTH),
        "conv_b": 0.01 * jax.random.normal(ks[14], (DEPTH, 2 * D_FF), f32),
        "w_down": nrm(ks[15], (DEPTH, D_FF, D_MODEL), D_FF),
        "g_final": gain(ks[16], (D_MODEL,)),
    }


def reference(x, positions, g_mix, w_in, g_cq, w_uq, g_ckv, w_ukv, g_sb_out, g_mla_out,
              w_out, g_ffn, w_up, conv_w, conv_b, w_down, g_final):
    B, S, _ = x.shape
    cos, sin = rope_tables(positions, MLA_ROPE_DIM)
    cos_h, sin_h = cos[:, :, None, :], sin[:, :, None, :]
    split_at = np.cumsum([SB_WIDTH, SB_WIDTH, SB_WIDTH, Q_LORA_RANK, KV_LORA_RANK])

    for l in range(DEPTH):
        h = rmsnorm(x, g_mix[l])
        p = h @ w_in[l]
        q_sb, k_sb, v_sb, c_q, c_kv, k_rope = jnp.split(p, split_at, axis=-1)

        o_sb = stick_breaking_attention(
            q_sb.reshape(B, S, SB_HEADS, SB_HEAD_DIM),
            k_sb.reshape(B, S, SB_HEADS, SB_HEAD_DIM),
            v_sb.reshape(B, S, SB_HEADS, SB_HEAD_DIM)).reshape(B, S, SB_WIDTH)

        q = (rmsnorm(c_q, g_cq[l]) @ w_uq[l]).reshape(B, S, MLA_HEADS, MLA_QK_DIM)
        q_nope, q_rope = q[..., :MLA_NOPE_DIM], q[..., MLA_NOPE_DIM:]
        q_rope = apply_rope(q_rope, cos_h, sin_h)
        kv = (rmsnorm(c_kv, g_ckv[l]) @ w_ukv[l]).reshape(B, S, MLA_HEADS, MLA_NOPE_DIM + MLA_V_DIM)
        k_nope, v_mla = kv[..., :MLA_NOPE_DIM], kv[..., MLA_NOPE_DIM:]
        k_rope = apply_rope(k_rope, cos, sin)
        o_mla = mla_attention(q_nope, q_rope, k_nope, k_rope, v_mla).reshape(B, S, MLA_WIDTH)

        o = jnp.concatenate([rmsnorm(o_sb, g_sb_out[l]), rmsnorm(o_mla, g_mla_out[l])], axis=-1)
        x = x + o @ w_out[l]

        u = rmsnorm(x, g_ffn[l]) @ w_up[l]
        u = causal_depthwise_conv(u, conv_w[l], conv_b[l])
        gate, val = u[..., :D_FF], u[..., D_FF:]
        x = x + (jax.nn.silu(gate) * val) @ w_down[l]

    return rmsnorm(x, g_final)
```

```python
import contextlib
import math
import numpy as np
import ml_dtypes
import concourse.bass as bass
import concourse.mybir as mybir
from concourse.bass_utils import run_bass_kernel_spmd

F32 = mybir.dt.float32
BF16 = mybir.dt.bfloat16
I32 = mybir.dt.int32
U8 = mybir.dt.uint8
AF = mybir.ActivationFunctionType
ALU = mybir.AluOpType

D = 1024
NCORES = 8
EPS = 1e-6
DFF = 2816
NJ = DFF // 128
SB_SCALE = 64 ** -0.5
MLA_SCALE = 96 ** -0.5
TWO_PI = 2.0 * math.pi
CW1 = 6.28125
CW2 = TWO_PI - CW1
PI_SAFE = 3.1415925
PHASE_LIMIT = 4
P3_MODE = 'all'
OPLIM = None
DEBUG = False
LAST_RES = None


class Sched:
    ENGS = ('pe', 'act', 'dve', 'pool', 'sp')

    def __init__(self, nc, es):
        self.nc = nc
        self.es = es
        self.streams = {e: [] for e in self.ENGS}
        self.cnt = {e: 0 for e in self.ENGS}
        self.esem = {e: es.enter_context(nc.semaphore("sem_" + e)) for e in self.ENGS}
        self.waited = {e: {} for e in self.ENGS}
        self.bufs = {}
        self.dsem = {}
        self.semobj = {}
        self.final_tokens = []
        self.tags = {}

    def _st(self, b):
        return self.bufs.setdefault(b, {'w': [], 'r': []})

    def _deps(self, reads, writes, pwrites):
        toks = []
        for b in reads:
            st = self._st(b)
            toks += st['w']
            if isinstance(b, tuple) and b[0] == 'pb':
                toks += st['r']
        for b in writes:
            st = self._st(b)
            toks += st['w'] + st['r']
        for b in pwrites:
            st = self._st(b)
            toks += st['w'] + st['r']
        return toks

    def _update(self, tok, reads, writes, pwrites):
        for b in reads:
            self.bufs[b]['r'].append(tok)
        for b in writes:
            self.bufs[b]['w'] = [tok]
            self.bufs[b]['r'] = []
        for b in pwrites:
            st = self.bufs[b]
            st['w'].append(tok)
            if len(st['w']) > 64:
                best = {}
                for t in st['w']:
                    k = id(t[0])
                    if k not in best or best[k][1] < t[1]:
                        best[k] = t
                st['w'] = list(best.values())
            st['r'] = []

    def _waits(self, eng, toks):
        need = {}
        for (sem, val, src) in toks:
            if src == eng and eng == 'pe':
                continue
            k = id(sem)
            self.semobj[k] = sem
            if self.waited[eng].get(k, 0) >= val:
                continue
            if need.get(k, 0) < val:
                need[k] = val
        out = []
        for k, v in need.items():
            self.waited[eng][k] = v
            out.append((self.semobj[k], v))
        return out

    enabled = True

    oplimit = None

    def op(self, eng, fn, reads=(), writes=(), pwrites=()):
        if not self.enabled:
            return None
        if self.oplimit is not None:
            if self.oplimit <= 0:
                return None
            self.oplimit -= 1
        toks = self._deps(reads, writes, pwrites)
        waits = self._waits(eng, toks)
        self.cnt[eng] += 1
        tok = (self.esem[eng], self.cnt[eng], eng)
        self.streams[eng].append((waits, fn, self.esem[eng], 1))
        self._update(tok, reads, writes, pwrites)
        return tok

    def dma(self, eng, key, fn, reads=(), writes=(), pwrites=(), final=False):
        if not self.enabled:
            return None
        toks = self._deps(reads, writes, pwrites)
        if key not in self.dsem:
            self.dsem[key] = [self.es.enter_context(self.nc.semaphore("dsem_%d" % len(self.dsem))), 0]
        ent = self.dsem[key]
        if ent[1] > 0:
            toks = toks + [(ent[0], 16 * ent[1], 'dma')]
        waits = self._waits(eng, toks)
        ent[1] += 1
        tok = (ent[0], 16 * ent[1], 'dma')
        self.streams[eng].append((waits, fn, ent[0], 16))
        self._update(tok, reads, writes, pwrites)
        if final:
            self.final_tokens.append(tok)
        return tok

    def wtag(self, key, tag):
        self.tags[key] = tag

    def rtag(self, key, tag):
        assert self.tags.get(key) == tag, ("pipeline slot hazard", key, self.tags.get(key), tag)

    def barrier(self):
        toks = [(self.esem[e], self.cnt[e], e + '_b') for e in self.ENGS if self.cnt[e] > 0]
        toks += [(ent[0], 16 * ent[1], 'dma') for ent in self.dsem.values() if ent[1] > 0]
        for e in self.ENGS:
            waits = self._waits(e, [t for t in toks if not t[2].startswith(e + '_')])
            if waits:
                self.streams[e].append((waits, None, None, 0))
        self.bufs = {}

    def emit(self):
        nc = self.nc
        with nc.Block() as block:
            def replay(name):
                def f(eng):
                    for (waits, fn, sem, inc) in self.streams[name]:
                        for (s, v) in waits:
                            eng.wait_ge(s, v)
                        if fn is None:
                            continue
                        ins = fn(eng)
                        ins.then_inc(sem, inc)
                    if name == 'sp':
                        best = {}
                        for (s, v, _) in self.final_tokens:
                            k = id(s)
                            if k not in best or best[k][1] < v:
                                best[k] = (s, v)
                        for (s, v) in best.values():
                            eng.wait_ge(s, v)
                return f
            if self.streams['pe']:
                block.tensor(replay('pe'))
            if self.streams['act']:
                block.scalar(replay('act'))
            if self.streams['dve']:
                block.vector(replay('dve'))
            if self.streams['pool']:
                block.gpsimd(replay('pool'))
            block.sync(replay('sp'))


def run_pipeline(items, stages):
    n = len(items)
    maxlag = max(l for l, _ in stages)
    for t in range(n + maxlag):
        for lag, fn in stages:
            i = t - lag
            if 0 <= i < n:
                fn(items[i])


def pipeline_gen(items, stages):
    n = len(items)
    maxlag = max(l for l, _ in stages)
    for t in range(n + maxlag):
        for lag, fn in stages:
            i = t - lag
            if 0 <= i < n:
                fn(items[i])
        yield


def weave(main_gen, side_gen, every):
    k = 0
    for _ in main_gen:
        k += 1
        if side_gen is not None and k % every == 0:
            next(side_gen, None)
    if side_gen is not None:
        for _ in side_gen:
            pass


class Arena:
    def __init__(self, ap, start, limit):
        self.ap = ap
        self.off = start
        self.limit = limit
        self.n = 0

    def take(self, shape, dt):
        esz = {F32: 4, BF16: 2, I32: 4}[dt]
        free = 1
        for s in shape[1:]:
            free *= s
        nbytes = (free * esz + 63) // 64 * 64
        assert self.off + nbytes <= self.limit, ("arena overflow", self.off, nbytes, self.limit)
        v = self.ap[:, self.off:self.off + free * esz].bitcast(dt)
        self.off += nbytes
        if len(shape) == 3:
            v = v.rearrange("p (a b) -> p a b", a=shape[1])
        elif len(shape) == 4:
            v = v.rearrange("p (a b c) -> p a b c", a=shape[1], b=shape[2])
        if shape[0] < 128:
            v = v[0:shape[0]]
        return v


def build_program(S, NSEQ):
    NB = S // 128
    NT = S // 512
    assert S % 512 == 0
    nc = bass.Bass("TRN2", target_bir_lowering=False)

    def din(name, shape, dt):
        return nc.dram_tensor(name, list(shape), dt, kind="ExternalInput").ap()

    x_d = din("x", [NSEQ, S, D], F32)
    pos_d = din("pos", [NSEQ, 128, NB], I32)
    win_d = din("w_in", [D, 2208], F32)
    wuq_d = din("w_uq", [384, 768], F32)
    wukv_d = din("w_ukv", [256, 1024], F32)
    wout_d = din("w_out", [D, D], F32)
    wup_d = din("w_up", [D, 2 * DFF], F32)
    wdn_d = din("w_down", [DFF, D], F32)
    gmix_d = din("g_mix", [128, D], F32)
    gc_d = din("g_c", [128, 640], F32)
    go_d = din("g_o", [128, D], F32)
    gffn_d = din("g_ffn", [128, D], F32)
    gfin_d = din("g_final", [128, D], F32)
    cw_d = din("cw", [128, 2 * NJ * 4], F32)
    cb_d = din("cbf", [128, 7 * 128], BF16)
    invf_d = din("invf", [128, 16], F32)
    out_d = nc.dram_tensor("out", [NSEQ, S, D], F32, kind="ExternalOutput").ap()
    dbg_d = nc.dram_tensor("dbg", [NSEQ, 128, NB, D], BF16, kind="ExternalOutput").ap() if DEBUG else None

    win_v = win_d.rearrange("(kc p) n -> p kc n", p=128)
    wuq_v = wuq_d.rearrange("(kc p) n -> p kc n", p=128)
    wukv_v = wukv_d.rearrange("(kc p) n -> p kc n", p=128)
    wout_v = wout_d.rearrange("(kc p) n -> p kc n", p=128)
    wup_v = wup_d.rearrange("(kc p) n -> p kc n", p=128)
    wdn_v = wdn_d.rearrange("(j p) n -> p j n", p=128)

    with contextlib.ExitStack() as es:
        S_ = Sched(nc, es)
        op, dma = S_.op, S_.dma

        def sbt(name, shape, dt):
            return es.enter_context(nc.sbuf_tensor(name, list(shape), dt))

        ARENA_BYTES = 172 * 1024
        arena = sbt("arena", [128, ARENA_BYTES], U8)
        o_tm = sbt("o_tm", [128, NB, D], BF16)
        cbf = sbt("cbf_sb", [128, 7 * 128], BF16)
        zer = sbt("zer", [128, 512], BF16)
        invf = sbt("invf_sb", [128, 16], F32)
        cw = sbt("cw_sb", [128, 2 * NJ * 4], F32)
        cscale = sbt("cscale", [128, 2], F32)
        stat = sbt("stat", [128, 64], F32)
        ident = cbf[:, 0:128]
        tri = cbf[:, 128:256]
        ones = cbf[:, 256:384]
        mstrict = cbf[:, 384:512]
        mincl = cbf[:, 512:640]
        ntri = cbf[:, 640:768]
        nones = cbf[:, 768:896]

        banks = [es.enter_context(nc.psum_tensor("pb%d" % i, [128, 512], F32)) for i in range(8)]

        def bk(i):
            return ('pb', i)

        dma('sp', 'c0', lambda e: e.dma_start(out=cbf[:], in_=cb_d[:, :]), writes=['cbf'])
        dma('sp', 'c1', lambda e: e.dma_start(out=invf[:], in_=invf_d[:, :]), writes=['invf'])
        dma('sp', 'c2', lambda e: e.dma_start(out=cw[:], in_=cw_d[:, :]), writes=['cw'])
        op('pool', lambda e: e.memset(zer[:], 0.0), writes=['zer'])
        op('pool', lambda e: e.memset(cscale[:, 0:1], 1.0 / 384.0), pwrites=['cscale'])
        op('pool', lambda e: e.memset(cscale[:, 1:2], 1.0 / 256.0), pwrites=['cscale'])

        evac_flip = [0]

        def evac(out_ap, in_ap, reads, writes=(), pwrites=(), scale=None):
            evac_flip[0] ^= 1
            if evac_flip[0]:
                if scale is None:
                    op('act', lambda e: e.activation(out=out_ap, in_=in_ap, func=AF.Copy), reads=reads, writes=writes, pwrites=pwrites)
                else:
                    op('act', lambda e: e.activation(out=out_ap, in_=in_ap, func=AF.Copy, scale=scale), reads=reads, writes=writes, pwrites=pwrites)
            else:
                if scale is None:
                    op('dve', lambda e: e.tensor_copy(out_ap, in_ap), reads=reads, writes=writes, pwrites=pwrites)
                else:
                    op('dve', lambda e: e.tensor_scalar(out=out_ap, in0=in_ap, scalar1=scale, scalar2=None, op0=ALU.mult), reads=reads, writes=writes, pwrites=pwrites)

        stat_ctr = [0]

        def stat_cols(n):
            c = stat_ctr[0]
            if c + n > 64:
                c = 0
            stat_ctr[0] = c + n
            return c

        def rms_rstd(src_list, inv_n, junk, junk_key):
            c = stat_cols(4)
            key = ('stat', c)
            assert len(src_list) == 1
            ap, rk = src_list[0]
            op('act', lambda e: e.activation(out=junk[:, 0:ap.shape[1]], in_=ap, func=AF.Square, accum_out=stat[:, c:c + 1]),
               reads=rk, writes=[junk_key, key])
            op('act', lambda e: e.activation(out=stat[:, c + 2:c + 3], in_=stat[:, c:c + 1], func=AF.Sqrt, scale=inv_n, bias=EPS),
               reads=[key], writes=[key])
            op('dve', lambda e: e.reciprocal(stat[:, c + 3:c + 4], stat[:, c + 2:c + 3]), reads=[key], writes=[key])
            return stat[:, c + 3:c + 4], key

        R1_END = 90 * 1024

        for b in range(NSEQ):
            A1 = Arena(arena, 0, R1_END)
            QT = A1.take([128, 4, S], BF16)
            KTz_flat = A1.take([128, 8 * S], BF16)
            KTz = KTz_flat.rearrange("p (a b) -> p a b", a=8)
            KT = KTz_flat[:, 0:4 * S].rearrange("p (a b) -> p a b", a=4)
            Vs_flat = A1.take([128, NB * 512], BF16)
            Vs = Vs_flat.rearrange("p (a b) -> p a b", a=NB)
            cT = A1.take([128, 5, S], BF16)
            kr_tm = A1.take([128, NB, 32], F32)
            cos_t = A1.take([128, NB, 16], F32)
            sin_t = A1.take([128, NB, 16], F32)
            kro = A1.take([128, NB, 32], BF16)

            A2 = Arena(arena, R1_END, ARENA_BYTES)
            win = A2.take([128, 8, 2208], BF16)
            gmix = A2.take([128, D], F32)
            gcb = A2.take([128, 640], F32)
            xt = [A2.take([128, D], F32) for _ in range(2)]
            xn = [A2.take([128, D], BF16) for _ in range(2)]
            hT = [A2.take([128, 8, 512], BF16) for _ in range(2)]
            junk = A2.take([128, D], BF16)
            cn = [A2.take([128, 640], BF16) for _ in range(2)]
            posi = A2.take([128, NB], I32)
            posf = A2.take([128, NB], F32)
            ang = A2.take([128, NB, 16], F32)
            rt = [A2.take([128, NB, 16], F32) for _ in range(4)]
            ki = A2.take([128, NB, 16], I32)
            sc = [A2.take([128, 8], F32) for _ in range(2)]

            for kc in range(8):
                dma('pool', ('win', kc), lambda e, kc=kc: e.dma_start(out=win[:, kc, :], in_=win_v[:, kc, :]), writes=[('win', kc)])
            dma('sp', 'gmix', lambda e: e.dma_start(out=gmix, in_=gmix_d[:, :]), writes=['gmix'])
            dma('sp', 'gcb', lambda e: e.dma_start(out=gcb, in_=gc_d[:, :]), writes=['gcb'])
            dma('sp', 'posi', lambda e, b=b: e.dma_start(out=posi, in_=pos_d[b, :, :]), writes=['posi'])

            op('pool', lambda e: e.memset(KTz_flat, 0.0), writes=[('KT', c_) for c_ in range(4)])

            op('dve', lambda e: e.tensor_copy(posf, posi), reads=['posi'], writes=['posf'])
            op('dve', lambda e: e.tensor_tensor(out=ang, in0=invf[:].unsqueeze(1).to_broadcast([128, NB, 16]),
                                                in1=posf.unsqueeze(2).to_broadcast([128, NB, 16]), op=ALU.mult),
               reads=['posf', 'invf'], writes=['ang'])
            for which, shift, dst in ((0, 0.0, sin_t), (1, 0.25, cos_t)):
                t0, t1 = rt[0], rt[1]
                op('dve', lambda e, shift=shift: e.tensor_scalar(out=t0, in0=ang, scalar1=1.0 / TWO_PI, scalar2=shift, op0=ALU.mult, op1=ALU.add),
                   reads=['ang'], writes=['rt0'])
                op('dve', lambda e: e.tensor_copy(ki, t0), reads=['rt0'], writes=['ki'])
                op('dve', lambda e: e.tensor_copy(t0, ki), reads=['ki'], writes=['rt0'])
                op('dve', lambda e: e.scalar_tensor_tensor(out=t1, in0=t0, scalar=-CW1, in1=ang, op0=ALU.mult, op1=ALU.add),
                   reads=['rt0', 'ang'], writes=['rt1'])
                op('dve', lambda e: e.scalar_tensor_tensor(out=t1, in0=t0, scalar=-CW2, in1=t1, op0=ALU.mult, op1=ALU.add),
                   reads=['rt0', 'rt1'], writes=['rt1'])
                if which == 1:
                    op('dve', lambda e: e.tensor_scalar(out=t1, in0=t1, scalar1=math.pi / 2, scalar2=None, op0=ALU.add),
                       reads=['rt1'], writes=['rt1'])
                op('dve', lambda e: e.tensor_scalar(out=t1, in0=t1, scalar1=PI_SAFE, scalar2=-PI_SAFE, op0=ALU.min, op1=ALU.max),
                   reads=['rt1'], writes=['rt1'])
                op('act', lambda e, dst=dst: e.activation(out=dst, in_=t1, func=AF.Sin), reads=['rt1'], writes=[('trig', which)])

            pbi = [0]

            def nextbank(lo=0, hi=4):
                i = lo + pbi[0] % (hi - lo)
                pbi[0] += 1
                return i

            def p1_norm(T, blk):
                tb = T * 4 + blk
                s = tb % 2
                dma('sp', ('xt', s), lambda e, b=b: e.dma_start(out=xt[s], in_=x_d[b, tb * 128:(tb + 1) * 128, :]),
                    writes=[('xt', s)])
                rstd, rkey = rms_rstd([(xt[s], [('xt', s)])], 1.0 / D, junk, 'junk')
                op('dve', lambda e: e.scalar_tensor_tensor(out=xn[s], in0=xt[s], scalar=rstd, in1=gmix, op0=ALU.mult, op1=ALU.mult),
                   reads=[('xt', s), rkey, 'gmix'], writes=[('xn', s)])
                S_.wtag(('xn', s), tb)

            def p1_tr(T, blk):
                tb = T * 4 + blk
                s = tb % 2
                hs = T % 2
                pi = 6 + (tb % 2)
                ptr = banks[pi][:].bitcast(BF16)
                S_.rtag(('xn', s), tb)

                def tr8(e):
                    for kc in range(8):
                        ins = e.transpose(ptr[:, kc * 128:(kc + 1) * 128], xn[s][:, kc * 128:(kc + 1) * 128], ident)
                    return ins
                op('pe', tr8, reads=[('xn', s), 'cbf'], writes=[bk(pi)])
                evac(hT[hs][:, :, blk * 128:(blk + 1) * 128], ptr.rearrange("p (a b) -> p a b", a=8),
                     reads=[bk(pi)], pwrites=[('hT', hs)])

            qkb = [0]

            def p1_qk(T, oc):
                hs = T % 2
                hTs, hkey = hT[hs], ('hT', hs)
                bi = 4 + qkb[0] % 2
                qkb[0] += 1

                def mmqk(e):
                    for kc in range(8):
                        ins = e.matmul(banks[bi][:], lhsT=win[:, kc, oc * 128:(oc + 1) * 128], rhs=hTs[:, kc, :],
                                       start=(kc == 0), stop=(kc == 7))
                    return ins
                op('pe', mmqk, reads=[hkey] + [('win', kc) for kc in range(8)], writes=[bk(bi)])
                if oc < 4:
                    evac(QT[:, oc, T * 512:(T + 1) * 512], banks[bi][:], reads=[bk(bi)], pwrites=[('QT', oc)], scale=SB_SCALE)
                else:
                    c_ = oc - 4
                    evac(KTz[0:64, 2 * c_, T * 512:(T + 1) * 512], banks[bi][0:64, :], reads=[bk(bi)], pwrites=[('KT', c_)])
                    evac(KTz[64:128, 2 * c_ + 1, T * 512:(T + 1) * 512], banks[bi][64:128, :], reads=[bk(bi)], pwrites=[('KT', c_)])

            def p1_v(T, blk):
                tb = T * 4 + blk
                hs = T % 2
                hTs, hkey = hT[hs], ('hT', hs)
                bi = 4 + qkb[0] % 2
                qkb[0] += 1

                def mmv(e):
                    for kc in range(8):
                        ins = e.matmul(banks[bi][:], lhsT=hTs[:, kc, blk * 128:(blk + 1) * 128], rhs=win[:, kc, 1024:1536],
                                       start=(kc == 0), stop=(kc == 7))
                    return ins
                op('pe', mmv, reads=[hkey] + [('win', kc) for kc in range(8)], writes=[bk(bi)])
                evac(Vs[:, tb, :], banks[bi][:], reads=[bk(bi)], writes=[('V', tb)])

            def p1_c_mm(T, blk):
                tb = T * 4 + blk
                hs = T % 2
                hTs, hkey = hT[hs], ('hT', hs)
                ba, bb = (0, 1) if blk % 2 == 0 else (2, 3)

                def mmc(e):
                    for kc in range(8):
                        e.matmul(banks[ba][:, 0:416], lhsT=hTs[:, kc, blk * 128:(blk + 1) * 128], rhs=win[:, kc, 1536:1952],
                                 start=(kc == 0), stop=(kc == 7))
                    for kc in range(8):
                        ins = e.matmul(banks[bb][:, 0:256], lhsT=hTs[:, kc, blk * 128:(blk + 1) * 128], rhs=win[:, kc, 1952:2208],
                                       start=(kc == 0), stop=(kc == 7))
                    return ins
                op('pe', mmc, reads=[hkey] + [('win', kc) for kc in range(8)], writes=[bk(ba), bk(bb)])
                cs = tb % 2
                sck = ('sc', cs)
                scs = sc[cs]
                op('act', lambda e: e.activation(out=junk[:, 0:384], in_=banks[ba][:, 0:384], func=AF.Square, accum_out=scs[:, 0:1]),
                   reads=[bk(ba)], writes=['junk', sck])
                op('act', lambda e: e.activation(out=junk[:, 0:256], in_=banks[bb][:, 0:256], func=AF.Square, accum_out=scs[:, 1:2]),
                   reads=[bk(bb)], writes=['junk'], pwrites=[sck])
                op('act', lambda e: e.activation(out=kr_tm[:, tb, :], in_=banks[ba][:, 384:416], func=AF.Copy),
                   reads=[bk(ba)], pwrites=['kr_tm'])
                op('dve', lambda e: e.tensor_tensor(out=scs[:, 2:4], in0=scs[:, 0:2], in1=cscale[:, 0:2], op=ALU.mult),
                   reads=[sck, 'cscale'], writes=[sck])
                op('act', lambda e: e.activation(out=scs[:, 4:6], in_=scs[:, 2:4], func=AF.Sqrt, bias=EPS), reads=[sck], writes=[sck])
                op('dve', lambda e: e.reciprocal(scs[:, 6:8], scs[:, 4:6]), reads=[sck], writes=[sck])
                op('dve', lambda e: e.scalar_tensor_tensor(out=cn[cs][:, 0:384], in0=banks[ba][:, 0:384], scalar=scs[:, 6:7],
                                                           in1=gcb[:, 0:384], op0=ALU.mult, op1=ALU.mult),
                   reads=[bk(ba), sck, 'gcb'], writes=[('cn', cs)])
                op('dve', lambda e: e.scalar_tensor_tensor(out=cn[cs][:, 384:640], in0=banks[bb][:, 0:256], scalar=scs[:, 7:8],
                                                           in1=gcb[:, 384:640], op0=ALU.mult, op1=ALU.mult),
                   reads=[bk(bb), sck, 'gcb'], pwrites=[('cn', cs)])
                S_.wtag(('cn', cs), tb)

            def p1_c_tr(T, blk):
                tb = T * 4 + blk
                cs = tb % 2
                pi = 6 + (tb % 2)
                ptr = banks[pi][:].bitcast(BF16)
                S_.rtag(('cn', cs), tb)

                def tr5(e):
                    for i in range(5):
                        ins = e.transpose(ptr[:, i * 128:(i + 1) * 128], cn[cs][:, i * 128:(i + 1) * 128], ident)
                    return ins
                op('pe', tr5, reads=[('cn', cs), 'cbf'], writes=[bk(pi)])
                evac(cT[:, :, tb * 128:(tb + 1) * 128], ptr[:, 0:640].rearrange("p (a b) -> p a b", a=5),
                     reads=[bk(pi)], writes=[('cT', tb)])

            p1_norm(0, 0)
            for blk in range(4):
                if blk + 1 < 4:
                    p1_norm(0, blk + 1)
                p1_tr(0, blk)
            for T in range(NT):
                nxt = (T + 1 < NT)
                p1_c_mm(T, 0)
                p1_c_mm(T, 1)
                if nxt:
                    p1_norm(T + 1, 0)
                for oc in range(0, 4):
                    p1_qk(T, oc)
                p1_c_tr(T, 0)
                p1_c_tr(T, 1)
                p1_c_mm(T, 2)
                p1_c_mm(T, 3)
                if nxt:
                    p1_tr(T + 1, 0)
                    p1_norm(T + 1, 1)
                for oc in range(4, 8):
                    p1_qk(T, oc)
                p1_c_tr(T, 2)
                p1_c_tr(T, 3)
                if nxt:
                    p1_tr(T + 1, 1)
                    p1_norm(T + 1, 2)
                for blk in range(4):
                    p1_v(T, blk)
                    if nxt and blk == 1:
                        p1_tr(T + 1, 2)
                        p1_norm(T + 1, 3)
                if nxt:
                    p1_tr(T + 1, 3)
            x1v, x2v = kr_tm[:, :, 0:16], kr_tm[:, :, 16:32]
            op('dve', lambda e: e.tensor_tensor(out=rt[0], in0=x1v, in1=cos_t, op=ALU.mult), reads=['kr_tm', ('trig', 1)], writes=['rt0'])
            op('dve', lambda e: e.tensor_tensor(out=rt[1], in0=x2v, in1=sin_t, op=ALU.mult), reads=['kr_tm', ('trig', 0)], writes=['rt1'])
            op('dve', lambda e: e.tensor_tensor(out=kro[:, :, 0:16], in0=rt[0], in1=rt[1], op=ALU.subtract), reads=['rt0', 'rt1'], pwrites=['kro'])
            op('dve', lambda e: e.tensor_tensor(out=rt[2], in0=x2v, in1=cos_t, op=ALU.mult), reads=['kr_tm', ('trig', 1)], writes=['rt2'])
            op('dve', lambda e: e.tensor_tensor(out=rt[3], in0=x1v, in1=sin_t, op=ALU.mult), reads=['kr_tm', ('trig', 0)], writes=['rt3'])
            op('dve', lambda e: e.tensor_tensor(out=kro[:, :, 16:32], in0=rt[2], in1=rt[3], op=ALU.add), reads=['rt2', 'rt3'], pwrites=['kro'])

            S_.barrier()
            if PHASE_LIMIT < 2:
                S_.enabled = False

            A3 = Arena(arena, R1_END, ARENA_BYTES)
            e_sb = [A3.take([128, 512], F32) for _ in range(3)]
            sp_sb = [A3.take([128, 512], BF16) for _ in range(3)]
            a_sb = [A3.take([128, 512], BF16) for _ in range(3)]
            LKSb = [A3.take([128, S], BF16) for _ in range(2)]
            wuq = A3.take([128, 3, 768], BF16)
            wukv = A3.take([128, 2, 1024], BF16)
            qtm = [A3.take([128, 4, 128], BF16) for _ in range(2)]
            ktm = [A3.take([128, 4, 128], BF16) for _ in range(2)]
            rq = [A3.take([128, 4, 16], F32) for _ in range(4)]
            rec = [A3.take([128, 8], F32) for _ in range(4)]
            mpc = [0]
            qr = A3.take([128, 4, 32], F32)
            QTm2 = A3.take([128, 4, S], BF16)
            KTm2 = A3.take([128, 4, S], BF16)
            Vm2_flat = A3.take([128, NB * 260], BF16)
            Vm2 = Vm2_flat.rearrange("p (a h d) -> p a h d", a=NB, h=4)

            for kc in range(3):
                dma('pool', ('wuq', kc), lambda e, kc=kc: e.dma_start(out=wuq[:, kc, :], in_=wuq_v[:, kc, :]), writes=[('wuq', kc)])
            for kc in range(2):
                dma('pool', ('wukv', kc), lambda e, kc=kc: e.dma_start(out=wukv[:, kc, :], in_=wukv_v[:, kc, :]), writes=[('wukv', kc)])

            def pieces():
                lst = []
                for kb in range(NB - 1, -1, -1):
                    for c in range(NT):
                        lo = max(512 * c, kb * 128)
                        hi = 512 * c + 512
                        if hi > lo:
                            lst.append((kb, c, lo, hi))
                return lst

            QPB = 8
            n_ob = (NB + QPB - 1) // QPB
            assert n_ob <= 2
            pcs = pieces()
            sb_items = []
            for h in range(8):
                for idx, (kb, c, lo, hi) in enumerate(pcs):
                    sb_items.append(dict(h=h, kb=kb, c=c, lo=lo, hi=hi, p=len(sb_items), first=(idx == 0), last=(idx == len(pcs) - 1)))

            def sb_zmm(e, bank_ap, h, kb, lo, hi, start, stop):
                return e.matmul(bank_ap, lhsT=KTz[:, h, kb * 128:(kb + 1) * 128], rhs=QT[:, h // 2, lo:hi], start=start, stop=stop)

            def sb_s0(it):
                h, kb, c, lo, hi, p = it['h'], it['kb'], it['c'], it['lo'], it['hi'], it['p']
                n = hi - lo
                hp = h % 2
                ch = h // 2
                if it['first']:
                    op('pool', lambda e: e.memset(LKSb[hp], 0.0), writes=[('LKSb', hp, cc) for cc in range(NT)])
                zb = p % 2
                op('pe', lambda e: sb_zmm(e, banks[zb][:, 0:n], h, kb, lo, hi, True, True),
                   reads=[('QT', ch), ('KT', ch)], writes=[bk(zb)])
                S_.wtag(bk(zb), ('z', p))

            def sb_s1(it):
                kb, lo, hi, p = it['kb'], it['lo'], it['hi'], it['p']
                n = hi - lo
                zb, es_ = p % 2, p % 3
                S_.rtag(bk(zb), ('z', p))
                op('act', lambda e: e.activation(out=e_sb[es_][:, 0:n], in_=banks[zb][:, 0:n], func=AF.Exp),
                   reads=[bk(zb)], writes=[('e', es_)])
                op('act', lambda e: e.activation(out=sp_sb[es_][:, 0:n], in_=e_sb[es_][:, 0:n], func=AF.Ln, bias=1.0),
                   reads=[('e', es_)], writes=[('sp', es_)])
                if lo == kb * 128:
                    op('pool', lambda e: e.tensor_tensor(out=sp_sb[es_][:, 0:128], in0=sp_sb[es_][:, 0:128], in1=mstrict, op=ALU.mult),
                       reads=[('sp', es_), 'cbf'], writes=[('sp', es_)])
                S_.wtag(('sp', es_), p)

            def sb_s2(it):
                h, kb, c, lo, hi, p = it['h'], it['kb'], it['c'], it['lo'], it['hi'], it['p']
                n = hi - lo
                hp = h % 2
                ch = h // 2
                cbk, ss_ = 2 + p % 2, p % 3
                diag = (lo == kb * 128)
                has_lks = (n > 128) or (not diag)
                S_.rtag(('sp', ss_), p)

                def mmc2(e):
                    e.matmul(banks[cbk][:, 0:n], lhsT=ntri, rhs=sp_sb[ss_][:, 0:n], start=True, stop=False)
                    if has_lks:
                        e.matmul(banks[cbk][:, 0:n], lhsT=nones, rhs=LKSb[hp][:, lo:hi], start=False, stop=False)
                    return sb_zmm(e, banks[cbk][:, 0:n], h, kb, lo, hi, False, True)
                op('pe', mmc2, reads=[('sp', ss_), 'cbf', ('QT', ch), ('KT', ch)] + ([('LKSb', hp, c)] if has_lks else []), writes=[bk(cbk)])
                S_.wtag(bk(cbk), ('c', p))
                if kb > 0:
                    op('dve', lambda e: e.tensor_tensor(out=LKSb[hp][:, lo:hi], in0=LKSb[hp][:, lo:hi], in1=sp_sb[ss_][:, 0:n], op=ALU.add),
                       reads=[('sp', ss_)], writes=[('LKSb', hp, c)])

            def sb_s3(it):
                kb, lo, hi, p = it['kb'], it['lo'], it['hi'], it['p']
                n = hi - lo
                cbk, as_ = 2 + p % 2, p % 3
                S_.rtag(bk(cbk), ('c', p))
                op('act', lambda e: e.activation(out=a_sb[as_][:, 0:n], in_=banks[cbk][:, 0:n], func=AF.Exp),
                   reads=[bk(cbk)], writes=[('a', as_)])
                if lo == kb * 128:
                    op('pool', lambda e: e.tensor_tensor(out=a_sb[as_][:, 0:128], in0=a_sb[as_][:, 0:128], in1=mstrict, op=ALU.mult),
                       reads=[('a', as_), 'cbf'], writes=[('a', as_)])
                S_.wtag(('a', as_), p)

            def sb_s4(it):
                h, kb, lo, hi, p = it['h'], it['kb'], it['lo'], it['hi'], it['p']
                as_ = p % 3
                ob0 = 4
                obanks = [ob0 + i for i in range(n_ob)]
                S_.rtag(('a', as_), p)
                if it['first']:
                    def zero_o(e):
                        for bi in obanks:
                            ins = e.matmul(banks[bi][:], lhsT=zer[:, 0:128], rhs=zer[:, :], start=True, stop=False, skip_group_check=True)
                        return ins
                    op('pe', zero_o, reads=['zer'], writes=[bk(bi) for bi in obanks])

                def mmav(e):
                    for qb in range(lo // 128, hi // 128):
                        bi = ob0 + qb // QPB
                        sl = (qb % QPB) * 64
                        last = (kb == 0) and (qb % QPB == QPB - 1 or qb == NB - 1)
                        ins = e.matmul(banks[bi][:, sl:sl + 64], lhsT=a_sb[as_][:, qb * 128 - lo:qb * 128 - lo + 128],
                                       rhs=Vs[:, kb, h * 64:(h + 1) * 64], start=False, stop=last, skip_group_check=True)
                    return ins
                obs = sorted(set(ob0 + qb // QPB for qb in range(lo // 128, hi // 128)))
                op('pe', mmav, reads=[('a', as_), ('V', kb)], pwrites=[bk(bi) for bi in obs])
                if it['last']:
                    for i, bi in enumerate(obanks):
                        nq = min(QPB, NB - i * QPB)
                        evac(o_tm[:, i * QPB:i * QPB + nq, h * 64:(h + 1) * 64],
                             banks[bi][:, 0:nq * 64].rearrange("p (a b) -> p a b", a=nq),
                             reads=[bk(bi)], pwrites=[('o', i)])

            MQ = 7
            n_mb = (NB + MQ - 1) // MQ
            bufA = dict(QT=QT, KT=KT, V=Vs[:, :, 0:260].rearrange("p a (h d) -> p a h d", h=4),
                        qk=lambda hl: ('QT', hl), kk=lambda hl: ('KT', hl // 2), vk=lambda tb: ('V', tb))
            bufB = dict(QT=QTm2, KT=KTm2, V=Vm2,
                        qk=lambda hl: ('QTb', hl), kk=lambda hl: ('KTb', hl), vk=lambda tb: ('Vb', tb))

            def mla_proj_gen(g, bufs, bq, bkv, bt):
                QTd, KTd, Vd = bufs['QT'], bufs['KT'], bufs['V']

                def step_A(tb):
                    def mmq(e):
                        for kc in range(3):
                            ins = e.matmul(banks[bq][:, 0:384], lhsT=cT[:, kc, tb * 128:(tb + 1) * 128], rhs=wuq[:, kc, g * 384:(g + 1) * 384],
                                           start=(kc == 0), stop=(kc == 2))
                        return ins
                    op('pe', mmq, reads=[('cT', tb)] + [('wuq', kc) for kc in range(3)], writes=[bk(bq)])

                    def mmkv(e):
                        for kc in range(2):
                            ins = e.matmul(banks[bkv][:, :], lhsT=cT[:, 3 + kc, tb * 128:(tb + 1) * 128], rhs=wukv[:, kc, g * 512:(g + 1) * 512],
                                           start=(kc == 0), stop=(kc == 1))
                        return ins
                    op('pe', mmkv, reads=[('cT', tb)] + [('wukv', kc) for kc in range(2)], writes=[bk(bkv)])

                def step_B(tb):
                    qs = tb % 2
                    qk_, kk_ = ('qtm', qs), ('ktm', qs)
                    qv = banks[bq][:, 0:384].rearrange("p (h d) -> p h d", h=4)
                    kvv = banks[bkv][:, :].rearrange("p (h d) -> p h d", h=4)
                    cosb = cos_t[:, tb, :].unsqueeze(1).to_broadcast([128, 4, 16])
                    sinb = sin_t[:, tb, :].unsqueeze(1).to_broadcast([128, 4, 16])
                    op('act', lambda e: e.activation(out=qr, in_=qv[:, :, 64:96], func=AF.Copy), reads=[bk(bq)], writes=['qr'])
                    op('act', lambda e: e.activation(out=qtm[qs][:, :, 0:64], in_=qv[:, :, 0:64], func=AF.Copy),
                       reads=[bk(bq)], pwrites=[qk_])
                    op('act', lambda e: e.activation(out=ktm[qs][:, :, 0:64], in_=kvv[:, :, 0:64], func=AF.Copy),
                       reads=[bk(bkv)], pwrites=[kk_])
                    op('dve', lambda e: e.tensor_tensor(out=rq[0], in0=qr[:, :, 0:16], in1=cosb, op=ALU.mult), reads=['qr', ('trig', 1)], writes=['rq0'])
                    op('dve', lambda e: e.tensor_tensor(out=rq[1], in0=qr[:, :, 16:32], in1=sinb, op=ALU.mult), reads=['qr', ('trig', 0)], writes=['rq1'])
                    op('dve', lambda e: e.tensor_tensor(out=qtm[qs][:, :, 64:80], in0=rq[0], in1=rq[1], op=ALU.subtract), reads=['rq0', 'rq1'], pwrites=[qk_])
                    op('dve', lambda e: e.tensor_tensor(out=rq[2], in0=qr[:, :, 16:32], in1=cosb, op=ALU.mult), reads=['qr', ('trig', 1)], writes=['rq2'])
                    op('dve', lambda e: e.tensor_tensor(out=rq[3], in0=qr[:, :, 0:16], in1=sinb, op=ALU.mult), reads=['qr', ('trig', 0)], writes=['rq3'])
                    op('dve', lambda e: e.tensor_tensor(out=qtm[qs][:, :, 80:96], in0=rq[2], in1=rq[3], op=ALU.add), reads=['rq2', 'rq3'], pwrites=[qk_])
                    for hl_ in range(4):
                        op('pool', lambda e, hl_=hl_: e.tensor_copy(ktm[qs][:, hl_, 64:96], kro[:, tb, :]),
                           reads=['kro'], pwrites=[kk_])
                    op('dve', lambda e: e.tensor_copy(Vd[:, tb, :, 0:64], kvv[:, :, 64:128]), reads=[bk(bkv)], pwrites=[bufs['vk'](tb)])

                def step_C(tb):
                    qs = tb % 2
                    qk_, kk_ = ('qtm', qs), ('ktm', qs)
                    ptr = banks[bt][:].bitcast(BF16)

                    def trqk(e):
                        for hl in range(4):
                            e.transpose(ptr[:, hl * 128:(hl + 1) * 128], qtm[qs][:, hl, :], ident)
                        for hl in range(4):
                            ins = e.transpose(ptr[:, (4 + hl) * 128:(5 + hl) * 128], ktm[qs][:, hl, :], ident)
                        return ins
                    op('pe', trqk, reads=[qk_, kk_, 'cbf'], writes=[bk(bt)])
                    evac(QTd[:, :, tb * 128:(tb + 1) * 128], ptr[:, 0:512].rearrange("p (a b) -> p a b", a=4),
                         reads=[bk(bt)], pwrites=sorted(set(bufs['qk'](hl) for hl in range(4))))
                    evac(KTd[:, :, tb * 128:(tb + 1) * 128], ptr[:, 512:1024].rearrange("p (a b) -> p a b", a=4),
                         reads=[bk(bt)], pwrites=sorted(set(bufs['kk'](hl) for hl in range(4))))

                for tb in range(NB):
                    step_A(tb)
                    yield
                    step_B(tb)
                    yield
                    step_C(tb)
                    yield

            def mla_attn_stages(g, bufs):
                QTd, KTd, Vd = bufs['QT'], bufs['KT'], bufs['V']
                base = mpc[0]
                m_items = []
                for hl in range(4):
                    for idx, (kb, c, lo, hi) in enumerate(pcs):
                        m_items.append(dict(hl=hl, kb=kb, c=c, lo=lo, hi=hi, idx=len(m_items), first=(idx == 0), last=(idx == len(pcs) - 1)))
                mpc[0] += len(m_items)

                def ml_s0(it):
                    hl, kb, lo, hi = it['hl'], it['kb'], it['lo'], it['hi']
                    n = hi - lo
                    p = base + it['idx']
                    zb = p % 2
                    op('pe', lambda e: e.matmul(banks[zb][:, 0:n], lhsT=KTd[:, hl, kb * 128:(kb + 1) * 128], rhs=QTd[:, hl, lo:hi], start=True, stop=True),
                       reads=[bufs['qk'](hl), bufs['kk'](hl)], writes=[bk(zb)])
                    S_.wtag(bk(zb), ('zm', p))

                def ml_s1(it):
                    kb, lo, hi = it['kb'], it['lo'], it['hi']
                    n = hi - lo
                    p = base + it['idx']
                    zb, as_ = p % 2, p % 3
                    S_.rtag(bk(zb), ('zm', p))
                    op('act', lambda e: e.activation(out=a_sb[as_][:, 0:n], in_=banks[zb][:, 0:n], func=AF.Exp, scale=MLA_SCALE),
                       reads=[bk(zb)], writes=[('a', as_)])
                    if lo == kb * 128:
                        op('pool', lambda e: e.tensor_tensor(out=a_sb[as_][:, 0:128], in0=a_sb[as_][:, 0:128], in1=mincl, op=ALU.mult),
                           reads=[('a', as_), 'cbf'], writes=[('a', as_)])
                    S_.wtag(('a', as_), ('m', p))

                def ml_s2(it):
                    hl, kb, lo, hi = it['hl'], it['kb'], it['lo'], it['hi']
                    h = g * 4 + hl
                    p = base + it['idx']
                    as_ = p % 3
                    ob0 = 2
                    obanks = [ob0 + i for i in range(n_mb)]
                    S_.rtag(('a', as_), ('m', p))
                    if it['first']:
                        def zero_m(e):
                            for bi in obanks:
                                ins = e.matmul(banks[bi][:], lhsT=zer[:, 0:128], rhs=zer[:, :], start=True, stop=False, skip_group_check=True)
                            return ins
                        op('pe', zero_m, reads=['zer'], writes=[bk(bi) for bi in obanks])

                    def mmavm(e):
                        for qb in range(lo // 128, hi // 128):
                            bi = ob0 + qb // MQ
                            sl = (qb % MQ) * 65
                            last = (kb == 0) and (qb % MQ == MQ - 1 or qb == NB - 1)
                            ins = e.matmul(banks[bi][:, sl:sl + 65], lhsT=a_sb[as_][:, qb * 128 - lo:qb * 128 - lo + 128],
                                           rhs=Vd[:, kb, hl, :], start=False, stop=last, skip_group_check=True)
                        return ins
                    obs = sorted(set(ob0 + qb // MQ for qb in range(lo // 128, hi // 128)))
                    op('pe', mmavm, reads=[('a', as_), bufs['vk'](kb)], pwrites=[bk(bi) for bi in obs])
                    if it['last']:
                        for i, bi in enumerate(obanks):
                            nq = min(MQ, NB - i * MQ)
                            ov = banks[bi][:, 0:nq * 65].rearrange("p (a b) -> p a b", a=nq)
                            rs = rec[(h * 4 + i) % 4]
                            rk = ('rec', (h * 4 + i) % 4)
                            op('dve', lambda e, ov=ov, nq=nq, rs=rs: e.reciprocal(rs[:, 0:nq], ov[:, :, 64]), reads=[bk(bi)], writes=[rk])
                            qbs = range(i * MQ, i * MQ + nq)
                            okeys = sorted(set(('o', qb // QPB) for qb in qbs))
                            op('dve', lambda e, ov=ov, nq=nq, i=i, h=h, rs=rs: e.tensor_tensor(
                                out=o_tm[:, i * MQ:i * MQ + nq, 512 + h * 64:512 + (h + 1) * 64], in0=ov[:, :, 0:64],
                                in1=rs[:, 0:nq].unsqueeze(2).to_broadcast([128, nq, 64]), op=ALU.mult),
                               reads=[bk(bi), rk], pwrites=okeys)
                return m_items, [(0, ml_s0), (1, ml_s1), (2, ml_s2)]

            for qs_ in range(2):
                op('pool', lambda e, qs_=qs_: e.memset(qtm[qs_].rearrange("p a b -> p (a b)"), 0.0), writes=[('qtm', qs_)])
                op('pool', lambda e, qs_=qs_: e.memset(ktm[qs_].rearrange("p a b -> p (a b)"), 0.0), writes=[('ktm', qs_)])
            op('pool', lambda e: e.memset(Vm2_flat, 1.0), writes=[('Vb', tb) for tb in range(NB)])

            n_micro = 3 * NB
            weave(pipeline_gen(sb_items, [(0, sb_s0), (1, sb_s1), (2, sb_s2), (3, sb_s3), (4, sb_s4)]),
                  mla_proj_gen(0, bufB, 6, 7, 6), max(1, len(sb_items) // (n_micro + 2)))
            if PHASE_LIMIT < 3:
                S_.enabled = False
            op('pool', lambda e: e.memset(Vs_flat, 1.0), writes=[('V', tb) for tb in range(NB)])
            items0, stages0 = mla_attn_stages(0, bufB)
            weave(pipeline_gen(items0, stages0), mla_proj_gen(1, bufA, 5, 6, 7), max(1, len(items0) // (n_micro + 2)))
            items1, stages1 = mla_attn_stages(1, bufA)
            run_pipeline(items1, stages1)

            S_.barrier()
            if PHASE_LIMIT < 4:
                S_.enabled = False

            A4 = Arena(arena, 0, ARENA_BYTES)
            if DEBUG:
                dma('sp', 'dbg', lambda e, b=b: e.dma_start(out=dbg_d[b], in_=o_tm[:]), reads=[('o', i) for i in range(n_ob)], final=True)
            wout = A4.take([128, 8, D], BF16)
            wdn = A4.take([128, NJ, D], BF16)
            gob = A4.take([128, D], F32)
            gffn = A4.take([128, D], F32)
            gfin = A4.take([128, D], F32)
            xn4a = [A4.take([128, D], BF16) for _ in range(2)]
            xn4b = [A4.take([128, D], BF16) for _ in range(2)]
            junk4 = A4.take([128, D], BF16)
            oT = [A4.take([128, 8, 128], BF16) for _ in range(2)]
            h2T = A4.take([128, 8, 512], BF16)
            x1b = [A4.take([128, 4, D], F32) for _ in range(2)]
            gT = A4.take([128, NJ, 512], BF16)
            wup = [A4.take([128, 8, 256], BF16) for _ in range(3)]
            cg = [A4.take([128, 512], F32) for _ in range(2)]
            cv = [A4.take([128, 512], F32) for _ in range(2)]
            sg = [A4.take([128, 512], BF16) for _ in range(2)]
            halo = [A4.take([128, 2 * NJ, 2], F32) for _ in range(2)]
            so = [A4.take([128, 8], F32) for _ in range(4)]

            for kc in range(8):
                dma('pool', ('wout', kc), lambda e, kc=kc: e.dma_start(out=wout[:, kc, :], in_=wout_v[:, kc, :]), writes=[('wout', kc)])
            for j0 in range(0, NJ, 2):
                dma('pool', ('wdn', j0), lambda e, j0=j0: e.dma_start(out=wdn[:, j0:j0 + 2, :], in_=wdn_v[:, j0:j0 + 2, :]), writes=[('wdn', j0), ('wdn', j0 + 1)])
            dma('sp', 'gob', lambda e: e.dma_start(out=gob, in_=go_d[:, :]), writes=['gob'])
            dma('sp', 'gffn', lambda e: e.dma_start(out=gffn, in_=gffn_d[:, :]), writes=['gffn'])
            dma('sp', 'gfin', lambda e: e.dma_start(out=gfin, in_=gfin_d[:, :]), writes=['gfin'])
            wu_ptr = [0]

            def ffn_dma_upto(g_hi):
                while wu_ptr[0] <= g_hi and wu_ptr[0] < NT * NJ:
                    g_ = wu_ptr[0]
                    wu_ptr[0] += 1
                    ws, j = g_ % 3, g_ % NJ
                    dma('pool', ('wup', ws), lambda e, ws=ws, j=j: e.dma_start(out=wup[ws], in_=wup_v[:, :, j * 256:(j + 1) * 256]), writes=[('wup', ws)])
                    S_.wtag(('wup', ws), g_)

            def ffn_mmu(T, j):
                g_ = T * NJ + j
                ws = g_ % 3
                bg, bv = (0, 1) if j % 2 == 0 else (2, 3)
                S_.rtag(('wup', ws), g_)

                def mmu(e):
                    for kc in range(8):
                        e.matmul(banks[bg][:], lhsT=wup[ws][:, kc, 0:128], rhs=h2T[:, kc, :], start=(kc == 0), stop=(kc == 7))
                    for kc in range(8):
                        ins = e.matmul(banks[bv][:], lhsT=wup[ws][:, kc, 128:256], rhs=h2T[:, kc, :], start=(kc == 0), stop=(kc == 7))
                    return ins
                op('pe', mmu, reads=[('wup', ws), 'h2T'], writes=[bk(bg), bk(bv)])
                S_.wtag(bk(bg), ('u', g_))

            def ffn_conv(T, j):
                g_ = T * NJ + j
                bg, bv = (0, 1) if j % 2 == 0 else (2, 3)
                S_.rtag(bk(bg), ('u', g_))
                us = j % 2
                hw, hr = halo[T % 2], halo[(T + 1) % 2]
                halves = ((0, bg, cg[us], ('cg', us)), (1, bv, cv[us], ('cv', us)))
                for (half, bi, dst, dkey) in halves:
                    cj = j * 2 + half
                    w2 = cw[:, cj * 4 + 2:cj * 4 + 3]
                    bb_ = cw[:, cj * 4 + 3:cj * 4 + 4]
                    op('act', lambda e, bi=bi, dst=dst, w2=w2, bb_=bb_: e.activation(out=dst, in_=banks[bi][:], func=AF.Identity, scale=w2, bias=bb_),
                       reads=[bk(bi), 'cw'], writes=[dkey])
                    if T < NT - 1:
                        op('act', lambda e, bi=bi, cj=cj: e.activation(out=hw[:, cj, :], in_=banks[bi][:, 510:512], func=AF.Copy),
                           reads=[bk(bi)], writes=[('halo', T % 2, cj)])
                for (half, bi, dst, dkey) in halves:
                    cj = j * 2 + half
                    w0 = cw[:, cj * 4 + 0:cj * 4 + 1]
                    w1 = cw[:, cj * 4 + 1:cj * 4 + 2]
                    hk = ('halo', (T + 1) % 2, cj)
                    op('dve', lambda e, bi=bi, dst=dst, w1=w1: e.scalar_tensor_tensor(out=dst[:, 1:512], in0=banks[bi][:, 0:511], scalar=w1, in1=dst[:, 1:512],
                                                                                   op0=ALU.mult, op1=ALU.add),
                       reads=[bk(bi), dkey, 'cw'], writes=[dkey])
                    op('dve', lambda e, bi=bi, dst=dst, w0=w0: e.scalar_tensor_tensor(out=dst[:, 2:512], in0=banks[bi][:, 0:510], scalar=w0, in1=dst[:, 2:512],
                                                                                   op0=ALU.mult, op1=ALU.add),
                       reads=[bk(bi), dkey, 'cw'], writes=[dkey])
                    if T > 0:
                        op('dve', lambda e, dst=dst, w1=w1, cj=cj: e.scalar_tensor_tensor(out=dst[:, 0:1], in0=hr[:, cj, 1:2], scalar=w1, in1=dst[:, 0:1],
                                                                                       op0=ALU.mult, op1=ALU.add),
                           reads=[hk, dkey, 'cw'], writes=[dkey])
                        op('dve', lambda e, dst=dst, w0=w0, cj=cj: e.scalar_tensor_tensor(out=dst[:, 0:2], in0=hr[:, cj, 0:2], scalar=w0, in1=dst[:, 0:2],
                                                                                       op0=ALU.mult, op1=ALU.add),
                           reads=[hk, dkey, 'cw'], writes=[dkey])
                    if half == 0:
                        op('act', lambda e: e.activation(out=sg[us], in_=cg[us], func=AF.Silu), reads=[('cg', us)], writes=[('sg', us)])
                op('pool', lambda e: e.tensor_tensor(out=gT[:, j, :], in0=sg[us], in1=cv[us], op=ALU.mult),
                   reads=[('sg', us), ('cv', us)], writes=[('gT', j)])

            def pro_A(T, i):
                tb = T * 4 + i
                par, s = T % 2, i % 2
                x1 = x1b[par]
                okey = ('o', tb // QPB)
                dma('sp', ('xin', par, i), lambda e, b=b: e.dma_start(out=x1[:, i, :], in_=x_d[b, tb * 128:(tb + 1) * 128, :]),
                    writes=[('x1', par, i)])
                sos = so[i]
                sok = ('so', i)
                op('act', lambda e: e.activation(out=junk4[:, 0:512], in_=o_tm[:, tb, 0:512], func=AF.Square, accum_out=sos[:, 0:1]),
                   reads=[okey], writes=['junk4', sok])
                op('act', lambda e: e.activation(out=junk4[:, 512:1024], in_=o_tm[:, tb, 512:1024], func=AF.Square, accum_out=sos[:, 1:2]),
                   reads=[okey], pwrites=['junk4', sok])
                op('act', lambda e: e.activation(out=sos[:, 2:4], in_=sos[:, 0:2], func=AF.Sqrt, scale=1.0 / 512.0, bias=EPS), reads=[sok], writes=[sok])
                op('dve', lambda e: e.reciprocal(sos[:, 4:6], sos[:, 2:4]), reads=[sok], writes=[sok])
                for half in range(2):
                    op('dve', lambda e, half=half: e.scalar_tensor_tensor(
                        out=xn4a[s][:, half * 512:(half + 1) * 512], in0=o_tm[:, tb, half * 512:(half + 1) * 512], scalar=sos[:, 4 + half:5 + half],
                        in1=gob[:, half * 512:(half + 1) * 512], op0=ALU.mult, op1=ALU.mult),
                       reads=[okey, sok, 'gob'], **({'writes': [('xn4a', s)]} if half == 0 else {'pwrites': [('xn4a', s)]}))
                S_.wtag(('xn4a', s), tb)

            def pro_B(T, i):
                tb = T * 4 + i
                s = i % 2
                pi = 6 + s
                ptr = banks[pi][:].bitcast(BF16)
                S_.rtag(('xn4a', s), tb)

                def tro(e):
                    for kc in range(8):
                        ins = e.transpose(ptr[:, kc * 128:(kc + 1) * 128], xn4a[s][:, kc * 128:(kc + 1) * 128], ident)
                    return ins
                op('pe', tro, reads=[('xn4a', s), 'cbf'], writes=[bk(pi)])
                evac(oT[s], ptr.rearrange("p (a b) -> p a b", a=8), reads=[bk(pi)], writes=[('oT', s)])
                S_.wtag(('oT', s), tb)

            def pro_C(T, i):
                tb = T * 4 + i
                par, s = T % 2, i % 2
                x1 = x1b[par]
                S_.rtag(('oT', s), tb)
                for half in range(2):
                    bi = 4 + half

                    def mmo(e, half=half, bi=bi):
                        for kc in range(8):
                            ins = e.matmul(banks[bi][:], lhsT=oT[s][:, kc, :], rhs=wout[:, kc, half * 512:(half + 1) * 512],
                                           start=(kc == 0), stop=(kc == 7))
                        return ins
                    op('pe', mmo, reads=[('oT', s)] + [('wout', kc) for kc in range(8)], writes=[bk(bi)])
                    op('dve', lambda e, bi=bi, half=half: e.tensor_tensor(
                        out=x1[:, i, half * 512:(half + 1) * 512], in0=banks[bi][:], in1=x1[:, i, half * 512:(half + 1) * 512], op=ALU.add),
                       reads=[bk(bi), ('x1', par, i)], pwrites=[('x1', par, i)])

            def pro_D(T, i):
                tb = T * 4 + i
                par, s = T % 2, i % 2
                x1 = x1b[par]
                rstd, rkey = rms_rstd([(x1[:, i, :], [('x1', par, i)])], 1.0 / D, junk4, 'junk4')
                op('dve', lambda e: e.scalar_tensor_tensor(out=xn4b[s], in0=x1[:, i, :], scalar=rstd, in1=gffn, op0=ALU.mult, op1=ALU.mult),
                   reads=[('x1', par, i), rkey, 'gffn'], writes=[('xn4b', s)])
                S_.wtag(('xn4b', s), tb)

            def pro_E(T, i):
                tb = T * 4 + i
                s = i % 2
                pi = 6 + s
                ptr = banks[pi][:].bitcast(BF16)
                S_.rtag(('xn4b', s), tb)

                def tro(e):
                    for kc in range(8):
                        ins = e.transpose(ptr[:, kc * 128:(kc + 1) * 128], xn4b[s][:, kc * 128:(kc + 1) * 128], ident)
                    return ins
                op('pe', tro, reads=[('xn4b', s), 'cbf'], writes=[bk(pi)])
                evac(h2T[:, :, i * 128:(i + 1) * 128], ptr.rearrange("p (a b) -> p a b", a=8), reads=[bk(pi)], pwrites=['h2T'])

            def epi_A(T, i):
                par = T % 2
                x1 = x1b[par]
                for half in range(2):
                    bi = (i % 2) * 2 + half

                    def mmd(e, half=half, bi=bi):
                        for j in range(NJ):
                            ins = e.matmul(banks[bi][:], lhsT=gT[:, j, i * 128:(i + 1) * 128], rhs=wdn[:, j, half * 512:(half + 1) * 512],
                                           start=(j == 0), stop=(j == NJ - 1))
                        return ins
                    op('pe', mmd, reads=[('gT', j) for j in range(NJ)] + [('wdn', j) for j in range(NJ)], writes=[bk(bi)])
                    op('dve', lambda e, bi=bi, half=half: e.tensor_tensor(
                        out=x1[:, i, half * 512:(half + 1) * 512], in0=banks[bi][:], in1=x1[:, i, half * 512:(half + 1) * 512], op=ALU.add),
                       reads=[bk(bi), ('x1', par, i)], pwrites=[('x1', par, i)])

            def epi_B(T, i):
                tb = T * 4 + i
                par = T % 2
                x1 = x1b[par]
                rstd, rkey = rms_rstd([(x1[:, i, :], [('x1', par, i)])], 1.0 / D, junk4, 'junk4')
                op('dve', lambda e: e.scalar_tensor_tensor(out=x1[:, i, :], in0=x1[:, i, :], scalar=rstd, in1=gfin, op0=ALU.mult, op1=ALU.mult),
                   reads=[rkey, 'gfin'], writes=[('x1', par, i)])
                dma('sp', ('yout', par, i), lambda e, b=b: e.dma_start(out=out_d[b, tb * 128:(tb + 1) * 128, :], in_=x1[:, i, :]),
                    reads=[('x1', par, i)], final=True)

            def pro_epi(Tp, Te):
                stages = []
                if Tp is not None:
                    stages.append((0, lambda i: pro_A(Tp, i)))
                if Te is not None:
                    stages.append((0, lambda i: epi_A(Te, i)))
                    stages.append((1, lambda i: epi_B(Te, i)))
                if Tp is not None:
                    stages.append((1, lambda i: pro_B(Tp, i)))
                    stages.append((2, lambda i: pro_C(Tp, i)))
                    stages.append((2, lambda i: pro_D(Tp, i)))
                    stages.append((3, lambda i: pro_E(Tp, i)))
                stages.sort(key=lambda t: t[0])
                run_pipeline(list(range(4)), stages)

            ffn_dma_upto(1)
            pro_epi(0, None)
            for T in range(NT):
                for j in range(NJ):
                    ffn_dma_upto(T * NJ + j + 2)
                    ffn_mmu(T, j)
                    if j >= 1:
                        ffn_conv(T, j - 1)
                ffn_conv(T, NJ - 1)
                pro_epi(T + 1 if T + 1 < NT else None, T)

            S_.barrier()

        S_.emit()
    return nc


def _prep_shared(inputs):
    f32 = np.float32
    w_in = np.asarray(inputs["w_in"][0], f32)
    perm = np.concatenate([np.arange(0, 1536), np.arange(1536, 1920), np.arange(2176, 2208), np.arange(1920, 2176)])
    w_in_p = np.ascontiguousarray(w_in[:, perm])
    w_up = np.asarray(inputs["w_up"][0], f32)
    cols = np.concatenate([np.concatenate([np.arange(j * 128, (j + 1) * 128), DFF + np.arange(j * 128, (j + 1) * 128)]) for j in range(NJ)])
    w_up_p = np.ascontiguousarray(w_up[:, cols])
    conv_w = np.asarray(inputs["conv_w"][0], f32)
    conv_b = np.asarray(inputs["conv_b"][0], f32)
    cw = np.zeros((128, 2 * NJ, 4), f32)
    for j in range(NJ):
        for half in range(2):
            ch = half * DFF + j * 128 + np.arange(128)
            cw[:, j * 2 + half, 0] = conv_w[0, ch]
            cw[:, j * 2 + half, 1] = conv_w[1, ch]
            cw[:, j * 2 + half, 2] = conv_w[2, ch]
            cw[:, j * 2 + half, 3] = conv_b[ch]

    def bc(v):
        return np.ascontiguousarray(np.broadcast_to(np.asarray(v, f32)[None, :], (128, v.shape[0])))
    g_c = np.concatenate([np.asarray(inputs["g_cq"][0], f32), np.asarray(inputs["g_ckv"][0], f32)])
    g_o = np.concatenate([np.asarray(inputs["g_sb_out"][0], f32), np.asarray(inputs["g_mla_out"][0], f32)])
    ii = np.arange(128)
    ident = np.eye(128, dtype=f32)
    tri = (ii[:, None] >= ii[None, :]).astype(f32)
    ones = np.ones((128, 128), f32)
    mstrict = (ii[:, None] < ii[None, :]).astype(f32)
    mincl = (ii[:, None] <= ii[None, :]).astype(f32)
    cbf = np.concatenate([ident, tri, ones, mstrict, mincl, -tri, -ones], axis=1).astype(ml_dtypes.bfloat16)
    half = 16
    inv_freq = (1.0 / (np.float32(10000.0) ** (np.arange(half, dtype=f32) * np.float32(2.0 / 32)))).astype(f32)
    return {
        "w_in": w_in_p,
        "w_uq": np.ascontiguousarray(np.asarray(inputs["w_uq"][0], f32)),
        "w_ukv": np.ascontiguousarray(np.asarray(inputs["w_ukv"][0], f32)),
        "w_out": np.ascontiguousarray(np.asarray(inputs["w_out"][0], f32)),
        "w_up": w_up_p,
        "w_down": np.ascontiguousarray(np.asarray(inputs["w_down"][0], f32)),
        "g_mix": bc(inputs["g_mix"][0]),
        "g_c": bc(g_c),
        "g_o": bc(g_o),
        "g_ffn": bc(inputs["g_ffn"][0]),
        "g_final": bc(inputs["g_final"]),
        "cw": np.ascontiguousarray(cw.reshape(128, 2 * NJ * 4)),
        "cbf": cbf,
        "invf": bc(inv_freq),
    }


_PROG_CACHE = {}


def kernel(**inputs):
    x = np.asarray(inputs["x"], np.float32)
    pos = np.asarray(inputs["positions"], np.int32)
    B, S, _ = x.shape
    ncores = NCORES if B % NCORES == 0 else B
    nseq = B // ncores
    shared = _prep_shared(inputs)
    key = (S, nseq)
    if key not in _PROG_CACHE:
        _PROG_CACHE[key] = build_program(S, nseq)
    nc = _PROG_CACHE[key]
    NB = S // 128
    in_maps = []
    for c in range(ncores):
        m = dict(shared)
        m["x"] = np.ascontiguousarray(x[c * nseq:(c + 1) * nseq])
        p = pos[c * nseq:(c + 1) * nseq].reshape(nseq, NB, 128).transpose(0, 2, 1)
        m["pos"] = np.ascontiguousarray(p)
        in_maps.append(m)
    res = run_bass_kernel_spmd(nc, in_maps, core_ids=list(range(ncores)))
    if DEBUG:
        global LAST_RES
        LAST_RES = res
    out = np.concatenate([np.asarray(r["out"], np.float32) for r in res.results], axis=0)
    return out
```

```python
import contextlib
import math
import numpy as np
import ml_dtypes
import concourse.bass as bass
import concourse.mybir as mybir
from concourse.bass_utils import run_bass_kernel_spmd

F32 = mybir.dt.float32
BF16 = mybir.dt.bfloat16
I32 = mybir.dt.int32
U8 = mybir.dt.uint8
AF = mybir.ActivationFunctionType
ALU = mybir.AluOpType

D = 1024
NCORES = 8
EPS = 1e-6
DFF = 2816
NJ = DFF // 128
SB_SCALE = 64 ** -0.5
MLA_SCALE = 96 ** -0.5
TWO_PI = 2.0 * math.pi
CW1 = 6.28125
CW2 = TWO_PI - CW1
PI_SAFE = 3.1415925
PHASE_LIMIT = 4
P3_MODE = 'all'
OPLIM = None
DEBUG = False
LAST_RES = None


class Sched:
    ENGS = ('pe', 'act', 'dve', 'pool', 'sp')

    def __init__(self, nc, es):
        self.nc = nc
        self.es = es
        self.streams = {e: [] for e in self.ENGS}
        self.cnt = {e: 0 for e in self.ENGS}
        self.esem = {e: es.enter_context(nc.semaphore("sem_" + e)) for e in self.ENGS}
        self.waited = {e: {} for e in self.ENGS}
        self.bufs = {}
        self.dsem = {}
        self.semobj = {}
        self.final_tokens = []
        self.tags = {}

    def _st(self, b):
        return self.bufs.setdefault(b, {'w': [], 'r': []})

    def _deps(self, reads, writes, pwrites):
        toks = []
        for b in reads:
            st = self._st(b)
            toks += st['w']
            if isinstance(b, tuple) and b[0] == 'pb':
                toks += st['r']
        for b in writes:
            st = self._st(b)
            toks += st['w'] + st['r']
        for b in pwrites:
            st = self._st(b)
            toks += st['w'] + st['r']
        return toks

    def _update(self, tok, reads, writes, pwrites):
        for b in reads:
            self.bufs[b]['r'].append(tok)
        for b in writes:
            self.bufs[b]['w'] = [tok]
            self.bufs[b]['r'] = []
        for b in pwrites:
            st = self.bufs[b]
            st['w'].append(tok)
            if len(st['w']) > 64:
                best = {}
                for t in st['w']:
                    k = id(t[0])
                    if k not in best or best[k][1] < t[1]:
                        best[k] = t
                st['w'] = list(best.values())
            st['r'] = []

    def _waits(self, eng, toks):
        need = {}
        for (sem, val, src) in toks:
            if src == eng and eng == 'pe':
                continue
            k = id(sem)
            self.semobj[k] = sem
            if self.waited[eng].get(k, 0) >= val:
                continue
            if need.get(k, 0) < val:
                need[k] = val
        out = []
        for k, v in need.items():
            self.waited[eng][k] = v
            out.append((self.semobj[k], v))
        return out

    enabled = True

    oplimit = None

    def op(self, eng, fn, reads=(), writes=(), pwrites=()):
        if not self.enabled:
            return None
        if self.oplimit is not None:
            if self.oplimit <= 0:
                return None
            self.oplimit -= 1
        toks = self._deps(reads, writes, pwrites)
        waits = self._waits(eng, toks)
        self.cnt[eng] += 1
        tok = (self.esem[eng], self.cnt[eng], eng)
        self.streams[eng].append((waits, fn, self.esem[eng], 1))
        self._update(tok, reads, writes, pwrites)
        return tok

    def dma(self, eng, key, fn, reads=(), writes=(), pwrites=(), final=False):
        if not self.enabled:
            return None
        toks = self._deps(reads, writes, pwrites)
        if key not in self.dsem:
            self.dsem[key] = [self.es.enter_context(self.nc.semaphore("dsem_%d" % len(self.dsem))), 0]
        ent = self.dsem[key]
        if ent[1] > 0:
            toks = toks + [(ent[0], 16 * ent[1], 'dma')]
        waits = self._waits(eng, toks)
        ent[1] += 1
        tok = (ent[0], 16 * ent[1], 'dma')
        self.streams[eng].append((waits, fn, ent[0], 16))
        self._update(tok, reads, writes, pwrites)
        if final:
            self.final_tokens.append(tok)
        return tok

    def wtag(self, key, tag):
        self.tags[key] = tag

    def rtag(self, key, tag):
        assert self.tags.get(key) == tag, ("pipeline slot hazard", key, self.tags.get(key), tag)

    def barrier(self):
        toks = [(self.esem[e], self.cnt[e], e + '_b') for e in self.ENGS if self.cnt[e] > 0]
        toks += [(ent[0], 16 * ent[1], 'dma') for ent in self.dsem.values() if ent[1] > 0]
        for e in self.ENGS:
            waits = self._waits(e, [t for t in toks if not t[2].startswith(e + '_')])
            if waits:
                self.streams[e].append((waits, None, None, 0))
        self.bufs = {}

    def emit(self):
        nc = self.nc
        with nc.Block() as block:
            def replay(name):
                def f(eng):
                    for (waits, fn, sem, inc) in self.streams[name]:
                        for (s, v) in waits:
                            eng.wait_ge(s, v)
                        if fn is None:
                            continue
                        ins = fn(eng)
                        ins.then_inc(sem, inc)
                    if name == 'sp':
                        best = {}
                        for (s, v, _) in self.final_tokens:
                            k = id(s)
                            if k not in best or best[k][1] < v:
                                best[k] = (s, v)
                        for (s, v) in best.values():
                            eng.wait_ge(s, v)
                return f
            if self.streams['pe']:
                block.tensor(replay('pe'))
            if self.streams['act']:
                block.scalar(replay('act'))
            if self.streams['dve']:
                block.vector(replay('dve'))
            if self.streams['pool']:
                block.gpsimd(replay('pool'))
            block.sync(replay('sp'))


def run_pipeline(items, stages):
    n = len(items)
    maxlag = max(l for l, _ in stages)
    for t in range(n + maxlag):
        for lag, fn in stages:
            i = t - lag
            if 0 <= i < n:
                fn(items[i])


class Arena:
    def __init__(self, ap, start, limit):
        self.ap = ap
        self.off = start
        self.limit = limit
        self.n = 0

    def take(self, shape, dt):
        esz = {F32: 4, BF16: 2, I32: 4}[dt]
        free = 1
        for s in shape[1:]:
            free *= s
        nbytes = (free * esz + 63) // 64 * 64
        assert self.off + nbytes <= self.limit, ("arena overflow", self.off, nbytes, self.limit)
        v = self.ap[:, self.off:self.off + free * esz].bitcast(dt)
        self.off += nbytes
        if len(shape) == 3:
            v = v.rearrange("p (a b) -> p a b", a=shape[1])
        elif len(shape) == 4:
            v = v.rearrange("p (a b c) -> p a b c", a=shape[1], b=shape[2])
        if shape[0] < 128:
            v = v[0:shape[0]]
        return v


def build_program(S, NSEQ):
    NB = S // 128
    NT = S // 512
    assert S % 512 == 0
    nc = bass.Bass("TRN2", target_bir_lowering=False)

    def din(name, shape, dt):
        return nc.dram_tensor(name, list(shape), dt, kind="ExternalInput").ap()

    x_d = din("x", [NSEQ, S, D], F32)
    pos_d = din("pos", [NSEQ, 128, NB], I32)
    win_d = din("w_in", [D, 2208], F32)
    wuq_d = din("w_uq", [384, 768], F32)
    wukv_d = din("w_ukv", [256, 1024], F32)
    wout_d = din("w_out", [D, D], F32)
    wup_d = din("w_up", [D, 2 * DFF], F32)
    wdn_d = din("w_down", [DFF, D], F32)
    gmix_d = din("g_mix", [128, D], F32)
    gc_d = din("g_c", [128, 640], F32)
    go_d = din("g_o", [128, D], F32)
    gffn_d = din("g_ffn", [128, D], F32)
    gfin_d = din("g_final", [128, D], F32)
    cw_d = din("cw", [128, 2 * NJ * 4], F32)
    cb_d = din("cbf", [128, 7 * 128], BF16)
    invf_d = din("invf", [128, 16], F32)
    out_d = nc.dram_tensor("out", [NSEQ, S, D], F32, kind="ExternalOutput").ap()
    dbg_d = nc.dram_tensor("dbg", [NSEQ, 128, NB, D], BF16, kind="ExternalOutput").ap() if DEBUG else None

    win_v = win_d.rearrange("(kc p) n -> p kc n", p=128)
    wuq_v = wuq_d.rearrange("(kc p) n -> p kc n", p=128)
    wukv_v = wukv_d.rearrange("(kc p) n -> p kc n", p=128)
    wout_v = wout_d.rearrange("(kc p) n -> p kc n", p=128)
    wup_v = wup_d.rearrange("(kc p) n -> p kc n", p=128)
    wdn_v = wdn_d.rearrange("(j p) n -> p j n", p=128)

    with contextlib.ExitStack() as es:
        S_ = Sched(nc, es)
        op, dma = S_.op, S_.dma

        def sbt(name, shape, dt):
            return es.enter_context(nc.sbuf_tensor(name, list(shape), dt))

        ARENA_BYTES = 172 * 1024
        arena = sbt("arena", [128, ARENA_BYTES], U8)
        o_tm = sbt("o_tm", [128, NB, D], BF16)
        cbf = sbt("cbf_sb", [128, 7 * 128], BF16)
        zer = sbt("zer", [128, 512], BF16)
        invf = sbt("invf_sb", [128, 16], F32)
        cw = sbt("cw_sb", [128, 2 * NJ * 4], F32)
        cscale = sbt("cscale", [128, 2], F32)
        stat = sbt("stat", [128, 64], F32)
        ident = cbf[:, 0:128]
        tri = cbf[:, 128:256]
        ones = cbf[:, 256:384]
        mstrict = cbf[:, 384:512]
        mincl = cbf[:, 512:640]
        ntri = cbf[:, 640:768]
        nones = cbf[:, 768:896]

        banks = [es.enter_context(nc.psum_tensor("pb%d" % i, [128, 512], F32)) for i in range(8)]

        def bk(i):
            return ('pb', i)

        dma('sp', 'c0', lambda e: e.dma_start(out=cbf[:], in_=cb_d[:, :]), writes=['cbf'])
        dma('sp', 'c1', lambda e: e.dma_start(out=invf[:], in_=invf_d[:, :]), writes=['invf'])
        dma('sp', 'c2', lambda e: e.dma_start(out=cw[:], in_=cw_d[:, :]), writes=['cw'])
        op('pool', lambda e: e.memset(zer[:], 0.0), writes=['zer'])
        op('pool', lambda e: e.memset(cscale[:, 0:1], 1.0 / 384.0), pwrites=['cscale'])
        op('pool', lambda e: e.memset(cscale[:, 1:2], 1.0 / 256.0), pwrites=['cscale'])

        evac_flip = [0]

        def evac(out_ap, in_ap, reads, writes=(), pwrites=(), scale=None):
            evac_flip[0] ^= 1
            if evac_flip[0]:
                if scale is None:
                    op('act', lambda e: e.activation(out=out_ap, in_=in_ap, func=AF.Copy), reads=reads, writes=writes, pwrites=pwrites)
                else:
                    op('act', lambda e: e.activation(out=out_ap, in_=in_ap, func=AF.Copy, scale=scale), reads=reads, writes=writes, pwrites=pwrites)
            else:
                if scale is None:
                    op('dve', lambda e: e.tensor_copy(out_ap, in_ap), reads=reads, writes=writes, pwrites=pwrites)
                else:
                    op('dve', lambda e: e.tensor_scalar(out=out_ap, in0=in_ap, scalar1=scale, scalar2=None, op0=ALU.mult), reads=reads, writes=writes, pwrites=pwrites)

        stat_ctr = [0]

        def stat_cols(n):
            c = stat_ctr[0]
            if c + n > 64:
                c = 0
            stat_ctr[0] = c + n
            return c

        def rms_rstd(src_list, inv_n, junk, junk_key):
            c = stat_cols(4)
            key = ('stat', c)
            assert len(src_list) == 1
            ap, rk = src_list[0]
            op('act', lambda e: e.activation(out=junk[:, 0:ap.shape[1]], in_=ap, func=AF.Square, accum_out=stat[:, c:c + 1]),
               reads=rk, writes=[junk_key, key])
            op('act', lambda e: e.activation(out=stat[:, c + 2:c + 3], in_=stat[:, c:c + 1], func=AF.Sqrt, scale=inv_n, bias=EPS),
               reads=[key], writes=[key])
            op('dve', lambda e: e.reciprocal(stat[:, c + 3:c + 4], stat[:, c + 2:c + 3]), reads=[key], writes=[key])
            return stat[:, c + 3:c + 4], key

        R1_END = 90 * 1024

        for b in range(NSEQ):
            A1 = Arena(arena, 0, R1_END)
            QT = A1.take([128, 4, S], BF16)
            KTz_flat = A1.take([128, 8 * S], BF16)
            KTz = KTz_flat.rearrange("p (a b) -> p a b", a=8)
            KT = KTz_flat[:, 0:4 * S].rearrange("p (a b) -> p a b", a=4)
            Vs_flat = A1.take([128, NB * 512], BF16)
            Vs = Vs_flat.rearrange("p (a b) -> p a b", a=NB)
            cT = A1.take([128, 5, S], BF16)
            kr_tm = A1.take([128, NB, 32], F32)
            cos_t = A1.take([128, NB, 16], F32)
            sin_t = A1.take([128, NB, 16], F32)
            kro = A1.take([128, NB, 32], BF16)

            A2 = Arena(arena, R1_END, ARENA_BYTES)
            win = A2.take([128, 8, 2208], BF16)
            gmix = A2.take([128, D], F32)
            gcb = A2.take([128, 640], F32)
            xt = [A2.take([128, D], F32) for _ in range(2)]
            xn = [A2.take([128, D], BF16) for _ in range(2)]
            hT = [A2.take([128, 8, 512], BF16) for _ in range(2)]
            junk = A2.take([128, D], BF16)
            cn = [A2.take([128, 640], BF16) for _ in range(2)]
            posi = A2.take([128, NB], I32)
            posf = A2.take([128, NB], F32)
            ang = A2.take([128, NB, 16], F32)
            rt = [A2.take([128, NB, 16], F32) for _ in range(4)]
            ki = A2.take([128, NB, 16], I32)
            sc = [A2.take([128, 8], F32) for _ in range(2)]

            for kc in range(8):
                dma('pool', ('win', kc), lambda e, kc=kc: e.dma_start(out=win[:, kc, :], in_=win_v[:, kc, :]), writes=[('win', kc)])
            dma('sp', 'gmix', lambda e: e.dma_start(out=gmix, in_=gmix_d[:, :]), writes=['gmix'])
            dma('sp', 'gcb', lambda e: e.dma_start(out=gcb, in_=gc_d[:, :]), writes=['gcb'])
            dma('sp', 'posi', lambda e, b=b: e.dma_start(out=posi, in_=pos_d[b, :, :]), writes=['posi'])

            op('pool', lambda e: e.memset(KTz_flat, 0.0), writes=[('KT', c_) for c_ in range(4)])

            op('dve', lambda e: e.tensor_copy(posf, posi), reads=['posi'], writes=['posf'])
            op('dve', lambda e: e.tensor_tensor(out=ang, in0=invf[:].unsqueeze(1).to_broadcast([128, NB, 16]),
                                                in1=posf.unsqueeze(2).to_broadcast([128, NB, 16]), op=ALU.mult),
               reads=['posf', 'invf'], writes=['ang'])
            for which, shift, dst in ((0, 0.0, sin_t), (1, 0.25, cos_t)):
                t0, t1 = rt[0], rt[1]
                op('dve', lambda e, shift=shift: e.tensor_scalar(out=t0, in0=ang, scalar1=1.0 / TWO_PI, scalar2=shift, op0=ALU.mult, op1=ALU.add),
                   reads=['ang'], writes=['rt0'])
                op('dve', lambda e: e.tensor_copy(ki, t0), reads=['rt0'], writes=['ki'])
                op('dve', lambda e: e.tensor_copy(t0, ki), reads=['ki'], writes=['rt0'])
                op('dve', lambda e: e.scalar_tensor_tensor(out=t1, in0=t0, scalar=-CW1, in1=ang, op0=ALU.mult, op1=ALU.add),
                   reads=['rt0', 'ang'], writes=['rt1'])
                op('dve', lambda e: e.scalar_tensor_tensor(out=t1, in0=t0, scalar=-CW2, in1=t1, op0=ALU.mult, op1=ALU.add),
                   reads=['rt0', 'rt1'], writes=['rt1'])
                if which == 1:
                    op('dve', lambda e: e.tensor_scalar(out=t1, in0=t1, scalar1=math.pi / 2, scalar2=None, op0=ALU.add),
                       reads=['rt1'], writes=['rt1'])
                op('dve', lambda e: e.tensor_scalar(out=t1, in0=t1, scalar1=PI_SAFE, scalar2=-PI_SAFE, op0=ALU.min, op1=ALU.max),
                   reads=['rt1'], writes=['rt1'])
                op('act', lambda e, dst=dst: e.activation(out=dst, in_=t1, func=AF.Sin), reads=['rt1'], writes=[('trig', which)])

            pbi = [0]

            def nextbank(lo=0, hi=4):
                i = lo + pbi[0] % (hi - lo)
                pbi[0] += 1
                return i

            def p1_norm(T, blk):
                tb = T * 4 + blk
                s = tb % 2
                dma('sp', ('xt', s), lambda e, b=b: e.dma_start(out=xt[s], in_=x_d[b, tb * 128:(tb + 1) * 128, :]),
                    writes=[('xt', s)])
                rstd, rkey = rms_rstd([(xt[s], [('xt', s)])], 1.0 / D, junk, 'junk')
                op('dve', lambda e: e.scalar_tensor_tensor(out=xn[s], in0=xt[s], scalar=rstd, in1=gmix, op0=ALU.mult, op1=ALU.mult),
                   reads=[('xt', s), rkey, 'gmix'], writes=[('xn', s)])
                S_.wtag(('xn', s), tb)

            def p1_tr(T, blk):
                tb = T * 4 + blk
                s = tb % 2
                hs = T % 2
                pi = 6 + (tb % 2)
                ptr = banks[pi][:].bitcast(BF16)
                S_.rtag(('xn', s), tb)

                def tr8(e):
                    for kc in range(8):
                        ins = e.transpose(ptr[:, kc * 128:(kc + 1) * 128], xn[s][:, kc * 128:(kc + 1) * 128], ident)
                    return ins
                op('pe', tr8, reads=[('xn', s), 'cbf'], writes=[bk(pi)])
                evac(hT[hs][:, :, blk * 128:(blk + 1) * 128], ptr.rearrange("p (a b) -> p a b", a=8),
                     reads=[bk(pi)], pwrites=[('hT', hs)])

            qkb = [0]

            def p1_qk(T, oc):
                hs = T % 2
                hTs, hkey = hT[hs], ('hT', hs)
                bi = 4 + qkb[0] % 2
                qkb[0] += 1

                def mmqk(e):
                    for kc in range(8):
                        ins = e.matmul(banks[bi][:], lhsT=win[:, kc, oc * 128:(oc + 1) * 128], rhs=hTs[:, kc, :],
                                       start=(kc == 0), stop=(kc == 7))
                    return ins
                op('pe', mmqk, reads=[hkey] + [('win', kc) for kc in range(8)], writes=[bk(bi)])
                if oc < 4:
                    evac(QT[:, oc, T * 512:(T + 1) * 512], banks[bi][:], reads=[bk(bi)], pwrites=[('QT', oc)], scale=SB_SCALE)
                else:
                    c_ = oc - 4
                    evac(KTz[0:64, 2 * c_, T * 512:(T + 1) * 512], banks[bi][0:64, :], reads=[bk(bi)], pwrites=[('KT', c_)])
                    evac(KTz[64:128, 2 * c_ + 1, T * 512:(T + 1) * 512], banks[bi][64:128, :], reads=[bk(bi)], pwrites=[('KT', c_)])

            def p1_v(T, blk):
                tb = T * 4 + blk
                hs = T % 2
                hTs, hkey = hT[hs], ('hT', hs)
                bi = 4 + qkb[0] % 2
                qkb[0] += 1

                def mmv(e):
                    for kc in range(8):
                        ins = e.matmul(banks[bi][:], lhsT=hTs[:, kc, blk * 128:(blk + 1) * 128], rhs=win[:, kc, 1024:1536],
                                       start=(kc == 0), stop=(kc == 7))
                    return ins
                op('pe', mmv, reads=[hkey] + [('win', kc) for kc in range(8)], writes=[bk(bi)])
                evac(Vs[:, tb, :], banks[bi][:], reads=[bk(bi)], writes=[('V', tb)])

            def p1_c_mm(T, blk):
                tb = T * 4 + blk
                hs = T % 2
                hTs, hkey = hT[hs], ('hT', hs)
                ba, bb = (0, 1) if blk % 2 == 0 else (2, 3)

                def mmc(e):
                    for kc in range(8):
                        e.matmul(banks[ba][:, 0:416], lhsT=hTs[:, kc, blk * 128:(blk + 1) * 128], rhs=win[:, kc, 1536:1952],
                                 start=(kc == 0), stop=(kc == 7))
                    for kc in range(8):
                        ins = e.matmul(banks[bb][:, 0:256], lhsT=hTs[:, kc, blk * 128:(blk + 1) * 128], rhs=win[:, kc, 1952:2208],
                                       start=(kc == 0), stop=(kc == 7))
                    return ins
                op('pe', mmc, reads=[hkey] + [('win', kc) for kc in range(8)], writes=[bk(ba), bk(bb)])
                cs = tb % 2
                sck = ('sc', cs)
                scs = sc[cs]
                op('act', lambda e: e.activation(out=junk[:, 0:384], in_=banks[ba][:, 0:384], func=AF.Square, accum_out=scs[:, 0:1]),
                   reads=[bk(ba)], writes=['junk', sck])
                op('act', lambda e: e.activation(out=junk[:, 0:256], in_=banks[bb][:, 0:256], func=AF.Square, accum_out=scs[:, 1:2]),
                   reads=[bk(bb)], writes=['junk'], pwrites=[sck])
                op('act', lambda e: e.activation(out=kr_tm[:, tb, :], in_=banks[ba][:, 384:416], func=AF.Copy),
                   reads=[bk(ba)], pwrites=['kr_tm'])
                op('dve', lambda e: e.tensor_tensor(out=scs[:, 2:4], in0=scs[:, 0:2], in1=cscale[:, 0:2], op=ALU.mult),
                   reads=[sck, 'cscale'], writes=[sck])
                op('act', lambda e: e.activation(out=scs[:, 4:6], in_=scs[:, 2:4], func=AF.Sqrt, bias=EPS), reads=[sck], writes=[sck])
                op('dve', lambda e: e.reciprocal(scs[:, 6:8], scs[:, 4:6]), reads=[sck], writes=[sck])
                op('dve', lambda e: e.scalar_tensor_tensor(out=cn[cs][:, 0:384], in0=banks[ba][:, 0:384], scalar=scs[:, 6:7],
                                                           in1=gcb[:, 0:384], op0=ALU.mult, op1=ALU.mult),
                   reads=[bk(ba), sck, 'gcb'], writes=[('cn', cs)])
                op('dve', lambda e: e.scalar_tensor_tensor(out=cn[cs][:, 384:640], in0=banks[bb][:, 0:256], scalar=scs[:, 7:8],
                                                           in1=gcb[:, 384:640], op0=ALU.mult, op1=ALU.mult),
                   reads=[bk(bb), sck, 'gcb'], pwrites=[('cn', cs)])
                S_.wtag(('cn', cs), tb)

            def p1_c_tr(T, blk):
                tb = T * 4 + blk
                cs = tb % 2
                pi = 6 + (tb % 2)
                ptr = banks[pi][:].bitcast(BF16)
                S_.rtag(('cn', cs), tb)

                def tr5(e):
                    for i in range(5):
                        ins = e.transpose(ptr[:, i * 128:(i + 1) * 128], cn[cs][:, i * 128:(i + 1) * 128], ident)
                    return ins
                op('pe', tr5, reads=[('cn', cs), 'cbf'], writes=[bk(pi)])
                evac(cT[:, :, tb * 128:(tb + 1) * 128], ptr[:, 0:640].rearrange("p (a b) -> p a b", a=5),
                     reads=[bk(pi)], writes=[('cT', tb)])

            p1_norm(0, 0)
            for blk in range(4):
                if blk + 1 < 4:
                    p1_norm(0, blk + 1)
                p1_tr(0, blk)
            for T in range(NT):
                nxt = (T + 1 < NT)
                p1_c_mm(T, 0)
                p1_c_mm(T, 1)
                if nxt:
                    p1_norm(T + 1, 0)
                for oc in range(0, 4):
                    p1_qk(T, oc)
                p1_c_tr(T, 0)
                p1_c_tr(T, 1)
                p1_c_mm(T, 2)
                p1_c_mm(T, 3)
                if nxt:
                    p1_tr(T + 1, 0)
                    p1_norm(T + 1, 1)
                for oc in range(4, 8):
                    p1_qk(T, oc)
                p1_c_tr(T, 2)
                p1_c_tr(T, 3)
                if nxt:
                    p1_tr(T + 1, 1)
                    p1_norm(T + 1, 2)
                for blk in range(4):
                    p1_v(T, blk)
                    if nxt and blk == 1:
                        p1_tr(T + 1, 2)
                        p1_norm(T + 1, 3)
                if nxt:
                    p1_tr(T + 1, 3)
            x1v, x2v = kr_tm[:, :, 0:16], kr_tm[:, :, 16:32]
            op('dve', lambda e: e.tensor_tensor(out=rt[0], in0=x1v, in1=cos_t, op=ALU.mult), reads=['kr_tm', ('trig', 1)], writes=['rt0'])
            op('dve', lambda e: e.tensor_tensor(out=rt[1], in0=x2v, in1=sin_t, op=ALU.mult), reads=['kr_tm', ('trig', 0)], writes=['rt1'])
            op('dve', lambda e: e.tensor_tensor(out=kro[:, :, 0:16], in0=rt[0], in1=rt[1], op=ALU.subtract), reads=['rt0', 'rt1'], pwrites=['kro'])
            op('dve', lambda e: e.tensor_tensor(out=rt[2], in0=x2v, in1=cos_t, op=ALU.mult), reads=['kr_tm', ('trig', 1)], writes=['rt2'])
            op('dve', lambda e: e.tensor_tensor(out=rt[3], in0=x1v, in1=sin_t, op=ALU.mult), reads=['kr_tm', ('trig', 0)], writes=['rt3'])
            op('dve', lambda e: e.tensor_tensor(out=kro[:, :, 16:32], in0=rt[2], in1=rt[3], op=ALU.add), reads=['rt2', 'rt3'], pwrites=['kro'])

            S_.barrier()
            if PHASE_LIMIT < 2:
                S_.enabled = False

            A3 = Arena(arena, R1_END, ARENA_BYTES)
            e_sb = [A3.take([128, 512], F32) for _ in range(3)]
            sp_sb = [A3.take([128, 512], BF16) for _ in range(3)]
            a_sb = [A3.take([128, 512], BF16) for _ in range(3)]
            LKSb = [A3.take([128, S], BF16) for _ in range(2)]
            wuq = A3.take([128, 3, 768], BF16)
            wukv = A3.take([128, 2, 1024], BF16)
            qtm = [A3.take([128, 4, 128], BF16) for _ in range(2)]
            ktm = [A3.take([128, 4, 128], BF16) for _ in range(2)]
            rq = [A3.take([128, 4, 16], F32) for _ in range(4)]
            rec = [A3.take([128, 8], F32) for _ in range(4)]
            mpc = [0]
            qr = A3.take([128, 4, 32], F32)

            for kc in range(3):
                dma('pool', ('wuq', kc), lambda e, kc=kc: e.dma_start(out=wuq[:, kc, :], in_=wuq_v[:, kc, :]), writes=[('wuq', kc)])
            for kc in range(2):
                dma('pool', ('wukv', kc), lambda e, kc=kc: e.dma_start(out=wukv[:, kc, :], in_=wukv_v[:, kc, :]), writes=[('wukv', kc)])

            def pieces():
                lst = []
                for kb in range(NB - 1, -1, -1):
                    for c in range(NT):
                        lo = max(512 * c, kb * 128)
                        hi = 512 * c + 512
                        if hi > lo:
                            lst.append((kb, c, lo, hi))
                return lst

            QPB = 8
            n_ob = (NB + QPB - 1) // QPB
            assert n_ob <= 2
            pcs = pieces()
            sb_items = []
            for h in range(8):
                for idx, (kb, c, lo, hi) in enumerate(pcs):
                    sb_items.append(dict(h=h, kb=kb, c=c, lo=lo, hi=hi, p=len(sb_items), first=(idx == 0), last=(idx == len(pcs) - 1)))

            def sb_zmm(e, bank_ap, h, kb, lo, hi, start, stop):
                return e.matmul(bank_ap, lhsT=KTz[:, h, kb * 128:(kb + 1) * 128], rhs=QT[:, h // 2, lo:hi], start=start, stop=stop)

            def sb_s0(it):
                h, kb, c, lo, hi, p = it['h'], it['kb'], it['c'], it['lo'], it['hi'], it['p']
                n = hi - lo
                hp = h % 2
                ch = h // 2
                if it['first']:
                    op('pool', lambda e: e.memset(LKSb[hp], 0.0), writes=[('LKSb', hp, cc) for cc in range(NT)])
                zb = p % 2
                op('pe', lambda e: sb_zmm(e, banks[zb][:, 0:n], h, kb, lo, hi, True, True),
                   reads=[('QT', ch), ('KT', ch)], writes=[bk(zb)])
                S_.wtag(bk(zb), ('z', p))

            def sb_s1(it):
                kb, lo, hi, p = it['kb'], it['lo'], it['hi'], it['p']
                n = hi - lo
                zb, es_ = p % 2, p % 3
                S_.rtag(bk(zb), ('z', p))
                op('act', lambda e: e.activation(out=e_sb[es_][:, 0:n], in_=banks[zb][:, 0:n], func=AF.Exp),
                   reads=[bk(zb)], writes=[('e', es_)])
                op('act', lambda e: e.activation(out=sp_sb[es_][:, 0:n], in_=e_sb[es_][:, 0:n], func=AF.Ln, bias=1.0),
                   reads=[('e', es_)], writes=[('sp', es_)])
                if lo == kb * 128:
                    op('pool', lambda e: e.tensor_tensor(out=sp_sb[es_][:, 0:128], in0=sp_sb[es_][:, 0:128], in1=mstrict, op=ALU.mult),
                       reads=[('sp', es_), 'cbf'], writes=[('sp', es_)])
                S_.wtag(('sp', es_), p)

            def sb_s2(it):
                h, kb, c, lo, hi, p = it['h'], it['kb'], it['c'], it['lo'], it['hi'], it['p']
                n = hi - lo
                hp = h % 2
                ch = h // 2
                cbk, ss_ = 2 + p % 2, p % 3
                diag = (lo == kb * 128)
                has_lks = (n > 128) or (not diag)
                S_.rtag(('sp', ss_), p)

                def mmc2(e):
                    e.matmul(banks[cbk][:, 0:n], lhsT=ntri, rhs=sp_sb[ss_][:, 0:n], start=True, stop=False)
                    if has_lks:
                        e.matmul(banks[cbk][:, 0:n], lhsT=nones, rhs=LKSb[hp][:, lo:hi], start=False, stop=False)
                    return sb_zmm(e, banks[cbk][:, 0:n], h, kb, lo, hi, False, True)
                op('pe', mmc2, reads=[('sp', ss_), 'cbf', ('QT', ch), ('KT', ch)] + ([('LKSb', hp, c)] if has_lks else []), writes=[bk(cbk)])
                S_.wtag(bk(cbk), ('c', p))
                if kb > 0:
                    op('dve', lambda e: e.tensor_tensor(out=LKSb[hp][:, lo:hi], in0=LKSb[hp][:, lo:hi], in1=sp_sb[ss_][:, 0:n], op=ALU.add),
                       reads=[('sp', ss_)], writes=[('LKSb', hp, c)])

            def sb_s3(it):
                kb, lo, hi, p = it['kb'], it['lo'], it['hi'], it['p']
                n = hi - lo
                cbk, as_ = 2 + p % 2, p % 3
                S_.rtag(bk(cbk), ('c', p))
                op('act', lambda e: e.activation(out=a_sb[as_][:, 0:n], in_=banks[cbk][:, 0:n], func=AF.Exp),
                   reads=[bk(cbk)], writes=[('a', as_)])
                if lo == kb * 128:
                    op('pool', lambda e: e.tensor_tensor(out=a_sb[as_][:, 0:128], in0=a_sb[as_][:, 0:128], in1=mstrict, op=ALU.mult),
                       reads=[('a', as_), 'cbf'], writes=[('a', as_)])
                S_.wtag(('a', as_), p)

            def sb_s4(it):
                h, kb, lo, hi, p = it['h'], it['kb'], it['lo'], it['hi'], it['p']
                as_ = p % 3
                ob0 = 4 if h % 2 == 0 else 6
                obanks = [ob0 + i for i in range(n_ob)]
                S_.rtag(('a', as_), p)
                if it['first']:
                    def zero_o(e):
                        for bi in obanks:
                            ins = e.matmul(banks[bi][:], lhsT=zer[:, 0:128], rhs=zer[:, :], start=True, stop=False, skip_group_check=True)
                        return ins
                    op('pe', zero_o, reads=['zer'], writes=[bk(bi) for bi in obanks])

                def mmav(e):
                    for qb in range(lo // 128, hi // 128):
                        bi = ob0 + qb // QPB
                        sl = (qb % QPB) * 64
                        last = (kb == 0) and (qb % QPB == QPB - 1 or qb == NB - 1)
                        ins = e.matmul(banks[bi][:, sl:sl + 64], lhsT=a_sb[as_][:, qb * 128 - lo:qb * 128 - lo + 128],
                                       rhs=Vs[:, kb, h * 64:(h + 1) * 64], start=False, stop=last, skip_group_check=True)
                    return ins
                obs = sorted(set(ob0 + qb // QPB for qb in range(lo // 128, hi // 128)))
                op('pe', mmav, reads=[('a', as_), ('V', kb)], pwrites=[bk(bi) for bi in obs])
                if it['last']:
                    for i, bi in enumerate(obanks):
                        nq = min(QPB, NB - i * QPB)
                        evac(o_tm[:, i * QPB:i * QPB + nq, h * 64:(h + 1) * 64],
                             banks[bi][:, 0:nq * 64].rearrange("p (a b) -> p a b", a=nq),
                             reads=[bk(bi)], pwrites=[('o', i)])

            run_pipeline(sb_items, [(0, sb_s0), (1, sb_s1), (2, sb_s2), (3, sb_s3), (4, sb_s4)])

            S_.barrier()
            if PHASE_LIMIT < 3:
                S_.enabled = False

            if OPLIM is not None:
                S_.oplimit = OPLIM
            QTm, KTm = QT, KT
            Vm = Vs[:, :, 0:260].rearrange("p a (h d) -> p a h d", h=4)
            MQ = 7
            n_mb = (NB + MQ - 1) // MQ
            for g in range(2):
                op('pool', lambda e: e.memset(Vs_flat, 1.0), writes=[('V', tb) for tb in range(NB)])
                if g == 0:
                    for qs_ in range(2):
                        op('pool', lambda e, qs_=qs_: e.memset(qtm[qs_].rearrange("p a b -> p (a b)"), 0.0), writes=[('qtm', qs_)])
                        op('pool', lambda e, qs_=qs_: e.memset(ktm[qs_].rearrange("p a b -> p (a b)"), 0.0), writes=[('ktm', qs_)])
                def mp_banks(tb):
                    return (0, 1) if tb % 2 == 0 else (2, 3)

                def mp_A(tb, g=g):
                    bq, bkv = mp_banks(tb)

                    def mmq(e):
                        for kc in range(3):
                            ins = e.matmul(banks[bq][:, 0:384], lhsT=cT[:, kc, tb * 128:(tb + 1) * 128], rhs=wuq[:, kc, g * 384:(g + 1) * 384],
                                           start=(kc == 0), stop=(kc == 2))
                        return ins
                    op('pe', mmq, reads=[('cT', tb)] + [('wuq', kc) for kc in range(3)], writes=[bk(bq)])

                    def mmkv(e):
                        for kc in range(2):
                            ins = e.matmul(banks[bkv][:, :], lhsT=cT[:, 3 + kc, tb * 128:(tb + 1) * 128], rhs=wukv[:, kc, g * 512:(g + 1) * 512],
                                           start=(kc == 0), stop=(kc == 1))
                        return ins
                    op('pe', mmkv, reads=[('cT', tb)] + [('wukv', kc) for kc in range(2)], writes=[bk(bkv)])
                    S_.wtag(bk(bq), ('mq', g, tb))

                def mp_B(tb, g=g):
                    qs = tb % 2
                    bq, bkv = mp_banks(tb)
                    S_.rtag(bk(bq), ('mq', g, tb))
                    qv = banks[bq][:, 0:384].rearrange("p (h d) -> p h d", h=4)
                    kvv = banks[bkv][:, :].rearrange("p (h d) -> p h d", h=4)
                    cosb = cos_t[:, tb, :].unsqueeze(1).to_broadcast([128, 4, 16])
                    sinb = sin_t[:, tb, :].unsqueeze(1).to_broadcast([128, 4, 16])
                    qk_ = ('qtm', qs)
                    kk_ = ('ktm', qs)
                    op('act', lambda e: e.activation(out=qr, in_=qv[:, :, 64:96], func=AF.Copy), reads=[bk(bq)], writes=['qr'])
                    op('act', lambda e: e.activation(out=qtm[qs][:, :, 0:64], in_=qv[:, :, 0:64], func=AF.Copy),
                       reads=[bk(bq)], pwrites=[qk_])
                    op('act', lambda e: e.activation(out=ktm[qs][:, :, 0:64], in_=kvv[:, :, 0:64], func=AF.Copy),
                       reads=[bk(bkv)], pwrites=[kk_])
                    op('dve', lambda e: e.tensor_tensor(out=rq[0], in0=qr[:, :, 0:16], in1=cosb, op=ALU.mult), reads=['qr', ('trig', 1)], writes=['rq0'])
                    op('dve', lambda e: e.tensor_tensor(out=rq[1], in0=qr[:, :, 16:32], in1=sinb, op=ALU.mult), reads=['qr', ('trig', 0)], writes=['rq1'])
                    op('dve', lambda e: e.tensor_tensor(out=qtm[qs][:, :, 64:80], in0=rq[0], in1=rq[1], op=ALU.subtract), reads=['rq0', 'rq1'], pwrites=[qk_])
                    op('dve', lambda e: e.tensor_tensor(out=rq[2], in0=qr[:, :, 16:32], in1=cosb, op=ALU.mult), reads=['qr', ('trig', 1)], writes=['rq2'])
                    op('dve', lambda e: e.tensor_tensor(out=rq[3], in0=qr[:, :, 0:16], in1=sinb, op=ALU.mult), reads=['qr', ('trig', 0)], writes=['rq3'])
                    op('dve', lambda e: e.tensor_tensor(out=qtm[qs][:, :, 80:96], in0=rq[2], in1=rq[3], op=ALU.add), reads=['rq2', 'rq3'], pwrites=[qk_])
                    for hl_ in range(4):
                        op('pool', lambda e, hl_=hl_: e.tensor_copy(ktm[qs][:, hl_, 64:96], kro[:, tb, :]),
                           reads=['kro'], pwrites=[kk_])
                    op('dve', lambda e: e.tensor_copy(Vm[:, tb, :, 0:64], kvv[:, :, 64:128]), reads=[bk(bkv)], pwrites=[('V', tb)])
                    S_.wtag(qk_, (g, tb))

                def mp_C(tb, g=g):
                    qs = tb % 2
                    qk_ = ('qtm', qs)
                    kk_ = ('ktm', qs)
                    S_.rtag(qk_, (g, tb))
                    pi = 6 + (tb % 2)
                    ptr = banks[pi][:].bitcast(BF16)

                    def trqk(e):
                        for hl in range(4):
                            e.transpose(ptr[:, hl * 128:(hl + 1) * 128], qtm[qs][:, hl, :], ident)
                        for hl in range(4):
                            ins = e.transpose(ptr[:, (4 + hl) * 128:(5 + hl) * 128], ktm[qs][:, hl, :], ident)
                        return ins
                    op('pe', trqk, reads=[qk_, kk_, 'cbf'], writes=[bk(pi)])
                    evac(QTm[:, :, tb * 128:(tb + 1) * 128], ptr[:, 0:512].rearrange("p (a b) -> p a b", a=4),
                         reads=[bk(pi)], pwrites=[('QT', hl) for hl in range(4)])
                    evac(KTm[:, :, tb * 128:(tb + 1) * 128], ptr[:, 512:1024].rearrange("p (a b) -> p a b", a=4),
                         reads=[bk(pi)], pwrites=[('KT', hl) for hl in range(4)])

                run_pipeline(list(range(NB)), [(0, mp_A), (1, mp_B), (2, mp_C)])
                if P3_MODE == 'proj':
                    S_.enabled = False
                m_items = []
                for hl in range(4):
                    for idx, (kb, c, lo, hi) in enumerate(pcs):
                        m_items.append(dict(hl=hl, kb=kb, c=c, lo=lo, hi=hi, idx=len(m_items), first=(idx == 0), last=(idx == len(pcs) - 1)))

                def ml_s0(it, g=g):
                    hl, kb, lo, hi = it['hl'], it['kb'], it['lo'], it['hi']
                    n = hi - lo
                    p = mpc[0] + it['idx']
                    zb = (0, 1, 5, 6)[p % 4]
                    op('pe', lambda e: e.matmul(banks[zb][:, 0:n], lhsT=KTm[:, hl, kb * 128:(kb + 1) * 128], rhs=QTm[:, hl, lo:hi], start=True, stop=True),
                       reads=[('QT', hl), ('KT', hl)], writes=[bk(zb)])
                    S_.wtag(bk(zb), ('zm', p))

                def ml_s1(it, g=g):
                    kb, lo, hi = it['kb'], it['lo'], it['hi']
                    n = hi - lo
                    p = mpc[0] + it['idx']
                    zb, as_ = (0, 1, 5, 6)[p % 4], p % 3
                    S_.rtag(bk(zb), ('zm', p))
                    op('act', lambda e: e.activation(out=a_sb[as_][:, 0:n], in_=banks[zb][:, 0:n], func=AF.Exp, scale=MLA_SCALE),
                       reads=[bk(zb)], writes=[('a', as_)])
                    if lo == kb * 128:
                        op('pool', lambda e: e.tensor_tensor(out=a_sb[as_][:, 0:128], in0=a_sb[as_][:, 0:128], in1=mincl, op=ALU.mult),
                           reads=[('a', as_), 'cbf'], writes=[('a', as_)])
                    S_.wtag(('a', as_), ('m', p))

                def ml_s2(it, g=g):
                    hl, kb, lo, hi = it['hl'], it['kb'], it['lo'], it['hi']
                    h = g * 4 + hl
                    p = mpc[0] + it['idx']
                    as_ = p % 3
                    ob0 = 2
                    obanks = [ob0 + i for i in range(n_mb)]
                    S_.rtag(('a', as_), ('m', p))
                    if it['first']:
                        def zero_m(e):
                            for bi in obanks:
                                ins = e.matmul(banks[bi][:], lhsT=zer[:, 0:128], rhs=zer[:, :], start=True, stop=False, skip_group_check=True)
                            return ins
                        op('pe', zero_m, reads=['zer'], writes=[bk(bi) for bi in obanks])

                    def mmavm(e):
                        for qb in range(lo // 128, hi // 128):
                            bi = ob0 + qb // MQ
                            sl = (qb % MQ) * 65
                            last = (kb == 0) and (qb % MQ == MQ - 1 or qb == NB - 1)
                            ins = e.matmul(banks[bi][:, sl:sl + 65], lhsT=a_sb[as_][:, qb * 128 - lo:qb * 128 - lo + 128],
                                           rhs=Vm[:, kb, hl, :], start=False, stop=last, skip_group_check=True)
                        return ins
                    obs = sorted(set(ob0 + qb // MQ for qb in range(lo // 128, hi // 128)))
                    op('pe', mmavm, reads=[('a', as_), ('V', kb)], pwrites=[bk(bi) for bi in obs])
                    if it['last']:
                        for i, bi in enumerate(obanks):
                            nq = min(MQ, NB - i * MQ)
                            ov = banks[bi][:, 0:nq * 65].rearrange("p (a b) -> p a b", a=nq)
                            rs = rec[(h * 4 + i) % 4]
                            rk = ('rec', (h * 4 + i) % 4)
                            op('dve', lambda e, ov=ov, nq=nq, rs=rs: e.reciprocal(rs[:, 0:nq], ov[:, :, 64]), reads=[bk(bi)], writes=[rk])
                            qbs = range(i * MQ, i * MQ + nq)
                            okeys = sorted(set(('o', qb // QPB) for qb in qbs))
                            op('dve', lambda e, ov=ov, nq=nq, i=i, h=h, rs=rs: e.tensor_tensor(
                                out=o_tm[:, i * MQ:i * MQ + nq, 512 + h * 64:512 + (h + 1) * 64], in0=ov[:, :, 0:64],
                                in1=rs[:, 0:nq].unsqueeze(2).to_broadcast([128, nq, 64]), op=ALU.mult),
                               reads=[bk(bi), rk], pwrites=okeys)

                run_pipeline(m_items, [(0, ml_s0), (2, ml_s1), (3, ml_s2)])
                mpc[0] += len(m_items)

            S_.barrier()
            if PHASE_LIMIT < 4:
                S_.enabled = False

            A4 = Arena(arena, 0, ARENA_BYTES)
            if DEBUG:
                dma('sp', 'dbg', lambda e, b=b: e.dma_start(out=dbg_d[b], in_=o_tm[:]), reads=[('o', i) for i in range(n_ob)], final=True)
            wout = A4.take([128, 8, D], BF16)
            wdn = A4.take([128, NJ, D], BF16)
            gob = A4.take([128, D], F32)
            gffn = A4.take([128, D], F32)
            gfin = A4.take([128, D], F32)
            xn4a = [A4.take([128, D], BF16) for _ in range(2)]
            xn4b = [A4.take([128, D], BF16) for _ in range(2)]
            junk4 = A4.take([128, D], BF16)
            oT = [A4.take([128, 8, 128], BF16) for _ in range(2)]
            h2T = A4.take([128, 8, 512], BF16)
            x1b = [A4.take([128, 4, D], F32) for _ in range(2)]
            gT = A4.take([128, NJ, 512], BF16)
            wup = [A4.take([128, 8, 256], BF16) for _ in range(3)]
            cg = [A4.take([128, 512], F32) for _ in range(2)]
            cv = [A4.take([128, 512], F32) for _ in range(2)]
            sg = [A4.take([128, 512], BF16) for _ in range(2)]
            halo = [A4.take([128, 2 * NJ, 2], F32) for _ in range(2)]
            so = [A4.take([128, 8], F32) for _ in range(4)]

            for kc in range(8):
                dma('pool', ('wout', kc), lambda e, kc=kc: e.dma_start(out=wout[:, kc, :], in_=wout_v[:, kc, :]), writes=[('wout', kc)])
            for j0 in range(0, NJ, 2):
                dma('pool', ('wdn', j0), lambda e, j0=j0: e.dma_start(out=wdn[:, j0:j0 + 2, :], in_=wdn_v[:, j0:j0 + 2, :]), writes=[('wdn', j0), ('wdn', j0 + 1)])
            dma('sp', 'gob', lambda e: e.dma_start(out=gob, in_=go_d[:, :]), writes=['gob'])
            dma('sp', 'gffn', lambda e: e.dma_start(out=gffn, in_=gffn_d[:, :]), writes=['gffn'])
            dma('sp', 'gfin', lambda e: e.dma_start(out=gfin, in_=gfin_d[:, :]), writes=['gfin'])
            wu_ptr = [0]

            def ffn_dma_upto(g_hi):
                while wu_ptr[0] <= g_hi and wu_ptr[0] < NT * NJ:
                    g_ = wu_ptr[0]
                    wu_ptr[0] += 1
                    ws, j = g_ % 3, g_ % NJ
                    dma('pool', ('wup', ws), lambda e, ws=ws, j=j: e.dma_start(out=wup[ws], in_=wup_v[:, :, j * 256:(j + 1) * 256]), writes=[('wup', ws)])
                    S_.wtag(('wup', ws), g_)

            def ffn_mmu(T, j):
                g_ = T * NJ + j
                ws = g_ % 3
                bg, bv = (0, 1) if j % 2 == 0 else (2, 3)
                S_.rtag(('wup', ws), g_)

                def mmu(e):
                    for kc in range(8):
                        e.matmul(banks[bg][:], lhsT=wup[ws][:, kc, 0:128], rhs=h2T[:, kc, :], start=(kc == 0), stop=(kc == 7))
                    for kc in range(8):
                        ins = e.matmul(banks[bv][:], lhsT=wup[ws][:, kc, 128:256], rhs=h2T[:, kc, :], start=(kc == 0), stop=(kc == 7))
                    return ins
                op('pe', mmu, reads=[('wup', ws), 'h2T'], writes=[bk(bg), bk(bv)])
                S_.wtag(bk(bg), ('u', g_))

            def ffn_conv(T, j):
                g_ = T * NJ + j
                bg, bv = (0, 1) if j % 2 == 0 else (2, 3)
                S_.rtag(bk(bg), ('u', g_))
                us = j % 2
                hw, hr = halo[T % 2], halo[(T + 1) % 2]
                halves = ((0, bg, cg[us], ('cg', us)), (1, bv, cv[us], ('cv', us)))
                for (half, bi, dst, dkey) in halves:
                    cj = j * 2 + half
                    w2 = cw[:, cj * 4 + 2:cj * 4 + 3]
                    bb_ = cw[:, cj * 4 + 3:cj * 4 + 4]
                    op('act', lambda e, bi=bi, dst=dst, w2=w2, bb_=bb_: e.activation(out=dst, in_=banks[bi][:], func=AF.Identity, scale=w2, bias=bb_),
                       reads=[bk(bi), 'cw'], writes=[dkey])
                    if T < NT - 1:
                        op('act', lambda e, bi=bi, cj=cj: e.activation(out=hw[:, cj, :], in_=banks[bi][:, 510:512], func=AF.Copy),
                           reads=[bk(bi)], writes=[('halo', T % 2, cj)])
                for (half, bi, dst, dkey) in halves:
                    cj = j * 2 + half
                    w0 = cw[:, cj * 4 + 0:cj * 4 + 1]
                    w1 = cw[:, cj * 4 + 1:cj * 4 + 2]
                    hk = ('halo', (T + 1) % 2, cj)
                    op('dve', lambda e, bi=bi, dst=dst, w1=w1: e.scalar_tensor_tensor(out=dst[:, 1:512], in0=banks[bi][:, 0:511], scalar=w1, in1=dst[:, 1:512],
                                                                                   op0=ALU.mult, op1=ALU.add),
                       reads=[bk(bi), dkey, 'cw'], writes=[dkey])
                    op('dve', lambda e, bi=bi, dst=dst, w0=w0: e.scalar_tensor_tensor(out=dst[:, 2:512], in0=banks[bi][:, 0:510], scalar=w0, in1=dst[:, 2:512],
                                                                                   op0=ALU.mult, op1=ALU.add),
                       reads=[bk(bi), dkey, 'cw'], writes=[dkey])
                    if T > 0:
                        op('dve', lambda e, dst=dst, w1=w1, cj=cj: e.scalar_tensor_tensor(out=dst[:, 0:1], in0=hr[:, cj, 1:2], scalar=w1, in1=dst[:, 0:1],
                                                                                       op0=ALU.mult, op1=ALU.add),
                           reads=[hk, dkey, 'cw'], writes=[dkey])
                        op('dve', lambda e, dst=dst, w0=w0, cj=cj: e.scalar_tensor_tensor(out=dst[:, 0:2], in0=hr[:, cj, 0:2], scalar=w0, in1=dst[:, 0:2],
                                                                                       op0=ALU.mult, op1=ALU.add),
                           reads=[hk, dkey, 'cw'], writes=[dkey])
                    if half == 0:
                        op('act', lambda e: e.activation(out=sg[us], in_=cg[us], func=AF.Silu), reads=[('cg', us)], writes=[('sg', us)])
                op('pool', lambda e: e.tensor_tensor(out=gT[:, j, :], in0=sg[us], in1=cv[us], op=ALU.mult),
                   reads=[('sg', us), ('cv', us)], writes=[('gT', j)])

            def pro_A(T, i):
                tb = T * 4 + i
                par, s = T % 2, i % 2
                x1 = x1b[par]
                okey = ('o', tb // QPB)
                dma('sp', ('xin', par, i), lambda e, b=b: e.dma_start(out=x1[:, i, :], in_=x_d[b, tb * 128:(tb + 1) * 128, :]),
                    writes=[('x1', par, i)])
                sos = so[i]
                sok = ('so', i)
                op('act', lambda e: e.activation(out=junk4[:, 0:512], in_=o_tm[:, tb, 0:512], func=AF.Square, accum_out=sos[:, 0:1]),
                   reads=[okey], writes=['junk4', sok])
                op('act', lambda e: e.activation(out=junk4[:, 512:1024], in_=o_tm[:, tb, 512:1024], func=AF.Square, accum_out=sos[:, 1:2]),
                   reads=[okey], pwrites=['junk4', sok])
                op('act', lambda e: e.activation(out=sos[:, 2:4], in_=sos[:, 0:2], func=AF.Sqrt, scale=1.0 / 512.0, bias=EPS), reads=[sok], writes=[sok])
                op('dve', lambda e: e.reciprocal(sos[:, 4:6], sos[:, 2:4]), reads=[sok], writes=[sok])
                for half in range(2):
                    op('dve', lambda e, half=half: e.scalar_tensor_tensor(
                        out=xn4a[s][:, half * 512:(half + 1) * 512], in0=o_tm[:, tb, half * 512:(half + 1) * 512], scalar=sos[:, 4 + half:5 + half],
                        in1=gob[:, half * 512:(half + 1) * 512], op0=ALU.mult, op1=ALU.mult),
                       reads=[okey, sok, 'gob'], **({'writes': [('xn4a', s)]} if half == 0 else {'pwrites': [('xn4a', s)]}))
                S_.wtag(('xn4a', s), tb)

            def pro_B(T, i):
                tb = T * 4 + i
                s = i % 2
                pi = 6 + s
                ptr = banks[pi][:].bitcast(BF16)
                S_.rtag(('xn4a', s), tb)

                def tro(e):
                    for kc in range(8):
                        ins = e.transpose(ptr[:, kc * 128:(kc + 1) * 128], xn4a[s][:, kc * 128:(kc + 1) * 128], ident)
                    return ins
                op('pe', tro, reads=[('xn4a', s), 'cbf'], writes=[bk(pi)])
                evac(oT[s], ptr.rearrange("p (a b) -> p a b", a=8), reads=[bk(pi)], writes=[('oT', s)])
                S_.wtag(('oT', s), tb)

            def pro_C(T, i):
                tb = T * 4 + i
                par, s = T % 2, i % 2
                x1 = x1b[par]
                S_.rtag(('oT', s), tb)
                for half in range(2):
                    bi = 4 + half

                    def mmo(e, half=half, bi=bi):
                        for kc in range(8):
                            ins = e.matmul(banks[bi][:], lhsT=oT[s][:, kc, :], rhs=wout[:, kc, half * 512:(half + 1) * 512],
                                           start=(kc == 0), stop=(kc == 7))
                        return ins
                    op('pe', mmo, reads=[('oT', s)] + [('wout', kc) for kc in range(8)], writes=[bk(bi)])
                    op('dve', lambda e, bi=bi, half=half: e.tensor_tensor(
                        out=x1[:, i, half * 512:(half + 1) * 512], in0=banks[bi][:], in1=x1[:, i, half * 512:(half + 1) * 512], op=ALU.add),
                       reads=[bk(bi), ('x1', par, i)], pwrites=[('x1', par, i)])

            def pro_D(T, i):
                tb = T * 4 + i
                par, s = T % 2, i % 2
                x1 = x1b[par]
                rstd, rkey = rms_rstd([(x1[:, i, :], [('x1', par, i)])], 1.0 / D, junk4, 'junk4')
                op('dve', lambda e: e.scalar_tensor_tensor(out=xn4b[s], in0=x1[:, i, :], scalar=rstd, in1=gffn, op0=ALU.mult, op1=ALU.mult),
                   reads=[('x1', par, i), rkey, 'gffn'], writes=[('xn4b', s)])
                S_.wtag(('xn4b', s), tb)

            def pro_E(T, i):
                tb = T * 4 + i
                s = i % 2
                pi = 6 + s
                ptr = banks[pi][:].bitcast(BF16)
                S_.rtag(('xn4b', s), tb)

                def tro(e):
                    for kc in range(8):
                        ins = e.transpose(ptr[:, kc * 128:(kc + 1) * 128], xn4b[s][:, kc * 128:(kc + 1) * 128], ident)
                    return ins
                op('pe', tro, reads=[('xn4b', s), 'cbf'], writes=[bk(pi)])
                evac(h2T[:, :, i * 128:(i + 1) * 128], ptr.rearrange("p (a b) -> p a b", a=8), reads=[bk(pi)], pwrites=['h2T'])

            def epi_A(T, i):
                par = T % 2
                x1 = x1b[par]
                for half in range(2):
                    bi = (i % 2) * 2 + half

                    def mmd(e, half=half, bi=bi):
                        for j in range(NJ):
                            ins = e.matmul(banks[bi][:], lhsT=gT[:, j, i * 128:(i + 1) * 128], rhs=wdn[:, j, half * 512:(half + 1) * 512],
                                           start=(j == 0), stop=(j == NJ - 1))
                        return ins
                    op('pe', mmd, reads=[('gT', j) for j in range(NJ)] + [('wdn', j) for j in range(NJ)], writes=[bk(bi)])
                    op('dve', lambda e, bi=bi, half=half: e.tensor_tensor(
                        out=x1[:, i, half * 512:(half + 1) * 512], in0=banks[bi][:], in1=x1[:, i, half * 512:(half + 1) * 512], op=ALU.add),
                       reads=[bk(bi), ('x1', par, i)], pwrites=[('x1', par, i)])

            def epi_B(T, i):
                tb = T * 4 + i
                par = T % 2
                x1 = x1b[par]
                rstd, rkey = rms_rstd([(x1[:, i, :], [('x1', par, i)])], 1.0 / D, junk4, 'junk4')
                op('dve', lambda e: e.scalar_tensor_tensor(out=x1[:, i, :], in0=x1[:, i, :], scalar=rstd, in1=gfin, op0=ALU.mult, op1=ALU.mult),
                   reads=[rkey, 'gfin'], writes=[('x1', par, i)])
                dma('sp', ('yout', par, i), lambda e, b=b: e.dma_start(out=out_d[b, tb * 128:(tb + 1) * 128, :], in_=x1[:, i, :]),
                    reads=[('x1', par, i)], final=True)

            def pro_epi(Tp, Te):
                stages = []
                if Tp is not None:
                    stages.append((0, lambda i: pro_A(Tp, i)))
                if Te is not None:
                    stages.append((0, lambda i: epi_A(Te, i)))
                    stages.append((1, lambda i: epi_B(Te, i)))
                if Tp is not None:
                    stages.append((1, lambda i: pro_B(Tp, i)))
                    stages.append((2, lambda i: pro_C(Tp, i)))
                    stages.append((2, lambda i: pro_D(Tp, i)))
                    stages.append((3, lambda i: pro_E(Tp, i)))
                stages.sort(key=lambda t: t[0])
                run_pipeline(list(range(4)), stages)

            ffn_dma_upto(1)
            pro_epi(0, None)
            for T in range(NT):
                for j in range(NJ):
                    ffn_dma_upto(T * NJ + j + 2)
                    ffn_mmu(T, j)
                    if j >= 1:
                        ffn_conv(T, j - 1)
                ffn_conv(T, NJ - 1)
                pro_epi(T + 1 if T + 1 < NT else None, T)

            S_.barrier()

        S_.emit()
    return nc


def _prep_shared(inputs):
    f32 = np.float32
    w_in = np.asarray(inputs["w_in"][0], f32)
    perm = np.concatenate([np.arange(0, 1536), np.arange(1536, 1920), np.arange(2176, 2208), np.arange(1920, 2176)])
    w_in_p = np.ascontiguousarray(w_in[:, perm])
    w_up = np.asarray(inputs["w_up"][0], f32)
    cols = np.concatenate([np.concatenate([np.arange(j * 128, (j + 1) * 128), DFF + np.arange(j * 128, (j + 1) * 128)]) for j in range(NJ)])
    w_up_p = np.ascontiguousarray(w_up[:, cols])
    conv_w = np.asarray(inputs["conv_w"][0], f32)
    conv_b = np.asarray(inputs["conv_b"][0], f32)
    cw = np.zeros((128, 2 * NJ, 4), f32)
    for j in range(NJ):
        for half in range(2):
            ch = half * DFF + j * 128 + np.arange(128)
            cw[:, j * 2 + half, 0] = conv_w[0, ch]
            cw[:, j * 2 + half, 1] = conv_w[1, ch]
            cw[:, j * 2 + half, 2] = conv_w[2, ch]
            cw[:, j * 2 + half, 3] = conv_b[ch]

    def bc(v):
        return np.ascontiguousarray(np.broadcast_to(np.asarray(v, f32)[None, :], (128, v.shape[0])))
    g_c = np.concatenate([np.asarray(inputs["g_cq"][0], f32), np.asarray(inputs["g_ckv"][0], f32)])
    g_o = np.concatenate([np.asarray(inputs["g_sb_out"][0], f32), np.asarray(inputs["g_mla_out"][0], f32)])
    ii = np.arange(128)
    ident = np.eye(128, dtype=f32)
    tri = (ii[:, None] >= ii[None, :]).astype(f32)
    ones = np.ones((128, 128), f32)
    mstrict = (ii[:, None] < ii[None, :]).astype(f32)
    mincl = (ii[:, None] <= ii[None, :]).astype(f32)
    cbf = np.concatenate([ident, tri, ones, mstrict, mincl, -tri, -ones], axis=1).astype(ml_dtypes.bfloat16)
    half = 16
    inv_freq = (1.0 / (np.float32(10000.0) ** (np.arange(half, dtype=f32) * np.float32(2.0 / 32)))).astype(f32)
    return {
        "w_in": w_in_p,
        "w_uq": np.ascontiguousarray(np.asarray(inputs["w_uq"][0], f32)),
        "w_ukv": np.ascontiguousarray(np.asarray(inputs["w_ukv"][0], f32)),
        "w_out": np.ascontiguousarray(np.asarray(inputs["w_out"][0], f32)),
        "w_up": w_up_p,
        "w_down": np.ascontiguousarray(np.asarray(inputs["w_down"][0], f32)),
        "g_mix": bc(inputs["g_mix"][0]),
        "g_c": bc(g_c),
        "g_o": bc(g_o),
        "g_ffn": bc(inputs["g_ffn"][0]),
        "g_final": bc(inputs["g_final"]),
        "cw": np.ascontiguousarray(cw.reshape(128, 2 * NJ * 4)),
        "cbf": cbf,
        "invf": bc(inv_freq),
    }


_PROG_CACHE = {}


def kernel(**inputs):
    x = np.asarray(inputs["x"], np.float32)
    pos = np.asarray(inputs["positions"], np.int32)
    B, S, _ = x.shape
    ncores = NCORES if B % NCORES == 0 else B
    nseq = B // ncores
    shared = _prep_shared(inputs)
    key = (S, nseq)
    if key not in _PROG_CACHE:
        _PROG_CACHE[key] = build_program(S, nseq)
    nc = _PROG_CACHE[key]
    NB = S // 128
    in_maps = []
    for c in range(ncores):
        m = dict(shared)
        m["x"] = np.ascontiguousarray(x[c * nseq:(c + 1) * nseq])
        p = pos[c * nseq:(c + 1) * nseq].reshape(nseq, NB, 128).transpose(0, 2, 1)
        m["pos"] = np.ascontiguousarray(p)
        in_maps.append(m)
    res = run_bass_kernel_spmd(nc, in_maps, core_ids=list(range(ncores)))
    if DEBUG:
        global LAST_RES
        LAST_RES = res
    out = np.concatenate([np.asarray(r["out"], np.float32) for r in res.results], axis=0)
    return out
```

```python
import contextlib
import math
import numpy as np
import ml_dtypes
import concourse.bass as bass
import concourse.mybir as mybir
from concourse.bass_utils import run_bass_kernel_spmd

F32 = mybir.dt.float32
BF16 = mybir.dt.bfloat16
I32 = mybir.dt.int32
U8 = mybir.dt.uint8
AF = mybir.ActivationFunctionType
ALU = mybir.AluOpType

D = 1024
NCORES = 8
EPS = 1e-6
DFF = 2816
NJ = DFF // 128
SB_SCALE = 64 ** -0.5
MLA_SCALE = 96 ** -0.5
TWO_PI = 2.0 * math.pi
CW1 = 6.28125
CW2 = TWO_PI - CW1
PI_SAFE = 3.1415925
PHASE_LIMIT = 4
P3_MODE = 'all'
OPLIM = None
DEBUG = False
LAST_RES = None


class Sched:
    ENGS = ('pe', 'act', 'dve', 'pool', 'sp')

    def __init__(self, nc, es):
        self.nc = nc
        self.es = es
        self.streams = {e: [] for e in self.ENGS}
        self.cnt = {e: 0 for e in self.ENGS}
        self.esem = {e: es.enter_context(nc.semaphore("sem_" + e)) for e in self.ENGS}
        self.waited = {e: {} for e in self.ENGS}
        self.bufs = {}
        self.dsem = {}
        self.semobj = {}
        self.final_tokens = []
        self.tags = {}

    def _st(self, b):
        return self.bufs.setdefault(b, {'w': [], 'r': []})

    def _deps(self, reads, writes, pwrites):
        toks = []
        for b in reads:
            st = self._st(b)
            toks += st['w']
            if isinstance(b, tuple) and b[0] == 'pb':
                toks += st['r']
        for b in writes:
            st = self._st(b)
            toks += st['w'] + st['r']
        for b in pwrites:
            st = self._st(b)
            toks += st['w'] + st['r']
        return toks

    def _update(self, tok, reads, writes, pwrites):
        for b in reads:
            self.bufs[b]['r'].append(tok)
        for b in writes:
            self.bufs[b]['w'] = [tok]
            self.bufs[b]['r'] = []
        for b in pwrites:
            st = self.bufs[b]
            st['w'].append(tok)
            if len(st['w']) > 64:
                best = {}
                for t in st['w']:
                    k = id(t[0])
                    if k not in best or best[k][1] < t[1]:
                        best[k] = t
                st['w'] = list(best.values())
            st['r'] = []

    def _waits(self, eng, toks):
        need = {}
        for (sem, val, src) in toks:
            if src == eng and eng == 'pe':
                continue
            k = id(sem)
            self.semobj[k] = sem
            if self.waited[eng].get(k, 0) >= val:
                continue
            if need.get(k, 0) < val:
                need[k] = val
        out = []
        for k, v in need.items():
            self.waited[eng][k] = v
            out.append((self.semobj[k], v))
        return out

    enabled = True

    oplimit = None

    def op(self, eng, fn, reads=(), writes=(), pwrites=()):
        if not self.enabled:
            return None
        if self.oplimit is not None:
            if self.oplimit <= 0:
                return None
            self.oplimit -= 1
        toks = self._deps(reads, writes, pwrites)
        waits = self._waits(eng, toks)
        self.cnt[eng] += 1
        tok = (self.esem[eng], self.cnt[eng], eng)
        self.streams[eng].append((waits, fn, self.esem[eng], 1))
        self._update(tok, reads, writes, pwrites)
        return tok

    def dma(self, eng, key, fn, reads=(), writes=(), pwrites=(), final=False):
        if not self.enabled:
            return None
        toks = self._deps(reads, writes, pwrites)
        if key not in self.dsem:
            self.dsem[key] = [self.es.enter_context(self.nc.semaphore("dsem_%d" % len(self.dsem))), 0]
        ent = self.dsem[key]
        if ent[1] > 0:
            toks = toks + [(ent[0], 16 * ent[1], 'dma')]
        waits = self._waits(eng, toks)
        ent[1] += 1
        tok = (ent[0], 16 * ent[1], 'dma')
        self.streams[eng].append((waits, fn, ent[0], 16))
        self._update(tok, reads, writes, pwrites)
        if final:
            self.final_tokens.append(tok)
        return tok

    def wtag(self, key, tag):
        self.tags[key] = tag

    def rtag(self, key, tag):
        assert self.tags.get(key) == tag, ("pipeline slot hazard", key, self.tags.get(key), tag)

    def barrier(self):
        toks = [(self.esem[e], self.cnt[e], e + '_b') for e in self.ENGS if self.cnt[e] > 0]
        toks += [(ent[0], 16 * ent[1], 'dma') for ent in self.dsem.values() if ent[1] > 0]
        for e in self.ENGS:
            waits = self._waits(e, [t for t in toks if not t[2].startswith(e + '_')])
            if waits:
                self.streams[e].append((waits, None, None, 0))
        self.bufs = {}

    def emit(self):
        nc = self.nc
        with nc.Block() as block:
            def replay(name):
                def f(eng):
                    for (waits, fn, sem, inc) in self.streams[name]:
                        for (s, v) in waits:
                            eng.wait_ge(s, v)
                        if fn is None:
                            continue
                        ins = fn(eng)
                        ins.then_inc(sem, inc)
                    if name == 'sp':
                        best = {}
                        for (s, v, _) in self.final_tokens:
                            k = id(s)
                            if k not in best or best[k][1] < v:
                                best[k] = (s, v)
                        for (s, v) in best.values():
                            eng.wait_ge(s, v)
                return f
            if self.streams['pe']:
                block.tensor(replay('pe'))
            if self.streams['act']:
                block.scalar(replay('act'))
            if self.streams['dve']:
                block.vector(replay('dve'))
            if self.streams['pool']:
                block.gpsimd(replay('pool'))
            block.sync(replay('sp'))


def run_pipeline(items, stages):
    n = len(items)
    maxlag = max(l for l, _ in stages)
    for t in range(n + maxlag):
        for lag, fn in stages:
            i = t - lag
            if 0 <= i < n:
                fn(items[i])


class Arena:
    def __init__(self, ap, start, limit):
        self.ap = ap
        self.off = start
        self.limit = limit
        self.n = 0

    def take(self, shape, dt):
        esz = {F32: 4, BF16: 2, I32: 4}[dt]
        free = 1
        for s in shape[1:]:
            free *= s
        nbytes = (free * esz + 63) // 64 * 64
        assert self.off + nbytes <= self.limit, ("arena overflow", self.off, nbytes, self.limit)
        v = self.ap[:, self.off:self.off + free * esz].bitcast(dt)
        self.off += nbytes
        if len(shape) == 3:
            v = v.rearrange("p (a b) -> p a b", a=shape[1])
        elif len(shape) == 4:
            v = v.rearrange("p (a b c) -> p a b c", a=shape[1], b=shape[2])
        if shape[0] < 128:
            v = v[0:shape[0]]
        return v


def build_program(S, NSEQ):
    NB = S // 128
    NT = S // 512
    assert S % 512 == 0
    nc = bass.Bass("TRN2", target_bir_lowering=False)

    def din(name, shape, dt):
        return nc.dram_tensor(name, list(shape), dt, kind="ExternalInput").ap()

    x_d = din("x", [NSEQ, S, D], F32)
    pos_d = din("pos", [NSEQ, 128, NB], I32)
    win_d = din("w_in", [D, 2208], F32)
    wuq_d = din("w_uq", [384, 768], F32)
    wukv_d = din("w_ukv", [256, 1024], F32)
    wout_d = din("w_out", [D, D], F32)
    wup_d = din("w_up", [D, 2 * DFF], F32)
    wdn_d = din("w_down", [DFF, D], F32)
    gmix_d = din("g_mix", [128, D], F32)
    gc_d = din("g_c", [128, 640], F32)
    go_d = din("g_o", [128, D], F32)
    gffn_d = din("g_ffn", [128, D], F32)
    gfin_d = din("g_final", [128, D], F32)
    cw_d = din("cw", [128, 2 * NJ * 4], F32)
    cb_d = din("cbf", [128, 7 * 128], BF16)
    invf_d = din("invf", [128, 16], F32)
    out_d = nc.dram_tensor("out", [NSEQ, S, D], F32, kind="ExternalOutput").ap()
    dbg_d = nc.dram_tensor("dbg", [NSEQ, 128, NB, D], BF16, kind="ExternalOutput").ap() if DEBUG else None

    win_v = win_d.rearrange("(kc p) n -> p kc n", p=128)
    wuq_v = wuq_d.rearrange("(kc p) n -> p kc n", p=128)
    wukv_v = wukv_d.rearrange("(kc p) n -> p kc n", p=128)
    wout_v = wout_d.rearrange("(kc p) n -> p kc n", p=128)
    wup_v = wup_d.rearrange("(kc p) n -> p kc n", p=128)
    wdn_v = wdn_d.rearrange("(j p) n -> p j n", p=128)

    with contextlib.ExitStack() as es:
        S_ = Sched(nc, es)
        op, dma = S_.op, S_.dma

        def sbt(name, shape, dt):
            return es.enter_context(nc.sbuf_tensor(name, list(shape), dt))

        ARENA_BYTES = 172 * 1024
        arena = sbt("arena", [128, ARENA_BYTES], U8)
        o_tm = sbt("o_tm", [128, NB, D], BF16)
        cbf = sbt("cbf_sb", [128, 7 * 128], BF16)
        zer = sbt("zer", [128, 512], BF16)
        invf = sbt("invf_sb", [128, 16], F32)
        cw = sbt("cw_sb", [128, 2 * NJ * 4], F32)
        cscale = sbt("cscale", [128, 2], F32)
        stat = sbt("stat", [128, 64], F32)
        ident = cbf[:, 0:128]
        tri = cbf[:, 128:256]
        ones = cbf[:, 256:384]
        mstrict = cbf[:, 384:512]
        mincl = cbf[:, 512:640]
        ntri = cbf[:, 640:768]
        nones = cbf[:, 768:896]

        banks = [es.enter_context(nc.psum_tensor("pb%d" % i, [128, 512], F32)) for i in range(8)]

        def bk(i):
            return ('pb', i)

        dma('sp', 'c0', lambda e: e.dma_start(out=cbf[:], in_=cb_d[:, :]), writes=['cbf'])
        dma('sp', 'c1', lambda e: e.dma_start(out=invf[:], in_=invf_d[:, :]), writes=['invf'])
        dma('sp', 'c2', lambda e: e.dma_start(out=cw[:], in_=cw_d[:, :]), writes=['cw'])
        op('pool', lambda e: e.memset(zer[:], 0.0), writes=['zer'])
        op('pool', lambda e: e.memset(cscale[:, 0:1], 1.0 / 384.0), pwrites=['cscale'])
        op('pool', lambda e: e.memset(cscale[:, 1:2], 1.0 / 256.0), pwrites=['cscale'])

        evac_flip = [0]

        def evac(out_ap, in_ap, reads, writes=(), pwrites=(), scale=None):
            evac_flip[0] ^= 1
            if evac_flip[0]:
                if scale is None:
                    op('act', lambda e: e.activation(out=out_ap, in_=in_ap, func=AF.Copy), reads=reads, writes=writes, pwrites=pwrites)
                else:
                    op('act', lambda e: e.activation(out=out_ap, in_=in_ap, func=AF.Copy, scale=scale), reads=reads, writes=writes, pwrites=pwrites)
            else:
                if scale is None:
                    op('dve', lambda e: e.tensor_copy(out_ap, in_ap), reads=reads, writes=writes, pwrites=pwrites)
                else:
                    op('dve', lambda e: e.tensor_scalar(out=out_ap, in0=in_ap, scalar1=scale, scalar2=None, op0=ALU.mult), reads=reads, writes=writes, pwrites=pwrites)

        stat_ctr = [0]

        def stat_cols(n):
            c = stat_ctr[0]
            if c + n > 64:
                c = 0
            stat_ctr[0] = c + n
            return c

        def rms_rstd(src_list, inv_n, junk, junk_key):
            c = stat_cols(4)
            key = ('stat', c)
            assert len(src_list) == 1
            ap, rk = src_list[0]
            op('act', lambda e: e.activation(out=junk[:, 0:ap.shape[1]], in_=ap, func=AF.Square, accum_out=stat[:, c:c + 1]),
               reads=rk, writes=[junk_key, key])
            op('act', lambda e: e.activation(out=stat[:, c + 2:c + 3], in_=stat[:, c:c + 1], func=AF.Sqrt, scale=inv_n, bias=EPS),
               reads=[key], writes=[key])
            op('dve', lambda e: e.reciprocal(stat[:, c + 3:c + 4], stat[:, c + 2:c + 3]), reads=[key], writes=[key])
            return stat[:, c + 3:c + 4], key

        R1_END = 90 * 1024

        for b in range(NSEQ):
            A1 = Arena(arena, 0, R1_END)
            QT = A1.take([128, 4, S], BF16)
            KTz_flat = A1.take([128, 8 * S], BF16)
            KTz = KTz_flat.rearrange("p (a b) -> p a b", a=8)
            KT = KTz_flat[:, 0:4 * S].rearrange("p (a b) -> p a b", a=4)
            Vs_flat = A1.take([128, NB * 512], BF16)
            Vs = Vs_flat.rearrange("p (a b) -> p a b", a=NB)
            cT = A1.take([128, 5, S], BF16)
            kr_tm = A1.take([128, NB, 32], F32)
            cos_t = A1.take([128, NB, 16], F32)
            sin_t = A1.take([128, NB, 16], F32)
            kro = A1.take([128, NB, 32], BF16)

            A2 = Arena(arena, R1_END, ARENA_BYTES)
            win = A2.take([128, 8, 2208], BF16)
            gmix = A2.take([128, D], F32)
            gcb = A2.take([128, 640], F32)
            xt = [A2.take([128, D], F32) for _ in range(2)]
            xn = [A2.take([128, D], BF16) for _ in range(2)]
            hT = [A2.take([128, 8, 512], BF16) for _ in range(2)]
            junk = A2.take([128, D], BF16)
            cn = [A2.take([128, 640], BF16) for _ in range(2)]
            posi = A2.take([128, NB], I32)
            posf = A2.take([128, NB], F32)
            ang = A2.take([128, NB, 16], F32)
            rt = [A2.take([128, NB, 16], F32) for _ in range(4)]
            ki = A2.take([128, NB, 16], I32)
            sc = [A2.take([128, 8], F32) for _ in range(2)]

            for kc in range(8):
                dma('pool', ('win', kc), lambda e, kc=kc: e.dma_start(out=win[:, kc, :], in_=win_v[:, kc, :]), writes=[('win', kc)])
            dma('sp', 'gmix', lambda e: e.dma_start(out=gmix, in_=gmix_d[:, :]), writes=['gmix'])
            dma('sp', 'gcb', lambda e: e.dma_start(out=gcb, in_=gc_d[:, :]), writes=['gcb'])
            dma('sp', 'posi', lambda e, b=b: e.dma_start(out=posi, in_=pos_d[b, :, :]), writes=['posi'])

            op('pool', lambda e: e.memset(KTz_flat, 0.0), writes=[('KT', c_) for c_ in range(4)])

            op('dve', lambda e: e.tensor_copy(posf, posi), reads=['posi'], writes=['posf'])
            op('dve', lambda e: e.tensor_tensor(out=ang, in0=invf[:].unsqueeze(1).to_broadcast([128, NB, 16]),
                                                in1=posf.unsqueeze(2).to_broadcast([128, NB, 16]), op=ALU.mult),
               reads=['posf', 'invf'], writes=['ang'])
            for which, shift, dst in ((0, 0.0, sin_t), (1, 0.25, cos_t)):
                t0, t1 = rt[0], rt[1]
                op('dve', lambda e, shift=shift: e.tensor_scalar(out=t0, in0=ang, scalar1=1.0 / TWO_PI, scalar2=shift, op0=ALU.mult, op1=ALU.add),
                   reads=['ang'], writes=['rt0'])
                op('dve', lambda e: e.tensor_copy(ki, t0), reads=['rt0'], writes=['ki'])
                op('dve', lambda e: e.tensor_copy(t0, ki), reads=['ki'], writes=['rt0'])
                op('dve', lambda e: e.scalar_tensor_tensor(out=t1, in0=t0, scalar=-CW1, in1=ang, op0=ALU.mult, op1=ALU.add),
                   reads=['rt0', 'ang'], writes=['rt1'])
                op('dve', lambda e: e.scalar_tensor_tensor(out=t1, in0=t0, scalar=-CW2, in1=t1, op0=ALU.mult, op1=ALU.add),
                   reads=['rt0', 'rt1'], writes=['rt1'])
                if which == 1:
                    op('dve', lambda e: e.tensor_scalar(out=t1, in0=t1, scalar1=math.pi / 2, scalar2=None, op0=ALU.add),
                       reads=['rt1'], writes=['rt1'])
                op('dve', lambda e: e.tensor_scalar(out=t1, in0=t1, scalar1=PI_SAFE, scalar2=-PI_SAFE, op0=ALU.min, op1=ALU.max),
                   reads=['rt1'], writes=['rt1'])
                op('act', lambda e, dst=dst: e.activation(out=dst, in_=t1, func=AF.Sin), reads=['rt1'], writes=[('trig', which)])

            pbi = [0]

            def nextbank(lo=0, hi=4):
                i = lo + pbi[0] % (hi - lo)
                pbi[0] += 1
                return i

            def p1_norm(T, blk):
                tb = T * 4 + blk
                s = tb % 2
                dma('sp', ('xt', s), lambda e, b=b: e.dma_start(out=xt[s], in_=x_d[b, tb * 128:(tb + 1) * 128, :]),
                    writes=[('xt', s)])
                rstd, rkey = rms_rstd([(xt[s], [('xt', s)])], 1.0 / D, junk, 'junk')
                op('dve', lambda e: e.scalar_tensor_tensor(out=xn[s], in0=xt[s], scalar=rstd, in1=gmix, op0=ALU.mult, op1=ALU.mult),
                   reads=[('xt', s), rkey, 'gmix'], writes=[('xn', s)])
                S_.wtag(('xn', s), tb)

            def p1_tr(T, blk):
                tb = T * 4 + blk
                s = tb % 2
                hs = T % 2
                pi = 6 + (tb % 2)
                ptr = banks[pi][:].bitcast(BF16)
                S_.rtag(('xn', s), tb)

                def tr8(e):
                    for kc in range(8):
                        ins = e.transpose(ptr[:, kc * 128:(kc + 1) * 128], xn[s][:, kc * 128:(kc + 1) * 128], ident)
                    return ins
                op('pe', tr8, reads=[('xn', s), 'cbf'], writes=[bk(pi)])
                evac(hT[hs][:, :, blk * 128:(blk + 1) * 128], ptr.rearrange("p (a b) -> p a b", a=8),
                     reads=[bk(pi)], pwrites=[('hT', hs)])

            qkb = [0]

            def p1_qk(T, oc):
                hs = T % 2
                hTs, hkey = hT[hs], ('hT', hs)
                bi = 4 + qkb[0] % 2
                qkb[0] += 1

                def mmqk(e):
                    for kc in range(8):
                        ins = e.matmul(banks[bi][:], lhsT=win[:, kc, oc * 128:(oc + 1) * 128], rhs=hTs[:, kc, :],
                                       start=(kc == 0), stop=(kc == 7))
                    return ins
                op('pe', mmqk, reads=[hkey] + [('win', kc) for kc in range(8)], writes=[bk(bi)])
                if oc < 4:
                    evac(QT[:, oc, T * 512:(T + 1) * 512], banks[bi][:], reads=[bk(bi)], pwrites=[('QT', oc)], scale=SB_SCALE)
                else:
                    c_ = oc - 4
                    evac(KTz[0:64, 2 * c_, T * 512:(T + 1) * 512], banks[bi][0:64, :], reads=[bk(bi)], pwrites=[('KT', c_)])
                    evac(KTz[64:128, 2 * c_ + 1, T * 512:(T + 1) * 512], banks[bi][64:128, :], reads=[bk(bi)], pwrites=[('KT', c_)])

            def p1_v(T, blk):
                tb = T * 4 + blk
                hs = T % 2
                hTs, hkey = hT[hs], ('hT', hs)
                bi = 4 + qkb[0] % 2
                qkb[0] += 1

                def mmv(e):
                    for kc in range(8):
                        ins = e.matmul(banks[bi][:], lhsT=hTs[:, kc, blk * 128:(blk + 1) * 128], rhs=win[:, kc, 1024:1536],
                                       start=(kc == 0), stop=(kc == 7))
                    return ins
                op('pe', mmv, reads=[hkey] + [('win', kc) for kc in range(8)], writes=[bk(bi)])
                evac(Vs[:, tb, :], banks[bi][:], reads=[bk(bi)], writes=[('V', tb)])

            def p1_c_mm(T, blk):
                tb = T * 4 + blk
                hs = T % 2
                hTs, hkey = hT[hs], ('hT', hs)
                ba, bb = (0, 1) if blk % 2 == 0 else (2, 3)

                def mmc(e):
                    for kc in range(8):
                        e.matmul(banks[ba][:, 0:416], lhsT=hTs[:, kc, blk * 128:(blk + 1) * 128], rhs=win[:, kc, 1536:1952],
                                 start=(kc == 0), stop=(kc == 7))
                    for kc in range(8):
                        ins = e.matmul(banks[bb][:, 0:256], lhsT=hTs[:, kc, blk * 128:(blk + 1) * 128], rhs=win[:, kc, 1952:2208],
                                       start=(kc == 0), stop=(kc == 7))
                    return ins
                op('pe', mmc, reads=[hkey] + [('win', kc) for kc in range(8)], writes=[bk(ba), bk(bb)])
                cs = tb % 2
                sck = ('sc', cs)
                scs = sc[cs]
                op('act', lambda e: e.activation(out=junk[:, 0:384], in_=banks[ba][:, 0:384], func=AF.Square, accum_out=scs[:, 0:1]),
                   reads=[bk(ba)], writes=['junk', sck])
                op('act', lambda e: e.activation(out=junk[:, 0:256], in_=banks[bb][:, 0:256], func=AF.Square, accum_out=scs[:, 1:2]),
                   reads=[bk(bb)], writes=['junk'], pwrites=[sck])
                op('act', lambda e: e.activation(out=kr_tm[:, tb, :], in_=banks[ba][:, 384:416], func=AF.Copy),
                   reads=[bk(ba)], pwrites=['kr_tm'])
                op('dve', lambda e: e.tensor_tensor(out=scs[:, 2:4], in0=scs[:, 0:2], in1=cscale[:, 0:2], op=ALU.mult),
                   reads=[sck, 'cscale'], writes=[sck])
                op('act', lambda e: e.activation(out=scs[:, 4:6], in_=scs[:, 2:4], func=AF.Sqrt, bias=EPS), reads=[sck], writes=[sck])
                op('dve', lambda e: e.reciprocal(scs[:, 6:8], scs[:, 4:6]), reads=[sck], writes=[sck])
                op('dve', lambda e: e.scalar_tensor_tensor(out=cn[cs][:, 0:384], in0=banks[ba][:, 0:384], scalar=scs[:, 6:7],
                                                           in1=gcb[:, 0:384], op0=ALU.mult, op1=ALU.mult),
                   reads=[bk(ba), sck, 'gcb'], writes=[('cn', cs)])
                op('dve', lambda e: e.scalar_tensor_tensor(out=cn[cs][:, 384:640], in0=banks[bb][:, 0:256], scalar=scs[:, 7:8],
                                                           in1=gcb[:, 384:640], op0=ALU.mult, op1=ALU.mult),
                   reads=[bk(bb), sck, 'gcb'], pwrites=[('cn', cs)])
                S_.wtag(('cn', cs), tb)

            def p1_c_tr(T, blk):
                tb = T * 4 + blk
                cs = tb % 2
                pi = 6 + (tb % 2)
                ptr = banks[pi][:].bitcast(BF16)
                S_.rtag(('cn', cs), tb)

                def tr5(e):
                    for i in range(5):
                        ins = e.transpose(ptr[:, i * 128:(i + 1) * 128], cn[cs][:, i * 128:(i + 1) * 128], ident)
                    return ins
                op('pe', tr5, reads=[('cn', cs), 'cbf'], writes=[bk(pi)])
                evac(cT[:, :, tb * 128:(tb + 1) * 128], ptr[:, 0:640].rearrange("p (a b) -> p a b", a=5),
                     reads=[bk(pi)], writes=[('cT', tb)])

            p1_norm(0, 0)
            for blk in range(4):
                if blk + 1 < 4:
                    p1_norm(0, blk + 1)
                p1_tr(0, blk)
            for T in range(NT):
                nxt = (T + 1 < NT)
                p1_c_mm(T, 0)
                p1_c_mm(T, 1)
                if nxt:
                    p1_norm(T + 1, 0)
                for oc in range(0, 4):
                    p1_qk(T, oc)
                p1_c_tr(T, 0)
                p1_c_tr(T, 1)
                p1_c_mm(T, 2)
                p1_c_mm(T, 3)
                if nxt:
                    p1_tr(T + 1, 0)
                    p1_norm(T + 1, 1)
                for oc in range(4, 8):
                    p1_qk(T, oc)
                p1_c_tr(T, 2)
                p1_c_tr(T, 3)
                if nxt:
                    p1_tr(T + 1, 1)
                    p1_norm(T + 1, 2)
                for blk in range(4):
                    p1_v(T, blk)
                    if nxt and blk == 1:
                        p1_tr(T + 1, 2)
                        p1_norm(T + 1, 3)
                if nxt:
                    p1_tr(T + 1, 3)
            x1v, x2v = kr_tm[:, :, 0:16], kr_tm[:, :, 16:32]
            op('dve', lambda e: e.tensor_tensor(out=rt[0], in0=x1v, in1=cos_t, op=ALU.mult), reads=['kr_tm', ('trig', 1)], writes=['rt0'])
            op('dve', lambda e: e.tensor_tensor(out=rt[1], in0=x2v, in1=sin_t, op=ALU.mult), reads=['kr_tm', ('trig', 0)], writes=['rt1'])
            op('dve', lambda e: e.tensor_tensor(out=kro[:, :, 0:16], in0=rt[0], in1=rt[1], op=ALU.subtract), reads=['rt0', 'rt1'], pwrites=['kro'])
            op('dve', lambda e: e.tensor_tensor(out=rt[2], in0=x2v, in1=cos_t, op=ALU.mult), reads=['kr_tm', ('trig', 1)], writes=['rt2'])
            op('dve', lambda e: e.tensor_tensor(out=rt[3], in0=x1v, in1=sin_t, op=ALU.mult), reads=['kr_tm', ('trig', 0)], writes=['rt3'])
            op('dve', lambda e: e.tensor_tensor(out=kro[:, :, 16:32], in0=rt[2], in1=rt[3], op=ALU.add), reads=['rt2', 'rt3'], pwrites=['kro'])

            S_.barrier()
            if PHASE_LIMIT < 2:
                S_.enabled = False

            A3 = Arena(arena, R1_END, ARENA_BYTES)
            e_sb = [A3.take([128, 512], F32) for _ in range(3)]
            sp_sb = [A3.take([128, 512], BF16) for _ in range(3)]
            a_sb = [A3.take([128, 512], BF16) for _ in range(3)]
            LKSb = [A3.take([128, S], BF16) for _ in range(2)]
            wuq = A3.take([128, 3, 768], BF16)
            wukv = A3.take([128, 2, 1024], BF16)
            qtm = [A3.take([128, 4, 128], BF16) for _ in range(2)]
            ktm = [A3.take([128, 4, 128], BF16) for _ in range(2)]
            rq = [A3.take([128, 4, 16], F32) for _ in range(4)]
            rec = [A3.take([128, 8], F32) for _ in range(4)]
            mpc = [0]
            qr = A3.take([128, 4, 32], F32)

            for kc in range(3):
                dma('pool', ('wuq', kc), lambda e, kc=kc: e.dma_start(out=wuq[:, kc, :], in_=wuq_v[:, kc, :]), writes=[('wuq', kc)])
            for kc in range(2):
                dma('pool', ('wukv', kc), lambda e, kc=kc: e.dma_start(out=wukv[:, kc, :], in_=wukv_v[:, kc, :]), writes=[('wukv', kc)])

            def pieces():
                lst = []
                for kb in range(NB - 1, -1, -1):
                    for c in range(NT):
                        lo = max(512 * c, kb * 128)
                        hi = 512 * c + 512
                        if hi > lo:
                            lst.append((kb, c, lo, hi))
                return lst

            QPB = 8
            n_ob = (NB + QPB - 1) // QPB
            assert n_ob <= 2
            pcs = pieces()
            sb_items = []
            for h in range(8):
                for idx, (kb, c, lo, hi) in enumerate(pcs):
                    sb_items.append(dict(h=h, kb=kb, c=c, lo=lo, hi=hi, p=len(sb_items), first=(idx == 0), last=(idx == len(pcs) - 1)))

            def sb_zmm(e, bank_ap, h, kb, lo, hi, start, stop):
                return e.matmul(bank_ap, lhsT=KTz[:, h, kb * 128:(kb + 1) * 128], rhs=QT[:, h // 2, lo:hi], start=start, stop=stop)

            def sb_s0(it):
                h, kb, c, lo, hi, p = it['h'], it['kb'], it['c'], it['lo'], it['hi'], it['p']
                n = hi - lo
                hp = h % 2
                ch = h // 2
                if it['first']:
                    op('pool', lambda e: e.memset(LKSb[hp], 0.0), writes=[('LKSb', hp, cc) for cc in range(NT)])
                zb = p % 2
                op('pe', lambda e: sb_zmm(e, banks[zb][:, 0:n], h, kb, lo, hi, True, True),
                   reads=[('QT', ch), ('KT', ch)], writes=[bk(zb)])
                S_.wtag(bk(zb), ('z', p))

            def sb_s1(it):
                kb, lo, hi, p = it['kb'], it['lo'], it['hi'], it['p']
                n = hi - lo
                zb, es_ = p % 2, p % 3
                S_.rtag(bk(zb), ('z', p))
                op('act', lambda e: e.activation(out=e_sb[es_][:, 0:n], in_=banks[zb][:, 0:n], func=AF.Exp),
                   reads=[bk(zb)], writes=[('e', es_)])
                op('act', lambda e: e.activation(out=sp_sb[es_][:, 0:n], in_=e_sb[es_][:, 0:n], func=AF.Ln, bias=1.0),
                   reads=[('e', es_)], writes=[('sp', es_)])
                if lo == kb * 128:
                    op('pool', lambda e: e.tensor_tensor(out=sp_sb[es_][:, 0:128], in0=sp_sb[es_][:, 0:128], in1=mstrict, op=ALU.mult),
                       reads=[('sp', es_), 'cbf'], writes=[('sp', es_)])
                S_.wtag(('sp', es_), p)

            def sb_s2(it):
                h, kb, c, lo, hi, p = it['h'], it['kb'], it['c'], it['lo'], it['hi'], it['p']
                n = hi - lo
                hp = h % 2
                ch = h // 2
                cbk, ss_ = 2 + p % 2, p % 3
                diag = (lo == kb * 128)
                has_lks = (n > 128) or (not diag)
                S_.rtag(('sp', ss_), p)

                def mmc2(e):
                    e.matmul(banks[cbk][:, 0:n], lhsT=ntri, rhs=sp_sb[ss_][:, 0:n], start=True, stop=False)
                    if has_lks:
                        e.matmul(banks[cbk][:, 0:n], lhsT=nones, rhs=LKSb[hp][:, lo:hi], start=False, stop=False)
                    return sb_zmm(e, banks[cbk][:, 0:n], h, kb, lo, hi, False, True)
                op('pe', mmc2, reads=[('sp', ss_), 'cbf', ('QT', ch), ('KT', ch)] + ([('LKSb', hp, c)] if has_lks else []), writes=[bk(cbk)])
                S_.wtag(bk(cbk), ('c', p))
                if kb > 0:
                    op('dve', lambda e: e.tensor_tensor(out=LKSb[hp][:, lo:hi], in0=LKSb[hp][:, lo:hi], in1=sp_sb[ss_][:, 0:n], op=ALU.add),
                       reads=[('sp', ss_)], writes=[('LKSb', hp, c)])

            def sb_s3(it):
                kb, lo, hi, p = it['kb'], it['lo'], it['hi'], it['p']
                n = hi - lo
                cbk, as_ = 2 + p % 2, p % 3
                S_.rtag(bk(cbk), ('c', p))
                op('act', lambda e: e.activation(out=a_sb[as_][:, 0:n], in_=banks[cbk][:, 0:n], func=AF.Exp),
                   reads=[bk(cbk)], writes=[('a', as_)])
                if lo == kb * 128:
                    op('pool', lambda e: e.tensor_tensor(out=a_sb[as_][:, 0:128], in0=a_sb[as_][:, 0:128], in1=mstrict, op=ALU.mult),
                       reads=[('a', as_), 'cbf'], writes=[('a', as_)])
                S_.wtag(('a', as_), p)

            def sb_s4(it):
                h, kb, lo, hi, p = it['h'], it['kb'], it['lo'], it['hi'], it['p']
                as_ = p % 3
                ob0 = 4 if h % 2 == 0 else 6
                obanks = [ob0 + i for i in range(n_ob)]
                S_.rtag(('a', as_), p)
                if it['first']:
                    def zero_o(e):
                        for bi in obanks:
                            ins = e.matmul(banks[bi][:], lhsT=zer[:, 0:128], rhs=zer[:, :], start=True, stop=False, skip_group_check=True)
                        return ins
                    op('pe', zero_o, reads=['zer'], writes=[bk(bi) for bi in obanks])

                def mmav(e):
                    for qb in range(lo // 128, hi // 128):
                        bi = ob0 + qb // QPB
                        sl = (qb % QPB) * 64
                        last = (kb == 0) and (qb % QPB == QPB - 1 or qb == NB - 1)
                        ins = e.matmul(banks[bi][:, sl:sl + 64], lhsT=a_sb[as_][:, qb * 128 - lo:qb * 128 - lo + 128],
                                       rhs=Vs[:, kb, h * 64:(h + 1) * 64], start=False, stop=last, skip_group_check=True)
                    return ins
                obs = sorted(set(ob0 + qb // QPB for qb in range(lo // 128, hi // 128)))
                op('pe', mmav, reads=[('a', as_), ('V', kb)], pwrites=[bk(bi) for bi in obs])
                if it['last']:
                    for i, bi in enumerate(obanks):
                        nq = min(QPB, NB - i * QPB)
                        evac(o_tm[:, i * QPB:i * QPB + nq, h * 64:(h + 1) * 64],
                             banks[bi][:, 0:nq * 64].rearrange("p (a b) -> p a b", a=nq),
                             reads=[bk(bi)], pwrites=[('o', i)])

            run_pipeline(sb_items, [(0, sb_s0), (1, sb_s1), (2, sb_s2), (3, sb_s3), (4, sb_s4)])

            S_.barrier()
            if PHASE_LIMIT < 3:
                S_.enabled = False

            if OPLIM is not None:
                S_.oplimit = OPLIM
            QTm, KTm = QT, KT
            Vm = Vs[:, :, 0:260].rearrange("p a (h d) -> p a h d", h=4)
            MQ = 7
            n_mb = (NB + MQ - 1) // MQ
            for g in range(2):
                op('pool', lambda e: e.memset(Vs_flat, 1.0), writes=[('V', tb) for tb in range(NB)])
                if g == 0:
                    for qs_ in range(2):
                        op('pool', lambda e, qs_=qs_: e.memset(qtm[qs_].rearrange("p a b -> p (a b)"), 0.0), writes=[('qtm', qs_)])
                        op('pool', lambda e, qs_=qs_: e.memset(ktm[qs_].rearrange("p a b -> p (a b)"), 0.0), writes=[('ktm', qs_)])
                def mp_banks(tb):
                    return (0, 1) if tb % 2 == 0 else (2, 3)

                def mp_A(tb, g=g):
                    bq, bkv = mp_banks(tb)

                    def mmq(e):
                        for kc in range(3):
                            ins = e.matmul(banks[bq][:, 0:384], lhsT=cT[:, kc, tb * 128:(tb + 1) * 128], rhs=wuq[:, kc, g * 384:(g + 1) * 384],
                                           start=(kc == 0), stop=(kc == 2))
                        return ins
                    op('pe', mmq, reads=[('cT', tb)] + [('wuq', kc) for kc in range(3)], writes=[bk(bq)])

                    def mmkv(e):
                        for kc in range(2):
                            ins = e.matmul(banks[bkv][:, :], lhsT=cT[:, 3 + kc, tb * 128:(tb + 1) * 128], rhs=wukv[:, kc, g * 512:(g + 1) * 512],
                                           start=(kc == 0), stop=(kc == 1))
                        return ins
                    op('pe', mmkv, reads=[('cT', tb)] + [('wukv', kc) for kc in range(2)], writes=[bk(bkv)])
                    S_.wtag(bk(bq), ('mq', g, tb))

                def mp_B(tb, g=g):
                    qs = tb % 2
                    bq, bkv = mp_banks(tb)
                    S_.rtag(bk(bq), ('mq', g, tb))
                    qv = banks[bq][:, 0:384].rearrange("p (h d) -> p h d", h=4)
                    kvv = banks[bkv][:, :].rearrange("p (h d) -> p h d", h=4)
                    cosb = cos_t[:, tb, :].unsqueeze(1).to_broadcast([128, 4, 16])
                    sinb = sin_t[:, tb, :].unsqueeze(1).to_broadcast([128, 4, 16])
                    qk_ = ('qtm', qs)
                    kk_ = ('ktm', qs)
                    op('act', lambda e: e.activation(out=qr, in_=qv[:, :, 64:96], func=AF.Copy), reads=[bk(bq)], writes=['qr'])
                    op('act', lambda e: e.activation(out=qtm[qs][:, :, 0:64], in_=qv[:, :, 0:64], func=AF.Copy),
                       reads=[bk(bq)], pwrites=[qk_])
                    op('act', lambda e: e.activation(out=ktm[qs][:, :, 0:64], in_=kvv[:, :, 0:64], func=AF.Copy),
                       reads=[bk(bkv)], pwrites=[kk_])
                    op('dve', lambda e: e.tensor_tensor(out=rq[0], in0=qr[:, :, 0:16], in1=cosb, op=ALU.mult), reads=['qr', ('trig', 1)], writes=['rq0'])
                    op('dve', lambda e: e.tensor_tensor(out=rq[1], in0=qr[:, :, 16:32], in1=sinb, op=ALU.mult), reads=['qr', ('trig', 0)], writes=['rq1'])
                    op('dve', lambda e: e.tensor_tensor(out=qtm[qs][:, :, 64:80], in0=rq[0], in1=rq[1], op=ALU.subtract), reads=['rq0', 'rq1'], pwrites=[qk_])
                    op('dve', lambda e: e.tensor_tensor(out=rq[2], in0=qr[:, :, 16:32], in1=cosb, op=ALU.mult), reads=['qr', ('trig', 1)], writes=['rq2'])
                    op('dve', lambda e: e.tensor_tensor(out=rq[3], in0=qr[:, :, 0:16], in1=sinb, op=ALU.mult), reads=['qr', ('trig', 0)], writes=['rq3'])
                    op('dve', lambda e: e.tensor_tensor(out=qtm[qs][:, :, 80:96], in0=rq[2], in1=rq[3], op=ALU.add), reads=['rq2', 'rq3'], pwrites=[qk_])
                    for hl_ in range(4):
                        op('pool', lambda e, hl_=hl_: e.tensor_copy(ktm[qs][:, hl_, 64:96], kro[:, tb, :]),
                           reads=['kro'], pwrites=[kk_])
                    op('dve', lambda e: e.tensor_copy(Vm[:, tb, :, 0:64], kvv[:, :, 64:128]), reads=[bk(bkv)], pwrites=[('V', tb)])
                    S_.wtag(qk_, (g, tb))

                def mp_C(tb, g=g):
                    qs = tb % 2
                    qk_ = ('qtm', qs)
                    kk_ = ('ktm', qs)
                    S_.rtag(qk_, (g, tb))
                    pi = 6 + (tb % 2)
                    ptr = banks[pi][:].bitcast(BF16)

                    def trqk(e):
                        for hl in range(4):
                            e.transpose(ptr[:, hl * 128:(hl + 1) * 128], qtm[qs][:, hl, :], ident)
                        for hl in range(4):
                            ins = e.transpose(ptr[:, (4 + hl) * 128:(5 + hl) * 128], ktm[qs][:, hl, :], ident)
                        return ins
                    op('pe', trqk, reads=[qk_, kk_, 'cbf'], writes=[bk(pi)])
                    evac(QTm[:, :, tb * 128:(tb + 1) * 128], ptr[:, 0:512].rearrange("p (a b) -> p a b", a=4),
                         reads=[bk(pi)], pwrites=[('QT', hl) for hl in range(4)])
                    evac(KTm[:, :, tb * 128:(tb + 1) * 128], ptr[:, 512:1024].rearrange("p (a b) -> p a b", a=4),
                         reads=[bk(pi)], pwrites=[('KT', hl) for hl in range(4)])

                run_pipeline(list(range(NB)), [(0, mp_A), (1, mp_B), (2, mp_C)])
                if P3_MODE == 'proj':
                    S_.enabled = False
                m_items = []
                for hl in range(4):
                    for idx, (kb, c, lo, hi) in enumerate(pcs):
                        m_items.append(dict(hl=hl, kb=kb, c=c, lo=lo, hi=hi, idx=len(m_items), first=(idx == 0), last=(idx == len(pcs) - 1)))

                def ml_s0(it, g=g):
                    hl, kb, lo, hi = it['hl'], it['kb'], it['lo'], it['hi']
                    n = hi - lo
                    p = mpc[0] + it['idx']
                    zb = (0, 1, 5, 6)[p % 4]
                    op('pe', lambda e: e.matmul(banks[zb][:, 0:n], lhsT=KTm[:, hl, kb * 128:(kb + 1) * 128], rhs=QTm[:, hl, lo:hi], start=True, stop=True),
                       reads=[('QT', hl), ('KT', hl)], writes=[bk(zb)])
                    S_.wtag(bk(zb), ('zm', p))

                def ml_s1(it, g=g):
                    kb, lo, hi = it['kb'], it['lo'], it['hi']
                    n = hi - lo
                    p = mpc[0] + it['idx']
                    zb, as_ = (0, 1, 5, 6)[p % 4], p % 3
                    S_.rtag(bk(zb), ('zm', p))
                    op('act', lambda e: e.activation(out=a_sb[as_][:, 0:n], in_=banks[zb][:, 0:n], func=AF.Exp, scale=MLA_SCALE),
                       reads=[bk(zb)], writes=[('a', as_)])
                    if lo == kb * 128:
                        op('pool', lambda e: e.tensor_tensor(out=a_sb[as_][:, 0:128], in0=a_sb[as_][:, 0:128], in1=mincl, op=ALU.mult),
                           reads=[('a', as_), 'cbf'], writes=[('a', as_)])
                    S_.wtag(('a', as_), ('m', p))

                def ml_s2(it, g=g):
                    hl, kb, lo, hi = it['hl'], it['kb'], it['lo'], it['hi']
                    h = g * 4 + hl
                    p = mpc[0] + it['idx']
                    as_ = p % 3
                    ob0 = 2
                    obanks = [ob0 + i for i in range(n_mb)]
                    S_.rtag(('a', as_), ('m', p))
                    if it['first']:
                        def zero_m(e):
                            for bi in obanks:
                                ins = e.matmul(banks[bi][:], lhsT=zer[:, 0:128], rhs=zer[:, :], start=True, stop=False, skip_group_check=True)
                            return ins
                        op('pe', zero_m, reads=['zer'], writes=[bk(bi) for bi in obanks])

                    def mmavm(e):
                        for qb in range(lo // 128, hi // 128):
                            bi = ob0 + qb // MQ
                            sl = (qb % MQ) * 65
                            last = (kb == 0) and (qb % MQ == MQ - 1 or qb == NB - 1)
                            ins = e.matmul(banks[bi][:, sl:sl + 65], lhsT=a_sb[as_][:, qb * 128 - lo:qb * 128 - lo + 128],
                                           rhs=Vm[:, kb, hl, :], start=False, stop=last, skip_group_check=True)
                        return ins
                    obs = sorted(set(ob0 + qb // MQ for qb in range(lo // 128, hi // 128)))
                    op('pe', mmavm, reads=[('a', as_), ('V', kb)], pwrites=[bk(bi) for bi in obs])
                    if it['last']:
                        for i, bi in enumerate(obanks):
                            nq = min(MQ, NB - i * MQ)
                            ov = banks[bi][:, 0:nq * 65].rearrange("p (a b) -> p a b", a=nq)
                            rs = rec[(h * 4 + i) % 4]
                            rk = ('rec', (h * 4 + i) % 4)
                            op('dve', lambda e, ov=ov, nq=nq, rs=rs: e.reciprocal(rs[:, 0:nq], ov[:, :, 64]), reads=[bk(bi)], writes=[rk])
                            qbs = range(i * MQ, i * MQ + nq)
                            okeys = sorted(set(('o', qb // QPB) for qb in qbs))
                            op('dve', lambda e, ov=ov, nq=nq, i=i, h=h, rs=rs: e.tensor_tensor(
                                out=o_tm[:, i * MQ:i * MQ + nq, 512 + h * 64:512 + (h + 1) * 64], in0=ov[:, :, 0:64],
                                in1=rs[:, 0:nq].unsqueeze(2).to_broadcast([128, nq, 64]), op=ALU.mult),
                               reads=[bk(bi), rk], pwrites=okeys)

                run_pipeline(m_items, [(0, ml_s0), (2, ml_s1), (3, ml_s2)])
                mpc[0] += len(m_items)

            S_.barrier()
            if PHASE_LIMIT < 4:
                S_.enabled = False

            A4 = Arena(arena, 0, ARENA_BYTES)
            if DEBUG:
                dma('sp', 'dbg', lambda e, b=b: e.dma_start(out=dbg_d[b], in_=o_tm[:]), reads=[('o', i) for i in range(n_ob)], final=True)
            wout = A4.take([128, 8, D], BF16)
            wdn = A4.take([128, NJ, D], BF16)
            gob = A4.take([128, D], F32)
            gffn = A4.take([128, D], F32)
            gfin = A4.take([128, D], F32)
            xn4a = [A4.take([128, D], BF16) for _ in range(2)]
            xn4b = [A4.take([128, D], BF16) for _ in range(2)]
            junk4 = A4.take([128, D], BF16)
            oT = [A4.take([128, 8, 128], BF16) for _ in range(2)]
            h2T = A4.take([128, 8, 512], BF16)
            x1b = [A4.take([128, 4, D], F32) for _ in range(2)]
            gT = A4.take([128, NJ, 512], BF16)
            wup = [A4.take([128, 8, 256], BF16) for _ in range(3)]
            cg = [A4.take([128, 512], F32) for _ in range(2)]
            cv = [A4.take([128, 512], F32) for _ in range(2)]
            sg = [A4.take([128, 512], BF16) for _ in range(2)]
            halo = [A4.take([128, 2 * NJ, 2], F32) for _ in range(2)]
            so = [A4.take([128, 8], F32) for _ in range(4)]

            for kc in range(8):
                dma('pool', ('wout', kc), lambda e, kc=kc: e.dma_start(out=wout[:, kc, :], in_=wout_v[:, kc, :]), writes=[('wout', kc)])
            for j0 in range(0, NJ, 2):
                dma('pool', ('wdn', j0), lambda e, j0=j0: e.dma_start(out=wdn[:, j0:j0 + 2, :], in_=wdn_v[:, j0:j0 + 2, :]), writes=[('wdn', j0), ('wdn', j0 + 1)])
            dma('sp', 'gob', lambda e: e.dma_start(out=gob, in_=go_d[:, :]), writes=['gob'])
            dma('sp', 'gffn', lambda e: e.dma_start(out=gffn, in_=gffn_d[:, :]), writes=['gffn'])
            dma('sp', 'gfin', lambda e: e.dma_start(out=gfin, in_=gfin_d[:, :]), writes=['gfin'])
            wu_ptr = [0]

            def ffn_dma_upto(g_hi):
                while wu_ptr[0] <= g_hi and wu_ptr[0] < NT * NJ:
                    g_ = wu_ptr[0]
                    wu_ptr[0] += 1
                    ws, j = g_ % 3, g_ % NJ
                    dma('pool', ('wup', ws), lambda e, ws=ws, j=j: e.dma_start(out=wup[ws], in_=wup_v[:, :, j * 256:(j + 1) * 256]), writes=[('wup', ws)])
                    S_.wtag(('wup', ws), g_)

            def ffn_mmu(T, j):
                g_ = T * NJ + j
                ws = g_ % 3
                bg, bv = ((0, 1), (2, 3), (4, 5))[j % 3]
                S_.rtag(('wup', ws), g_)

                def mmu(e):
                    for kc in range(8):
                        e.matmul(banks[bg][:], lhsT=wup[ws][:, kc, 0:128], rhs=h2T[:, kc, :], start=(kc == 0), stop=(kc == 7))
                    for kc in range(8):
                        ins = e.matmul(banks[bv][:], lhsT=wup[ws][:, kc, 128:256], rhs=h2T[:, kc, :], start=(kc == 0), stop=(kc == 7))
                    return ins
                op('pe', mmu, reads=[('wup', ws), 'h2T'], writes=[bk(bg), bk(bv)])
                S_.wtag(bk(bg), ('u', g_))

            def ffn_conv(T, j):
                g_ = T * NJ + j
                bg, bv = ((0, 1), (2, 3), (4, 5))[j % 3]
                S_.rtag(bk(bg), ('u', g_))
                us = j % 2
                hw, hr = halo[T % 2], halo[(T + 1) % 2]
                halves = ((0, bg, cg[us], ('cg', us)), (1, bv, cv[us], ('cv', us)))
                for (half, bi, dst, dkey) in halves:
                    cj = j * 2 + half
                    w2 = cw[:, cj * 4 + 2:cj * 4 + 3]
                    bb_ = cw[:, cj * 4 + 3:cj * 4 + 4]
                    op('act', lambda e, bi=bi, dst=dst, w2=w2, bb_=bb_: e.activation(out=dst, in_=banks[bi][:], func=AF.Identity, scale=w2, bias=bb_),
                       reads=[bk(bi), 'cw'], writes=[dkey])
                    if T < NT - 1:
                        op('act', lambda e, bi=bi, cj=cj: e.activation(out=hw[:, cj, :], in_=banks[bi][:, 510:512], func=AF.Copy),
                           reads=[bk(bi)], writes=[('halo', T % 2, cj)])
                for (half, bi, dst, dkey) in halves:
                    cj = j * 2 + half
                    w0 = cw[:, cj * 4 + 0:cj * 4 + 1]
                    w1 = cw[:, cj * 4 + 1:cj * 4 + 2]
                    hk = ('halo', (T + 1) % 2, cj)
                    op('dve', lambda e, bi=bi, dst=dst, w1=w1: e.scalar_tensor_tensor(out=dst[:, 1:512], in0=banks[bi][:, 0:511], scalar=w1, in1=dst[:, 1:512],
                                                                                   op0=ALU.mult, op1=ALU.add),
                       reads=[bk(bi), dkey, 'cw'], writes=[dkey])
                    op('dve', lambda e, bi=bi, dst=dst, w0=w0: e.scalar_tensor_tensor(out=dst[:, 2:512], in0=banks[bi][:, 0:510], scalar=w0, in1=dst[:, 2:512],
                                                                                   op0=ALU.mult, op1=ALU.add),
                       reads=[bk(bi), dkey, 'cw'], writes=[dkey])
                    if T > 0:
                        op('dve', lambda e, dst=dst, w1=w1, cj=cj: e.scalar_tensor_tensor(out=dst[:, 0:1], in0=hr[:, cj, 1:2], scalar=w1, in1=dst[:, 0:1],
                                                                                       op0=ALU.mult, op1=ALU.add),
                           reads=[hk, dkey, 'cw'], writes=[dkey])
                        op('dve', lambda e, dst=dst, w0=w0, cj=cj: e.scalar_tensor_tensor(out=dst[:, 0:2], in0=hr[:, cj, 0:2], scalar=w0, in1=dst[:, 0:2],
                                                                                       op0=ALU.mult, op1=ALU.add),
                           reads=[hk, dkey, 'cw'], writes=[dkey])
                    if half == 0:
                        op('act', lambda e: e.activation(out=sg[us], in_=cg[us], func=AF.Silu), reads=[('cg', us)], writes=[('sg', us)])
                op('pool', lambda e: e.tensor_tensor(out=gT[:, j, :], in0=sg[us], in1=cv[us], op=ALU.mult),
                   reads=[('sg', us), ('cv', us)], writes=[('gT', j)])

            def pro_A(T, i):
                tb = T * 4 + i
                par, s = T % 2, i % 2
                x1 = x1b[par]
                okey = ('o', tb // QPB)
                dma('sp', ('xin', par, i), lambda e, b=b: e.dma_start(out=x1[:, i, :], in_=x_d[b, tb * 128:(tb + 1) * 128, :]),
                    writes=[('x1', par, i)])
                sos = so[i]
                sok = ('so', i)
                op('act', lambda e: e.activation(out=junk4[:, 0:512], in_=o_tm[:, tb, 0:512], func=AF.Square, accum_out=sos[:, 0:1]),
                   reads=[okey], writes=['junk4', sok])
                op('act', lambda e: e.activation(out=junk4[:, 512:1024], in_=o_tm[:, tb, 512:1024], func=AF.Square, accum_out=sos[:, 1:2]),
                   reads=[okey], pwrites=['junk4', sok])
                op('act', lambda e: e.activation(out=sos[:, 2:4], in_=sos[:, 0:2], func=AF.Sqrt, scale=1.0 / 512.0, bias=EPS), reads=[sok], writes=[sok])
                op('dve', lambda e: e.reciprocal(sos[:, 4:6], sos[:, 2:4]), reads=[sok], writes=[sok])
                for half in range(2):
                    op('dve', lambda e, half=half: e.scalar_tensor_tensor(
                        out=xn4a[s][:, half * 512:(half + 1) * 512], in0=o_tm[:, tb, half * 512:(half + 1) * 512], scalar=sos[:, 4 + half:5 + half],
                        in1=gob[:, half * 512:(half + 1) * 512], op0=ALU.mult, op1=ALU.mult),
                       reads=[okey, sok, 'gob'], **({'writes': [('xn4a', s)]} if half == 0 else {'pwrites': [('xn4a', s)]}))
                S_.wtag(('xn4a', s), tb)

            def pro_B(T, i):
                tb = T * 4 + i
                s = i % 2
                pi = 6 + s
                ptr = banks[pi][:].bitcast(BF16)
                S_.rtag(('xn4a', s), tb)

                def tro(e):
                    for kc in range(8):
                        ins = e.transpose(ptr[:, kc * 128:(kc + 1) * 128], xn4a[s][:, kc * 128:(kc + 1) * 128], ident)
                    return ins
                op('pe', tro, reads=[('xn4a', s), 'cbf'], writes=[bk(pi)])
                evac(oT[s], ptr.rearrange("p (a b) -> p a b", a=8), reads=[bk(pi)], writes=[('oT', s)])
                S_.wtag(('oT', s), tb)

            def pro_C(T, i):
                tb = T * 4 + i
                par, s = T % 2, i % 2
                x1 = x1b[par]
                S_.rtag(('oT', s), tb)
                for half in range(2):
                    bi = 4 + half

                    def mmo(e, half=half, bi=bi):
                        for kc in range(8):
                            ins = e.matmul(banks[bi][:], lhsT=oT[s][:, kc, :], rhs=wout[:, kc, half * 512:(half + 1) * 512],
                                           start=(kc == 0), stop=(kc == 7))
                        return ins
                    op('pe', mmo, reads=[('oT', s)] + [('wout', kc) for kc in range(8)], writes=[bk(bi)])
                    op('dve', lambda e, bi=bi, half=half: e.tensor_tensor(
                        out=x1[:, i, half * 512:(half + 1) * 512], in0=banks[bi][:], in1=x1[:, i, half * 512:(half + 1) * 512], op=ALU.add),
                       reads=[bk(bi), ('x1', par, i)], pwrites=[('x1', par, i)])

            def pro_D(T, i):
                tb = T * 4 + i
                par, s = T % 2, i % 2
                x1 = x1b[par]
                rstd, rkey = rms_rstd([(x1[:, i, :], [('x1', par, i)])], 1.0 / D, junk4, 'junk4')
                op('dve', lambda e: e.scalar_tensor_tensor(out=xn4b[s], in0=x1[:, i, :], scalar=rstd, in1=gffn, op0=ALU.mult, op1=ALU.mult),
                   reads=[('x1', par, i), rkey, 'gffn'], writes=[('xn4b', s)])
                S_.wtag(('xn4b', s), tb)

            def pro_E(T, i):
                tb = T * 4 + i
                s = i % 2
                pi = 6 + s
                ptr = banks[pi][:].bitcast(BF16)
                S_.rtag(('xn4b', s), tb)

                def tro(e):
                    for kc in range(8):
                        ins = e.transpose(ptr[:, kc * 128:(kc + 1) * 128], xn4b[s][:, kc * 128:(kc + 1) * 128], ident)
                    return ins
                op('pe', tro, reads=[('xn4b', s), 'cbf'], writes=[bk(pi)])
                evac(h2T[:, :, i * 128:(i + 1) * 128], ptr.rearrange("p (a b) -> p a b", a=8), reads=[bk(pi)], pwrites=['h2T'])

            def epi_A(T, i):
                par = T % 2
                x1 = x1b[par]
                for half in range(2):
                    bi = (i % 2) * 2 + half

                    def mmd(e, half=half, bi=bi):
                        for j in range(NJ):
                            ins = e.matmul(banks[bi][:], lhsT=gT[:, j, i * 128:(i + 1) * 128], rhs=wdn[:, j, half * 512:(half + 1) * 512],
                                           start=(j == 0), stop=(j == NJ - 1))
                        return ins
                    op('pe', mmd, reads=[('gT', j) for j in range(NJ)] + [('wdn', j) for j in range(NJ)], writes=[bk(bi)])
                    op('dve', lambda e, bi=bi, half=half: e.tensor_tensor(
                        out=x1[:, i, half * 512:(half + 1) * 512], in0=banks[bi][:], in1=x1[:, i, half * 512:(half + 1) * 512], op=ALU.add),
                       reads=[bk(bi), ('x1', par, i)], pwrites=[('x1', par, i)])

            def epi_B(T, i):
                tb = T * 4 + i
                par = T % 2
                x1 = x1b[par]
                rstd, rkey = rms_rstd([(x1[:, i, :], [('x1', par, i)])], 1.0 / D, junk4, 'junk4')
                op('dve', lambda e: e.scalar_tensor_tensor(out=x1[:, i, :], in0=x1[:, i, :], scalar=rstd, in1=gfin, op0=ALU.mult, op1=ALU.mult),
                   reads=[rkey, 'gfin'], writes=[('x1', par, i)])
                dma('sp', ('yout', par, i), lambda e, b=b: e.dma_start(out=out_d[b, tb * 128:(tb + 1) * 128, :], in_=x1[:, i, :]),
                    reads=[('x1', par, i)], final=True)

            def pro_epi(Tp, Te):
                stages = []
                if Tp is not None:
                    stages.append((0, lambda i: pro_A(Tp, i)))
                if Te is not None:
                    stages.append((0, lambda i: epi_A(Te, i)))
                    stages.append((1, lambda i: epi_B(Te, i)))
                if Tp is not None:
                    stages.append((1, lambda i: pro_B(Tp, i)))
                    stages.append((2, lambda i: pro_C(Tp, i)))
                    stages.append((2, lambda i: pro_D(Tp, i)))
                    stages.append((3, lambda i: pro_E(Tp, i)))
                stages.sort(key=lambda t: t[0])
                run_pipeline(list(range(4)), stages)

            ffn_dma_upto(1)
            pro_epi(0, None)
            for T in range(NT):
                for j in range(NJ):
                    ffn_dma_upto(T * NJ + j + 2)
                    ffn_mmu(T, j)
                    if j >= 2:
                        ffn_conv(T, j - 2)
                ffn_conv(T, NJ - 2)
                ffn_conv(T, NJ - 1)
                pro_epi(T + 1 if T + 1 < NT else None, T)

            S_.barrier()

        S_.emit()
    return nc


def _prep_shared(inputs):
    f32 = np.float32
    w_in = np.asarray(inputs["w_in"][0], f32)
    perm = np.concatenate([np.arange(0, 1536), np.arange(1536, 1920), np.arange(2176, 2208), np.arange(1920, 2176)])
    w_in_p = np.ascontiguousarray(w_in[:, perm])
    w_up = np.asarray(inputs["w_up"][0], f32)
    cols = np.concatenate([np.concatenate([np.arange(j * 128, (j + 1) * 128), DFF + np.arange(j * 128, (j + 1) * 128)]) for j in range(NJ)])
    w_up_p = np.ascontiguousarray(w_up[:, cols])
    conv_w = np.asarray(inputs["conv_w"][0], f32)
    conv_b = np.asarray(inputs["conv_b"][0], f32)
    cw = np.zeros((128, 2 * NJ, 4), f32)
    for j in range(NJ):
        for half in range(2):
            ch = half * DFF + j * 128 + np.arange(128)
            cw[:, j * 2 + half, 0] = conv_w[0, ch]
            cw[:, j * 2 + half, 1] = conv_w[1, ch]
            cw[:, j * 2 + half, 2] = conv_w[2, ch]
            cw[:, j * 2 + half, 3] = conv_b[ch]

    def bc(v):
        return np.ascontiguousarray(np.broadcast_to(np.asarray(v, f32)[None, :], (128, v.shape[0])))
    g_c = np.concatenate([np.asarray(inputs["g_cq"][0], f32), np.asarray(inputs["g_ckv"][0], f32)])
    g_o = np.concatenate([np.asarray(inputs["g_sb_out"][0], f32), np.asarray(inputs["g_mla_out"][0], f32)])
    ii = np.arange(128)
    ident = np.eye(128, dtype=f32)
    tri = (ii[:, None] >= ii[None, :]).astype(f32)
    ones = np.ones((128, 128), f32)
    mstrict = (ii[:, None] < ii[None, :]).astype(f32)
    mincl = (ii[:, None] <= ii[None, :]).astype(f32)
    cbf = np.concatenate([ident, tri, ones, mstrict, mincl, -tri, -ones], axis=1).astype(ml_dtypes.bfloat16)
    half = 16
    inv_freq = (1.0 / (np.float32(10000.0) ** (np.arange(half, dtype=f32) * np.float32(2.0 / 32)))).astype(f32)
    return {
        "w_in": w_in_p,
        "w_uq": np.ascontiguousarray(np.asarray(inputs["w_uq"][0], f32)),
        "w_ukv": np.ascontiguousarray(np.asarray(inputs["w_ukv"][0], f32)),
        "w_out": np.ascontiguousarray(np.asarray(inputs["w_out"][0], f32)),
        "w_up": w_up_p,
        "w_down": np.ascontiguousarray(np.asarray(inputs["w_down"][0], f32)),
        "g_mix": bc(inputs["g_mix"][0]),
        "g_c": bc(g_c),
        "g_o": bc(g_o),
        "g_ffn": bc(inputs["g_ffn"][0]),
        "g_final": bc(inputs["g_final"]),
        "cw": np.ascontiguousarray(cw.reshape(128, 2 * NJ * 4)),
        "cbf": cbf,
        "invf": bc(inv_freq),
    }


_PROG_CACHE = {}


def kernel(**inputs):
    x = np.asarray(inputs["x"], np.float32)
    pos = np.asarray(inputs["positions"], np.int32)
    B, S, _ = x.shape
    ncores = NCORES if B % NCORES == 0 else B
    nseq = B // ncores
    shared = _prep_shared(inputs)
    key = (S, nseq)
    if key not in _PROG_CACHE:
        _PROG_CACHE[key] = build_program(S, nseq)
    nc = _PROG_CACHE[key]
    NB = S // 128
    in_maps = []
    for c in range(ncores):
        m = dict(shared)
        m["x"] = np.ascontiguousarray(x[c * nseq:(c + 1) * nseq])
        p = pos[c * nseq:(c + 1) * nseq].reshape(nseq, NB, 128).transpose(0, 2, 1)
        m["pos"] = np.ascontiguousarray(p)
        in_maps.append(m)
    res = run_bass_kernel_spmd(nc, in_maps, core_ids=list(range(ncores)))
    if DEBUG:
        global LAST_RES
        LAST_RES = res
    out = np.concatenate([np.asarray(r["out"], np.float32) for r in res.results], axis=0)
    return out
```

```python
import contextlib
import math
import numpy as np
import ml_dtypes
import concourse.bass as bass
import concourse.mybir as mybir
from concourse.bass_utils import run_bass_kernel_spmd

F32 = mybir.dt.float32
BF16 = mybir.dt.bfloat16
I32 = mybir.dt.int32
U8 = mybir.dt.uint8
AF = mybir.ActivationFunctionType
ALU = mybir.AluOpType

D = 1024
NCORES = 8
EPS = 1e-6
DFF = 2816
NJ = DFF // 128
SB_SCALE = 64 ** -0.5
MLA_SCALE = 96 ** -0.5
TWO_PI = 2.0 * math.pi
CW1 = 6.28125
CW2 = TWO_PI - CW1
PI_SAFE = 3.1415925
PHASE_LIMIT = 4
P3_MODE = 'all'
OPLIM = None
DEBUG = False
LAST_RES = None


class Sched:
    ENGS = ('pe', 'act', 'dve', 'pool', 'sp')

    def __init__(self, nc, es):
        self.nc = nc
        self.es = es
        self.streams = {e: [] for e in self.ENGS}
        self.cnt = {e: 0 for e in self.ENGS}
        self.esem = {e: es.enter_context(nc.semaphore("sem_" + e)) for e in self.ENGS}
        self.waited = {e: {} for e in self.ENGS}
        self.bufs = {}
        self.dsem = {}
        self.semobj = {}
        self.final_tokens = []
        self.tags = {}

    def _st(self, b):
        return self.bufs.setdefault(b, {'w': [], 'r': []})

    def _deps(self, reads, writes, pwrites):
        toks = []
        for b in reads:
            st = self._st(b)
            toks += st['w']
            if isinstance(b, tuple) and b[0] == 'pb':
                toks += st['r']
        for b in writes:
            st = self._st(b)
            toks += st['w'] + st['r']
        for b in pwrites:
            st = self._st(b)
            toks += st['w'] + st['r']
        return toks

    def _update(self, tok, reads, writes, pwrites):
        for b in reads:
            self.bufs[b]['r'].append(tok)
        for b in writes:
            self.bufs[b]['w'] = [tok]
            self.bufs[b]['r'] = []
        for b in pwrites:
            st = self.bufs[b]
            st['w'].append(tok)
            if len(st['w']) > 64:
                best = {}
                for t in st['w']:
                    k = id(t[0])
                    if k not in best or best[k][1] < t[1]:
                        best[k] = t
                st['w'] = list(best.values())
            st['r'] = []

    def _waits(self, eng, toks):
        need = {}
        for (sem, val, src) in toks:
            if src == eng and eng == 'pe':
                continue
            k = id(sem)
            self.semobj[k] = sem
            if self.waited[eng].get(k, 0) >= val:
                continue
            if need.get(k, 0) < val:
                need[k] = val
        out = []
        for k, v in need.items():
            self.waited[eng][k] = v
            out.append((self.semobj[k], v))
        return out

    enabled = True

    oplimit = None

    def op(self, eng, fn, reads=(), writes=(), pwrites=()):
        if not self.enabled:
            return None
        if self.oplimit is not None:
            if self.oplimit <= 0:
                return None
            self.oplimit -= 1
        toks = self._deps(reads, writes, pwrites)
        waits = self._waits(eng, toks)
        self.cnt[eng] += 1
        tok = (self.esem[eng], self.cnt[eng], eng)
        self.streams[eng].append((waits, fn, self.esem[eng], 1))
        self._update(tok, reads, writes, pwrites)
        return tok

    def dma(self, eng, key, fn, reads=(), writes=(), pwrites=(), final=False):
        if not self.enabled:
            return None
        toks = self._deps(reads, writes, pwrites)
        if key not in self.dsem:
            self.dsem[key] = [self.es.enter_context(self.nc.semaphore("dsem_%d" % len(self.dsem))), 0]
        ent = self.dsem[key]
        if ent[1] > 0:
            toks = toks + [(ent[0], 16 * ent[1], 'dma')]
        waits = self._waits(eng, toks)
        ent[1] += 1
        tok = (ent[0], 16 * ent[1], 'dma')
        self.streams[eng].append((waits, fn, ent[0], 16))
        self._update(tok, reads, writes, pwrites)
        if final:
            self.final_tokens.append(tok)
        return tok

    def wtag(self, key, tag):
        self.tags[key] = tag

    def rtag(self, key, tag):
        assert self.tags.get(key) == tag, ("pipeline slot hazard", key, self.tags.get(key), tag)

    def barrier(self):
        toks = [(self.esem[e], self.cnt[e], e + '_b') for e in self.ENGS if self.cnt[e] > 0]
        toks += [(ent[0], 16 * ent[1], 'dma') for ent in self.dsem.values() if ent[1] > 0]
        for e in self.ENGS:
            waits = self._waits(e, [t for t in toks if not t[2].startswith(e + '_')])
            if waits:
                self.streams[e].append((waits, None, None, 0))
        self.bufs = {}

    def emit(self):
        nc = self.nc
        with nc.Block() as block:
            def replay(name):
                def f(eng):
                    for (waits, fn, sem, inc) in self.streams[name]:
                        for (s, v) in waits:
                            eng.wait_ge(s, v)
                        if fn is None:
                            continue
                        ins = fn(eng)
                        ins.then_inc(sem, inc)
                    if name == 'sp':
                        best = {}
                        for (s, v, _) in self.final_tokens:
                            k = id(s)
                            if k not in best or best[k][1] < v:
                                best[k] = (s, v)
                        for (s, v) in best.values():
                            eng.wait_ge(s, v)
                return f
            if self.streams['pe']:
                block.tensor(replay('pe'))
            if self.streams['act']:
                block.scalar(replay('act'))
            if self.streams['dve']:
                block.vector(replay('dve'))
            if self.streams['pool']:
                block.gpsimd(replay('pool'))
            block.sync(replay('sp'))


def run_pipeline(items, stages):
    n = len(items)
    maxlag = max(l for l, _ in stages)
    for t in range(n + maxlag):
        for lag, fn in stages:
            i = t - lag
            if 0 <= i < n:
                fn(items[i])


class Arena:
    def __init__(self, ap, start, limit):
        self.ap = ap
        self.off = start
        self.limit = limit
        self.n = 0

    def take(self, shape, dt):
        esz = {F32: 4, BF16: 2, I32: 4}[dt]
        free = 1
        for s in shape[1:]:
            free *= s
        nbytes = (free * esz + 63) // 64 * 64
        assert self.off + nbytes <= self.limit, ("arena overflow", self.off, nbytes, self.limit)
        v = self.ap[:, self.off:self.off + free * esz].bitcast(dt)
        self.off += nbytes
        if len(shape) == 3:
            v = v.rearrange("p (a b) -> p a b", a=shape[1])
        elif len(shape) == 4:
            v = v.rearrange("p (a b c) -> p a b c", a=shape[1], b=shape[2])
        if shape[0] < 128:
            v = v[0:shape[0]]
        return v


def build_program(S, NSEQ):
    NB = S // 128
    NT = S // 512
    assert S % 512 == 0
    nc = bass.Bass("TRN2", target_bir_lowering=False)

    def din(name, shape, dt):
        return nc.dram_tensor(name, list(shape), dt, kind="ExternalInput").ap()

    x_d = din("x", [NSEQ, S, D], F32)
    pos_d = din("pos", [NSEQ, 128, NB], I32)
    win_d = din("w_in", [D, 2208], F32)
    wuq_d = din("w_uq", [384, 768], F32)
    wukv_d = din("w_ukv", [256, 1024], F32)
    wout_d = din("w_out", [D, D], F32)
    wup_d = din("w_up", [D, 2 * DFF], F32)
    wdn_d = din("w_down", [DFF, D], F32)
    gmix_d = din("g_mix", [128, D], F32)
    gc_d = din("g_c", [128, 640], F32)
    go_d = din("g_o", [128, D], F32)
    gffn_d = din("g_ffn", [128, D], F32)
    gfin_d = din("g_final", [128, D], F32)
    cw_d = din("cw", [128, 2 * NJ * 4], F32)
    cb_d = din("cbf", [128, 7 * 128], BF16)
    invf_d = din("invf", [128, 16], F32)
    out_d = nc.dram_tensor("out", [NSEQ, S, D], F32, kind="ExternalOutput").ap()
    dbg_d = nc.dram_tensor("dbg", [NSEQ, 128, NB, D], BF16, kind="ExternalOutput").ap() if DEBUG else None

    win_v = win_d.rearrange("(kc p) n -> p kc n", p=128)
    wuq_v = wuq_d.rearrange("(kc p) n -> p kc n", p=128)
    wukv_v = wukv_d.rearrange("(kc p) n -> p kc n", p=128)
    wout_v = wout_d.rearrange("(kc p) n -> p kc n", p=128)
    wup_v = wup_d.rearrange("(kc p) n -> p kc n", p=128)
    wdn_v = wdn_d.rearrange("(j p) n -> p j n", p=128)

    with contextlib.ExitStack() as es:
        S_ = Sched(nc, es)
        op, dma = S_.op, S_.dma

        def sbt(name, shape, dt):
            return es.enter_context(nc.sbuf_tensor(name, list(shape), dt))

        ARENA_BYTES = 172 * 1024
        arena = sbt("arena", [128, ARENA_BYTES], U8)
        o_tm = sbt("o_tm", [128, NB, D], BF16)
        cbf = sbt("cbf_sb", [128, 7 * 128], BF16)
        zer = sbt("zer", [128, 512], BF16)
        invf = sbt("invf_sb", [128, 16], F32)
        cw = sbt("cw_sb", [128, 2 * NJ * 4], F32)
        cscale = sbt("cscale", [128, 2], F32)
        stat = sbt("stat", [128, 64], F32)
        ident = cbf[:, 0:128]
        tri = cbf[:, 128:256]
        ones = cbf[:, 256:384]
        mstrict = cbf[:, 384:512]
        mincl = cbf[:, 512:640]
        ntri = cbf[:, 640:768]
        nones = cbf[:, 768:896]

        banks = [es.enter_context(nc.psum_tensor("pb%d" % i, [128, 512], F32)) for i in range(8)]

        def bk(i):
            return ('pb', i)

        dma('sp', 'c0', lambda e: e.dma_start(out=cbf[:], in_=cb_d[:, :]), writes=['cbf'])
        dma('sp', 'c1', lambda e: e.dma_start(out=invf[:], in_=invf_d[:, :]), writes=['invf'])
        dma('sp', 'c2', lambda e: e.dma_start(out=cw[:], in_=cw_d[:, :]), writes=['cw'])
        op('pool', lambda e: e.memset(zer[:], 0.0), writes=['zer'])
        op('pool', lambda e: e.memset(cscale[:, 0:1], 1.0 / 384.0), pwrites=['cscale'])
        op('pool', lambda e: e.memset(cscale[:, 1:2], 1.0 / 256.0), pwrites=['cscale'])

        evac_flip = [0]

        def evac(out_ap, in_ap, reads, writes=(), pwrites=(), scale=None):
            evac_flip[0] ^= 1
            if evac_flip[0]:
                if scale is None:
                    op('act', lambda e: e.activation(out=out_ap, in_=in_ap, func=AF.Copy), reads=reads, writes=writes, pwrites=pwrites)
                else:
                    op('act', lambda e: e.activation(out=out_ap, in_=in_ap, func=AF.Copy, scale=scale), reads=reads, writes=writes, pwrites=pwrites)
            else:
                if scale is None:
                    op('dve', lambda e: e.tensor_copy(out_ap, in_ap), reads=reads, writes=writes, pwrites=pwrites)
                else:
                    op('dve', lambda e: e.tensor_scalar(out=out_ap, in0=in_ap, scalar1=scale, scalar2=None, op0=ALU.mult), reads=reads, writes=writes, pwrites=pwrites)

        stat_ctr = [0]

        def stat_cols(n):
            c = stat_ctr[0]
            if c + n > 64:
                c = 0
            stat_ctr[0] = c + n
            return c

        def rms_rstd(src_list, inv_n, junk, junk_key):
            c = stat_cols(4)
            key = ('stat', c)
            assert len(src_list) == 1
            ap, rk = src_list[0]
            op('act', lambda e: e.activation(out=junk[:, 0:ap.shape[1]], in_=ap, func=AF.Square, accum_out=stat[:, c:c + 1]),
               reads=rk, writes=[junk_key, key])
            op('act', lambda e: e.activation(out=stat[:, c + 2:c + 3], in_=stat[:, c:c + 1], func=AF.Sqrt, scale=inv_n, bias=EPS),
               reads=[key], writes=[key])
            op('dve', lambda e: e.reciprocal(stat[:, c + 3:c + 4], stat[:, c + 2:c + 3]), reads=[key], writes=[key])
            return stat[:, c + 3:c + 4], key

        R1_END = 90 * 1024

        for b in range(NSEQ):
            A1 = Arena(arena, 0, R1_END)
            QT = A1.take([128, 4, S], BF16)
            KTz_flat = A1.take([128, 8 * S], BF16)
            KTz = KTz_flat.rearrange("p (a b) -> p a b", a=8)
            KT = KTz_flat[:, 0:4 * S].rearrange("p (a b) -> p a b", a=4)
            Vs_flat = A1.take([128, NB * 512], BF16)
            Vs = Vs_flat.rearrange("p (a b) -> p a b", a=NB)
            cT = A1.take([128, 5, S], BF16)
            kr_tm = A1.take([128, NB, 32], F32)
            cos_t = A1.take([128, NB, 16], F32)
            sin_t = A1.take([128, NB, 16], F32)
            kro = A1.take([128, NB, 32], BF16)

            A2 = Arena(arena, R1_END, ARENA_BYTES)
            win = A2.take([128, 8, 2208], BF16)
            gmix = A2.take([128, D], F32)
            gcb = A2.take([128, 640], F32)
            xt = [A2.take([128, D], F32) for _ in range(2)]
            xn = [A2.take([128, D], BF16) for _ in range(2)]
            hT = [A2.take([128, 8, 512], BF16) for _ in range(2)]
            junk = A2.take([128, D], BF16)
            cn = [A2.take([128, 640], BF16) for _ in range(2)]
            posi = A2.take([128, NB], I32)
            posf = A2.take([128, NB], F32)
            ang = A2.take([128, NB, 16], F32)
            rt = [A2.take([128, NB, 16], F32) for _ in range(4)]
            ki = A2.take([128, NB, 16], I32)
            sc = [A2.take([128, 8], F32) for _ in range(2)]

            for kc in range(8):
                dma('pool', ('win', kc), lambda e, kc=kc: e.dma_start(out=win[:, kc, :], in_=win_v[:, kc, :]), writes=[('win', kc)])
            dma('sp', 'gmix', lambda e: e.dma_start(out=gmix, in_=gmix_d[:, :]), writes=['gmix'])
            dma('sp', 'gcb', lambda e: e.dma_start(out=gcb, in_=gc_d[:, :]), writes=['gcb'])
            dma('sp', 'posi', lambda e, b=b: e.dma_start(out=posi, in_=pos_d[b, :, :]), writes=['posi'])

            op('pool', lambda e: e.memset(KTz_flat, 0.0), writes=[('KT', c_) for c_ in range(4)])

            op('dve', lambda e: e.tensor_copy(posf, posi), reads=['posi'], writes=['posf'])
            op('dve', lambda e: e.tensor_tensor(out=ang, in0=invf[:].unsqueeze(1).to_broadcast([128, NB, 16]),
                                                in1=posf.unsqueeze(2).to_broadcast([128, NB, 16]), op=ALU.mult),
               reads=['posf', 'invf'], writes=['ang'])
            for which, shift, dst in ((0, 0.0, sin_t), (1, 0.25, cos_t)):
                t0, t1 = rt[0], rt[1]
                op('dve', lambda e, shift=shift: e.tensor_scalar(out=t0, in0=ang, scalar1=1.0 / TWO_PI, scalar2=shift, op0=ALU.mult, op1=ALU.add),
                   reads=['ang'], writes=['rt0'])
                op('dve', lambda e: e.tensor_copy(ki, t0), reads=['rt0'], writes=['ki'])
                op('dve', lambda e: e.tensor_copy(t0, ki), reads=['ki'], writes=['rt0'])
                op('dve', lambda e: e.scalar_tensor_tensor(out=t1, in0=t0, scalar=-CW1, in1=ang, op0=ALU.mult, op1=ALU.add),
                   reads=['rt0', 'ang'], writes=['rt1'])
                op('dve', lambda e: e.scalar_tensor_tensor(out=t1, in0=t0, scalar=-CW2, in1=t1, op0=ALU.mult, op1=ALU.add),
                   reads=['rt0', 'rt1'], writes=['rt1'])
                if which == 1:
                    op('dve', lambda e: e.tensor_scalar(out=t1, in0=t1, scalar1=math.pi / 2, scalar2=None, op0=ALU.add),
                       reads=['rt1'], writes=['rt1'])
                op('dve', lambda e: e.tensor_scalar(out=t1, in0=t1, scalar1=PI_SAFE, scalar2=-PI_SAFE, op0=ALU.min, op1=ALU.max),
                   reads=['rt1'], writes=['rt1'])
                op('act', lambda e, dst=dst: e.activation(out=dst, in_=t1, func=AF.Sin), reads=['rt1'], writes=[('trig', which)])

            pbi = [0]

            def nextbank(lo=0, hi=4):
                i = lo + pbi[0] % (hi - lo)
                pbi[0] += 1
                return i

            def p1_norm(T, blk):
                tb = T * 4 + blk
                s = tb % 2
                dma('sp', ('xt', s), lambda e, b=b: e.dma_start(out=xt[s], in_=x_d[b, tb * 128:(tb + 1) * 128, :]),
                    writes=[('xt', s)])
                rstd, rkey = rms_rstd([(xt[s], [('xt', s)])], 1.0 / D, junk, 'junk')
                op('dve', lambda e: e.scalar_tensor_tensor(out=xn[s], in0=xt[s], scalar=rstd, in1=gmix, op0=ALU.mult, op1=ALU.mult),
                   reads=[('xt', s), rkey, 'gmix'], writes=[('xn', s)])
                S_.wtag(('xn', s), tb)

            def p1_tr(T, blk):
                tb = T * 4 + blk
                s = tb % 2
                hs = T % 2
                pi = 6 + (tb % 2)
                ptr = banks[pi][:].bitcast(BF16)
                S_.rtag(('xn', s), tb)

                def tr8(e):
                    for kc in range(8):
                        ins = e.transpose(ptr[:, kc * 128:(kc + 1) * 128], xn[s][:, kc * 128:(kc + 1) * 128], ident)
                    return ins
                op('pe', tr8, reads=[('xn', s), 'cbf'], writes=[bk(pi)])
                evac(hT[hs][:, :, blk * 128:(blk + 1) * 128], ptr.rearrange("p (a b) -> p a b", a=8),
                     reads=[bk(pi)], pwrites=[('hT', hs)])

            qkb = [0]

            def p1_qk(T, oc):
                hs = T % 2
                hTs, hkey = hT[hs], ('hT', hs)
                bi = 4 + qkb[0] % 2
                qkb[0] += 1

                def mmqk(e):
                    for kc in range(8):
                        ins = e.matmul(banks[bi][:], lhsT=win[:, kc, oc * 128:(oc + 1) * 128], rhs=hTs[:, kc, :],
                                       start=(kc == 0), stop=(kc == 7))
                    return ins
                op('pe', mmqk, reads=[hkey] + [('win', kc) for kc in range(8)], writes=[bk(bi)])
                if oc < 4:
                    evac(QT[:, oc, T * 512:(T + 1) * 512], banks[bi][:], reads=[bk(bi)], pwrites=[('QT', oc)], scale=SB_SCALE)
                else:
                    c_ = oc - 4
                    evac(KTz[0:64, 2 * c_, T * 512:(T + 1) * 512], banks[bi][0:64, :], reads=[bk(bi)], pwrites=[('KT', c_)])
                    evac(KTz[64:128, 2 * c_ + 1, T * 512:(T + 1) * 512], banks[bi][64:128, :], reads=[bk(bi)], pwrites=[('KT', c_)])

            def p1_v(T, blk):
                tb = T * 4 + blk
                hs = T % 2
                hTs, hkey = hT[hs], ('hT', hs)
                bi = 4 + qkb[0] % 2
                qkb[0] += 1

                def mmv(e):
                    for kc in range(8):
                        ins = e.matmul(banks[bi][:], lhsT=hTs[:, kc, blk * 128:(blk + 1) * 128], rhs=win[:, kc, 1024:1536],
                                       start=(kc == 0), stop=(kc == 7))
                    return ins
                op('pe', mmv, reads=[hkey] + [('win', kc) for kc in range(8)], writes=[bk(bi)])
                evac(Vs[:, tb, :], banks[bi][:], reads=[bk(bi)], writes=[('V', tb)])

            def p1_c_mm(T, blk):
                tb = T * 4 + blk
                hs = T % 2
                hTs, hkey = hT[hs], ('hT', hs)
                ba, bb = (0, 1) if blk % 2 == 0 else (2, 3)

                def mmc(e):
                    for kc in range(8):
                        e.matmul(banks[ba][:, 0:416], lhsT=hTs[:, kc, blk * 128:(blk + 1) * 128], rhs=win[:, kc, 1536:1952],
                                 start=(kc == 0), stop=(kc == 7))
                    for kc in range(8):
                        ins = e.matmul(banks[bb][:, 0:256], lhsT=hTs[:, kc, blk * 128:(blk + 1) * 128], rhs=win[:, kc, 1952:2208],
                                       start=(kc == 0), stop=(kc == 7))
                    return ins
                op('pe', mmc, reads=[hkey] + [('win', kc) for kc in range(8)], writes=[bk(ba), bk(bb)])
                cs = tb % 2
                sck = ('sc', cs)
                scs = sc[cs]
                op('act', lambda e: e.activation(out=junk[:, 0:384], in_=banks[ba][:, 0:384], func=AF.Square, accum_out=scs[:, 0:1]),
                   reads=[bk(ba)], writes=['junk', sck])
                op('act', lambda e: e.activation(out=junk[:, 0:256], in_=banks[bb][:, 0:256], func=AF.Square, accum_out=scs[:, 1:2]),
                   reads=[bk(bb)], writes=['junk'], pwrites=[sck])
                op('act', lambda e: e.activation(out=kr_tm[:, tb, :], in_=banks[ba][:, 384:416], func=AF.Copy),
                   reads=[bk(ba)], pwrites=['kr_tm'])
                op('dve', lambda e: e.tensor_tensor(out=scs[:, 2:4], in0=scs[:, 0:2], in1=cscale[:, 0:2], op=ALU.mult),
                   reads=[sck, 'cscale'], writes=[sck])
                op('act', lambda e: e.activation(out=scs[:, 4:6], in_=scs[:, 2:4], func=AF.Sqrt, bias=EPS), reads=[sck], writes=[sck])
                op('dve', lambda e: e.reciprocal(scs[:, 6:8], scs[:, 4:6]), reads=[sck], writes=[sck])
                op('dve', lambda e: e.scalar_tensor_tensor(out=cn[cs][:, 0:384], in0=banks[ba][:, 0:384], scalar=scs[:, 6:7],
                                                           in1=gcb[:, 0:384], op0=ALU.mult, op1=ALU.mult),
                   reads=[bk(ba), sck, 'gcb'], writes=[('cn', cs)])
                op('dve', lambda e: e.scalar_tensor_tensor(out=cn[cs][:, 384:640], in0=banks[bb][:, 0:256], scalar=scs[:, 7:8],
                                                           in1=gcb[:, 384:640], op0=ALU.mult, op1=ALU.mult),
                   reads=[bk(bb), sck, 'gcb'], pwrites=[('cn', cs)])
                S_.wtag(('cn', cs), tb)

            def p1_c_tr(T, blk):
                tb = T * 4 + blk
                cs = tb % 2
                pi = 6 + (tb % 2)
                ptr = banks[pi][:].bitcast(BF16)
                S_.rtag(('cn', cs), tb)

                def tr5(e):
                    for i in range(5):
                        ins = e.transpose(ptr[:, i * 128:(i + 1) * 128], cn[cs][:, i * 128:(i + 1) * 128], ident)
                    return ins
                op('pe', tr5, reads=[('cn', cs), 'cbf'], writes=[bk(pi)])
                evac(cT[:, :, tb * 128:(tb + 1) * 128], ptr[:, 0:640].rearrange("p (a b) -> p a b", a=5),
                     reads=[bk(pi)], writes=[('cT', tb)])

            p1_norm(0, 0)
            for blk in range(4):
                if blk + 1 < 4:
                    p1_norm(0, blk + 1)
                p1_tr(0, blk)
            for T in range(NT):
                nxt = (T + 1 < NT)
                p1_c_mm(T, 0)
                p1_c_mm(T, 1)
                if nxt:
                    p1_norm(T + 1, 0)
                for oc in range(0, 4):
                    p1_qk(T, oc)
                p1_c_tr(T, 0)
                p1_c_tr(T, 1)
                p1_c_mm(T, 2)
                p1_c_mm(T, 3)
                if nxt:
                    p1_tr(T + 1, 0)
                    p1_norm(T + 1, 1)
                for oc in range(4, 8):
                    p1_qk(T, oc)
                p1_c_tr(T, 2)
                p1_c_tr(T, 3)
                if nxt:
                    p1_tr(T + 1, 1)
                    p1_norm(T + 1, 2)
                for blk in range(4):
                    p1_v(T, blk)
                    if nxt and blk == 1:
                        p1_tr(T + 1, 2)
                        p1_norm(T + 1, 3)
                if nxt:
                    p1_tr(T + 1, 3)
            x1v, x2v = kr_tm[:, :, 0:16], kr_tm[:, :, 16:32]
            op('dve', lambda e: e.tensor_tensor(out=rt[0], in0=x1v, in1=cos_t, op=ALU.mult), reads=['kr_tm', ('trig', 1)], writes=['rt0'])
            op('dve', lambda e: e.tensor_tensor(out=rt[1], in0=x2v, in1=sin_t, op=ALU.mult), reads=['kr_tm', ('trig', 0)], writes=['rt1'])
            op('dve', lambda e: e.tensor_tensor(out=kro[:, :, 0:16], in0=rt[0], in1=rt[1], op=ALU.subtract), reads=['rt0', 'rt1'], pwrites=['kro'])
            op('dve', lambda e: e.tensor_tensor(out=rt[2], in0=x2v, in1=cos_t, op=ALU.mult), reads=['kr_tm', ('trig', 1)], writes=['rt2'])
            op('dve', lambda e: e.tensor_tensor(out=rt[3], in0=x1v, in1=sin_t, op=ALU.mult), reads=['kr_tm', ('trig', 0)], writes=['rt3'])
            op('dve', lambda e: e.tensor_tensor(out=kro[:, :, 16:32], in0=rt[2], in1=rt[3], op=ALU.add), reads=['rt2', 'rt3'], pwrites=['kro'])

            S_.barrier()
            if PHASE_LIMIT < 2:
                S_.enabled = False

            A3 = Arena(arena, R1_END, ARENA_BYTES)
            e_sb = [A3.take([128, 512], F32) for _ in range(3)]
            sp_sb = [A3.take([128, 512], BF16) for _ in range(3)]
            a_sb = [A3.take([128, 512], BF16) for _ in range(3)]
            LKSb = [A3.take([128, S], BF16) for _ in range(2)]
            wuq = A3.take([128, 3, 768], BF16)
            wukv = A3.take([128, 2, 1024], BF16)
            qtm = [A3.take([128, 4, 128], BF16) for _ in range(2)]
            ktm = [A3.take([128, 4, 128], BF16) for _ in range(2)]
            rq = [A3.take([128, 4, 16], F32) for _ in range(4)]
            rec = [A3.take([128, 8], F32) for _ in range(4)]
            mpc = [0]
            qr = A3.take([128, 4, 32], F32)

            for kc in range(3):
                dma('pool', ('wuq', kc), lambda e, kc=kc: e.dma_start(out=wuq[:, kc, :], in_=wuq_v[:, kc, :]), writes=[('wuq', kc)])
            for kc in range(2):
                dma('pool', ('wukv', kc), lambda e, kc=kc: e.dma_start(out=wukv[:, kc, :], in_=wukv_v[:, kc, :]), writes=[('wukv', kc)])

            def pieces():
                lst = []
                for kb in range(NB - 1, -1, -1):
                    for c in range(NT):
                        lo = max(512 * c, kb * 128)
                        hi = 512 * c + 512
                        if hi > lo:
                            lst.append((kb, c, lo, hi))
                return lst

            QPB = 8
            n_ob = (NB + QPB - 1) // QPB
            assert n_ob <= 2
            pcs = pieces()
            sb_items = []
            for h in range(8):
                for idx, (kb, c, lo, hi) in enumerate(pcs):
                    sb_items.append(dict(h=h, kb=kb, c=c, lo=lo, hi=hi, p=len(sb_items), first=(idx == 0), last=(idx == len(pcs) - 1)))

            def sb_zmm(e, bank_ap, h, kb, lo, hi, start, stop):
                return e.matmul(bank_ap, lhsT=KTz[:, h, kb * 128:(kb + 1) * 128], rhs=QT[:, h // 2, lo:hi], start=start, stop=stop)

            def sb_s0(it):
                h, kb, c, lo, hi, p = it['h'], it['kb'], it['c'], it['lo'], it['hi'], it['p']
                n = hi - lo
                hp = h % 2
                ch = h // 2
                if it['first']:
                    op('pool', lambda e: e.memset(LKSb[hp], 0.0), writes=[('LKSb', hp, cc) for cc in range(NT)])
                zb = p % 2
                op('pe', lambda e: sb_zmm(e, banks[zb][:, 0:n], h, kb, lo, hi, True, True),
                   reads=[('QT', ch), ('KT', ch)], writes=[bk(zb)])
                S_.wtag(bk(zb), ('z', p))

            def sb_s1(it):
                kb, lo, hi, p = it['kb'], it['lo'], it['hi'], it['p']
                n = hi - lo
                zb, es_ = p % 2, p % 3
                S_.rtag(bk(zb), ('z', p))
                op('act', lambda e: e.activation(out=e_sb[es_][:, 0:n], in_=banks[zb][:, 0:n], func=AF.Exp),
                   reads=[bk(zb)], writes=[('e', es_)])
                op('act', lambda e: e.activation(out=sp_sb[es_][:, 0:n], in_=e_sb[es_][:, 0:n], func=AF.Ln, bias=1.0),
                   reads=[('e', es_)], writes=[('sp', es_)])
                if lo == kb * 128:
                    op('pool', lambda e: e.tensor_tensor(out=sp_sb[es_][:, 0:128], in0=sp_sb[es_][:, 0:128], in1=mstrict, op=ALU.mult),
                       reads=[('sp', es_), 'cbf'], writes=[('sp', es_)])
                S_.wtag(('sp', es_), p)

            def sb_s2(it):
                h, kb, c, lo, hi, p = it['h'], it['kb'], it['c'], it['lo'], it['hi'], it['p']
                n = hi - lo
                hp = h % 2
                ch = h // 2
                cbk, ss_ = 2 + p % 2, p % 3
                diag = (lo == kb * 128)
                has_lks = (n > 128) or (not diag)
                S_.rtag(('sp', ss_), p)

                def mmc2(e):
                    e.matmul(banks[cbk][:, 0:n], lhsT=ntri, rhs=sp_sb[ss_][:, 0:n], start=True, stop=False)
                    if has_lks:
                        e.matmul(banks[cbk][:, 0:n], lhsT=nones, rhs=LKSb[hp][:, lo:hi], start=False, stop=False)
                    return sb_zmm(e, banks[cbk][:, 0:n], h, kb, lo, hi, False, True)
                op('pe', mmc2, reads=[('sp', ss_), 'cbf', ('QT', ch), ('KT', ch)] + ([('LKSb', hp, c)] if has_lks else []), writes=[bk(cbk)])
                S_.wtag(bk(cbk), ('c', p))
                if kb > 0:
                    op('dve', lambda e: e.tensor_tensor(out=LKSb[hp][:, lo:hi], in0=LKSb[hp][:, lo:hi], in1=sp_sb[ss_][:, 0:n], op=ALU.add),
                       reads=[('sp', ss_)], writes=[('LKSb', hp, c)])

            def sb_s3(it):
                kb, lo, hi, p = it['kb'], it['lo'], it['hi'], it['p']
                n = hi - lo
                cbk, as_ = 2 + p % 2, p % 3
                S_.rtag(bk(cbk), ('c', p))
                op('act', lambda e: e.activation(out=a_sb[as_][:, 0:n], in_=banks[cbk][:, 0:n], func=AF.Exp),
                   reads=[bk(cbk)], writes=[('a', as_)])
                if lo == kb * 128:
                    op('pool', lambda e: e.tensor_tensor(out=a_sb[as_][:, 0:128], in0=a_sb[as_][:, 0:128], in1=mstrict, op=ALU.mult),
                       reads=[('a', as_), 'cbf'], writes=[('a', as_)])
                S_.wtag(('a', as_), p)

            def sb_s4(it):
                h, kb, lo, hi, p = it['h'], it['kb'], it['lo'], it['hi'], it['p']
                as_ = p % 3
                ob0 = 4 if h % 2 == 0 else 6
                obanks = [ob0 + i for i in range(n_ob)]
                S_.rtag(('a', as_), p)
                if it['first']:
                    def zero_o(e):
                        for bi in obanks:
                            ins = e.matmul(banks[bi][:], lhsT=zer[:, 0:128], rhs=zer[:, :], start=True, stop=False, skip_group_check=True)
                        return ins
                    op('pe', zero_o, reads=['zer'], writes=[bk(bi) for bi in obanks])

                def mmav(e):
                    for qb in range(lo // 128, hi // 128):
                        bi = ob0 + qb // QPB
                        sl = (qb % QPB) * 64
                        last = (kb == 0) and (qb % QPB == QPB - 1 or qb == NB - 1)
                        ins = e.matmul(banks[bi][:, sl:sl + 64], lhsT=a_sb[as_][:, qb * 128 - lo:qb * 128 - lo + 128],
                                       rhs=Vs[:, kb, h * 64:(h + 1) * 64], start=False, stop=last, skip_group_check=True)
                    return ins
                obs = sorted(set(ob0 + qb // QPB for qb in range(lo // 128, hi // 128)))
                op('pe', mmav, reads=[('a', as_), ('V', kb)], pwrites=[bk(bi) for bi in obs])
                if it['last']:
                    for i, bi in enumerate(obanks):
                        nq = min(QPB, NB - i * QPB)
                        evac(o_tm[:, i * QPB:i * QPB + nq, h * 64:(h + 1) * 64],
                             banks[bi][:, 0:nq * 64].rearrange("p (a b) -> p a b", a=nq),
                             reads=[bk(bi)], pwrites=[('o', i)])

            run_pipeline(sb_items, [(0, sb_s0), (1, sb_s1), (2, sb_s2), (3, sb_s3), (4, sb_s4)])

            S_.barrier()
            if PHASE_LIMIT < 3:
                S_.enabled = False

            if OPLIM is not None:
                S_.oplimit = OPLIM
            QTm, KTm = QT, KT
            Vm = Vs[:, :, 0:260].rearrange("p a (h d) -> p a h d", h=4)
            MQ = 7
            n_mb = (NB + MQ - 1) // MQ
            for g in range(2):
                op('pool', lambda e: e.memset(Vs_flat, 1.0), writes=[('V', tb) for tb in range(NB)])
                if g == 0:
                    for qs_ in range(2):
                        op('pool', lambda e, qs_=qs_: e.memset(qtm[qs_].rearrange("p a b -> p (a b)"), 0.0), writes=[('qtm', qs_)])
                        op('pool', lambda e, qs_=qs_: e.memset(ktm[qs_].rearrange("p a b -> p (a b)"), 0.0), writes=[('ktm', qs_)])
                def mp_banks(tb):
                    return (0, 1) if tb % 2 == 0 else (2, 3)

                def mp_A(tb, g=g):
                    bq, bkv = mp_banks(tb)

                    def mmq(e):
                        for kc in range(3):
                            ins = e.matmul(banks[bq][:, 0:384], lhsT=cT[:, kc, tb * 128:(tb + 1) * 128], rhs=wuq[:, kc, g * 384:(g + 1) * 384],
                                           start=(kc == 0), stop=(kc == 2))
                        return ins
                    op('pe', mmq, reads=[('cT', tb)] + [('wuq', kc) for kc in range(3)], writes=[bk(bq)])

                    def mmkv(e):
                        for kc in range(2):
                            ins = e.matmul(banks[bkv][:, :], lhsT=cT[:, 3 + kc, tb * 128:(tb + 1) * 128], rhs=wukv[:, kc, g * 512:(g + 1) * 512],
                                           start=(kc == 0), stop=(kc == 1))
                        return ins
                    op('pe', mmkv, reads=[('cT', tb)] + [('wukv', kc) for kc in range(2)], writes=[bk(bkv)])
                    S_.wtag(bk(bq), ('mq', g, tb))

                def mp_B(tb, g=g):
                    qs = tb % 2
                    bq, bkv = mp_banks(tb)
                    S_.rtag(bk(bq), ('mq', g, tb))
                    qv = banks[bq][:, 0:384].rearrange("p (h d) -> p h d", h=4)
                    kvv = banks[bkv][:, :].rearrange("p (h d) -> p h d", h=4)
                    cosb = cos_t[:, tb, :].unsqueeze(1).to_broadcast([128, 4, 16])
                    sinb = sin_t[:, tb, :].unsqueeze(1).to_broadcast([128, 4, 16])
                    qk_ = ('qtm', qs)
                    kk_ = ('ktm', qs)
                    op('act', lambda e: e.activation(out=qr, in_=qv[:, :, 64:96], func=AF.Copy), reads=[bk(bq)], writes=['qr'])
                    op('act', lambda e: e.activation(out=qtm[qs][:, :, 0:64], in_=qv[:, :, 0:64], func=AF.Copy),
                       reads=[bk(bq)], pwrites=[qk_])
                    op('act', lambda e: e.activation(out=ktm[qs][:, :, 0:64], in_=kvv[:, :, 0:64], func=AF.Copy),
                       reads=[bk(bkv)], pwrites=[kk_])
                    op('dve', lambda e: e.tensor_tensor(out=rq[0], in0=qr[:, :, 0:16], in1=cosb, op=ALU.mult), reads=['qr', ('trig', 1)], writes=['rq0'])
                    op('dve', lambda e: e.tensor_tensor(out=rq[1], in0=qr[:, :, 16:32], in1=sinb, op=ALU.mult), reads=['qr', ('trig', 0)], writes=['rq1'])
                    op('dve', lambda e: e.tensor_tensor(out=qtm[qs][:, :, 64:80], in0=rq[0], in1=rq[1], op=ALU.subtract), reads=['rq0', 'rq1'], pwrites=[qk_])
                    op('dve', lambda e: e.tensor_tensor(out=rq[2], in0=qr[:, :, 16:32], in1=cosb, op=ALU.mult), reads=['qr', ('trig', 1)], writes=['rq2'])
                    op('dve', lambda e: e.tensor_tensor(out=rq[3], in0=qr[:, :, 0:16], in1=sinb, op=ALU.mult), reads=['qr', ('trig', 0)], writes=['rq3'])
                    op('dve', lambda e: e.tensor_tensor(out=qtm[qs][:, :, 80:96], in0=rq[2], in1=rq[3], op=ALU.add), reads=['rq2', 'rq3'], pwrites=[qk_])
                    for hl_ in range(4):
                        op('pool', lambda e, hl_=hl_: e.tensor_copy(ktm[qs][:, hl_, 64:96], kro[:, tb, :]),
                           reads=['kro'], pwrites=[kk_])
                    op('dve', lambda e: e.tensor_copy(Vm[:, tb, :, 0:64], kvv[:, :, 64:128]), reads=[bk(bkv)], pwrites=[('V', tb)])
                    S_.wtag(qk_, (g, tb))

                def mp_C(tb, g=g):
                    qs = tb % 2
                    qk_ = ('qtm', qs)
                    kk_ = ('ktm', qs)
                    S_.rtag(qk_, (g, tb))
                    pi = 6 + (tb % 2)
                    ptr = banks[pi][:].bitcast(BF16)

                    def trqk(e):
                        for hl in range(4):
                            e.transpose(ptr[:, hl * 128:(hl + 1) * 128], qtm[qs][:, hl, :], ident)
                        for hl in range(4):
                            ins = e.transpose(ptr[:, (4 + hl) * 128:(5 + hl) * 128], ktm[qs][:, hl, :], ident)
                        return ins
                    op('pe', trqk, reads=[qk_, kk_, 'cbf'], writes=[bk(pi)])
                    evac(QTm[:, :, tb * 128:(tb + 1) * 128], ptr[:, 0:512].rearrange("p (a b) -> p a b", a=4),
                         reads=[bk(pi)], pwrites=[('QT', hl) for hl in range(4)])
                    evac(KTm[:, :, tb * 128:(tb + 1) * 128], ptr[:, 512:1024].rearrange("p (a b) -> p a b", a=4),
                         reads=[bk(pi)], pwrites=[('KT', hl) for hl in range(4)])

                run_pipeline(list(range(NB)), [(0, mp_A), (1, mp_B), (2, mp_C)])
                if P3_MODE == 'proj':
                    S_.enabled = False
                m_items = []
                for hl in range(4):
                    for idx, (kb, c, lo, hi) in enumerate(pcs):
                        m_items.append(dict(hl=hl, kb=kb, c=c, lo=lo, hi=hi, idx=len(m_items), first=(idx == 0), last=(idx == len(pcs) - 1)))

                def ml_s0(it, g=g):
                    hl, kb, lo, hi = it['hl'], it['kb'], it['lo'], it['hi']
                    n = hi - lo
                    p = mpc[0] + it['idx']
                    zb = (0, 1, 5, 6)[p % 4]
                    op('pe', lambda e: e.matmul(banks[zb][:, 0:n], lhsT=KTm[:, hl, kb * 128:(kb + 1) * 128], rhs=QTm[:, hl, lo:hi], start=True, stop=True),
                       reads=[('QT', hl), ('KT', hl)], writes=[bk(zb)])
                    S_.wtag(bk(zb), ('zm', p))

                def ml_s1(it, g=g):
                    kb, lo, hi = it['kb'], it['lo'], it['hi']
                    n = hi - lo
                    p = mpc[0] + it['idx']
                    zb, as_ = (0, 1, 5, 6)[p % 4], p % 3
                    S_.rtag(bk(zb), ('zm', p))
                    op('act', lambda e: e.activation(out=a_sb[as_][:, 0:n], in_=banks[zb][:, 0:n], func=AF.Exp, scale=MLA_SCALE),
                       reads=[bk(zb)], writes=[('a', as_)])
                    if lo == kb * 128:
                        op('pool', lambda e: e.tensor_tensor(out=a_sb[as_][:, 0:128], in0=a_sb[as_][:, 0:128], in1=mincl, op=ALU.mult),
                           reads=[('a', as_), 'cbf'], writes=[('a', as_)])
                    S_.wtag(('a', as_), ('m', p))

                def ml_s2(it, g=g):
                    hl, kb, lo, hi = it['hl'], it['kb'], it['lo'], it['hi']
                    h = g * 4 + hl
                    p = mpc[0] + it['idx']
                    as_ = p % 3
                    ob0 = 2
                    obanks = [ob0 + i for i in range(n_mb)]
                    S_.rtag(('a', as_), ('m', p))
                    if it['first']:
                        def zero_m(e):
                            for bi in obanks:
                                ins = e.matmul(banks[bi][:], lhsT=zer[:, 0:128], rhs=zer[:, :], start=True, stop=False, skip_group_check=True)
                            return ins
                        op('pe', zero_m, reads=['zer'], writes=[bk(bi) for bi in obanks])

                    def mmavm(e):
                        for qb in range(lo // 128, hi // 128):
                            bi = ob0 + qb // MQ
                            sl = (qb % MQ) * 65
                            last = (kb == 0) and (qb % MQ == MQ - 1 or qb == NB - 1)
                            ins = e.matmul(banks[bi][:, sl:sl + 65], lhsT=a_sb[as_][:, qb * 128 - lo:qb * 128 - lo + 128],
                                           rhs=Vm[:, kb, hl, :], start=False, stop=last, skip_group_check=True)
                        return ins
                    obs = sorted(set(ob0 + qb // MQ for qb in range(lo // 128, hi // 128)))
                    op('pe', mmavm, reads=[('a', as_), ('V', kb)], pwrites=[bk(bi) for bi in obs])
                    if it['last']:
                        for i, bi in enumerate(obanks):
                            nq = min(MQ, NB - i * MQ)
                            ov = banks[bi][:, 0:nq * 65].rearrange("p (a b) -> p a b", a=nq)
                            rs = rec[(h * 4 + i) % 4]
                            rk = ('rec', (h * 4 + i) % 4)
                            op('dve', lambda e, ov=ov, nq=nq, rs=rs: e.reciprocal(rs[:, 0:nq], ov[:, :, 64]), reads=[bk(bi)], writes=[rk])
                            qbs = range(i * MQ, i * MQ + nq)
                            okeys = sorted(set(('o', qb // QPB) for qb in qbs))
                            op('dve', lambda e, ov=ov, nq=nq, i=i, h=h, rs=rs: e.tensor_tensor(
                                out=o_tm[:, i * MQ:i * MQ + nq, 512 + h * 64:512 + (h + 1) * 64], in0=ov[:, :, 0:64],
                                in1=rs[:, 0:nq].unsqueeze(2).to_broadcast([128, nq, 64]), op=ALU.mult),
                               reads=[bk(bi), rk], pwrites=okeys)

                run_pipeline(m_items, [(0, ml_s0), (2, ml_s1), (3, ml_s2)])
                mpc[0] += len(m_items)

            S_.barrier()
            if PHASE_LIMIT < 4:
                S_.enabled = False

            A4 = Arena(arena, 0, ARENA_BYTES)
            if DEBUG:
                dma('sp', 'dbg', lambda e, b=b: e.dma_start(out=dbg_d[b], in_=o_tm[:]), reads=[('o', i) for i in range(n_ob)], final=True)
            wout = A4.take([128, 8, D], BF16)
            wdn = A4.take([128, NJ, D], BF16)
            gob = A4.take([128, D], F32)
            gffn = A4.take([128, D], F32)
            gfin = A4.take([128, D], F32)
            xn4a = [A4.take([128, D], BF16) for _ in range(2)]
            xn4b = [A4.take([128, D], BF16) for _ in range(2)]
            junk4 = A4.take([128, D], BF16)
            oT = [A4.take([128, 8, 128], BF16) for _ in range(2)]
            h2T = A4.take([128, 8, 512], BF16)
            x1b = [A4.take([128, 4, D], F32) for _ in range(2)]
            gT = A4.take([128, NJ, 512], BF16)
            wup = [A4.take([128, 8, 256], BF16) for _ in range(3)]
            cg = [A4.take([128, 512], F32) for _ in range(2)]
            cv = [A4.take([128, 512], F32) for _ in range(2)]
            sg = [A4.take([128, 512], BF16) for _ in range(2)]
            halo = [A4.take([128, 2 * NJ, 2], F32) for _ in range(2)]
            so = [A4.take([128, 8], F32) for _ in range(4)]
            HC = A4.take([128, 2 * NJ, 2], F32)
            hct = A4.take([128, 2 * NJ], F32)
            cw3 = cw[:].rearrange("p (c k) -> p c k", k=4)

            for kc in range(8):
                dma('pool', ('wout', kc), lambda e, kc=kc: e.dma_start(out=wout[:, kc, :], in_=wout_v[:, kc, :]), writes=[('wout', kc)])
            for j0 in range(0, NJ, 2):
                dma('pool', ('wdn', j0), lambda e, j0=j0: e.dma_start(out=wdn[:, j0:j0 + 2, :], in_=wdn_v[:, j0:j0 + 2, :]), writes=[('wdn', j0), ('wdn', j0 + 1)])
            dma('sp', 'gob', lambda e: e.dma_start(out=gob, in_=go_d[:, :]), writes=['gob'])
            dma('sp', 'gffn', lambda e: e.dma_start(out=gffn, in_=gffn_d[:, :]), writes=['gffn'])
            dma('sp', 'gfin', lambda e: e.dma_start(out=gfin, in_=gfin_d[:, :]), writes=['gfin'])
            wu_ptr = [0]

            def ffn_dma_upto(g_hi):
                while wu_ptr[0] <= g_hi and wu_ptr[0] < NT * NJ:
                    g_ = wu_ptr[0]
                    wu_ptr[0] += 1
                    ws, j = g_ % 3, g_ % NJ
                    dma('pool', ('wup', ws), lambda e, ws=ws, j=j: e.dma_start(out=wup[ws], in_=wup_v[:, :, j * 256:(j + 1) * 256]), writes=[('wup', ws)])
                    S_.wtag(('wup', ws), g_)

            def ffn_mmu(T, j):
                g_ = T * NJ + j
                ws = g_ % 3
                bg, bv = ((0, 1), (2, 3), (4, 5))[j % 3]
                S_.rtag(('wup', ws), g_)

                def mmu(e):
                    for kc in range(8):
                        e.matmul(banks[bg][:], lhsT=wup[ws][:, kc, 0:128], rhs=h2T[:, kc, :], start=(kc == 0), stop=(kc == 7))
                    for kc in range(8):
                        ins = e.matmul(banks[bv][:], lhsT=wup[ws][:, kc, 128:256], rhs=h2T[:, kc, :], start=(kc == 0), stop=(kc == 7))
                    return ins
                op('pe', mmu, reads=[('wup', ws), 'h2T'], writes=[bk(bg), bk(bv)])
                S_.wtag(bk(bg), ('u', g_))

            def ffn_conv(T, j):
                g_ = T * NJ + j
                bg, bv = ((0, 1), (2, 3), (4, 5))[j % 3]
                S_.rtag(bk(bg), ('u', g_))
                us = j % 2
                hw, hr = halo[T % 2], halo[(T + 1) % 2]
                halves = ((0, bg, cg[us], ('cg', us)), (1, bv, cv[us], ('cv', us)))
                for (half, bi, dst, dkey) in halves:
                    cj = j * 2 + half
                    w2 = cw[:, cj * 4 + 2:cj * 4 + 3]
                    bb_ = cw[:, cj * 4 + 3:cj * 4 + 4]
                    op('act', lambda e, bi=bi, dst=dst, w2=w2, bb_=bb_: e.activation(out=dst, in_=banks[bi][:], func=AF.Identity, scale=w2, bias=bb_),
                       reads=[bk(bi), 'cw'], writes=[dkey])
                    if T < NT - 1:
                        op('act', lambda e, bi=bi, cj=cj: e.activation(out=hw[:, cj, :], in_=banks[bi][:, 510:512], func=AF.Copy),
                           reads=[bk(bi)], writes=[('halo', T % 2, cj)])
                for (half, bi, dst, dkey) in halves:
                    cj = j * 2 + half
                    w0 = cw[:, cj * 4 + 0:cj * 4 + 1]
                    w1 = cw[:, cj * 4 + 1:cj * 4 + 2]
                    hk = ('halo', (T + 1) % 2, cj)
                    op('dve', lambda e, bi=bi, dst=dst, w1=w1: e.scalar_tensor_tensor(out=dst[:, 1:512], in0=banks[bi][:, 0:511], scalar=w1, in1=dst[:, 1:512],
                                                                                   op0=ALU.mult, op1=ALU.add),
                       reads=[bk(bi), dkey, 'cw'], writes=[dkey])
                    op('dve', lambda e, bi=bi, dst=dst, w0=w0: e.scalar_tensor_tensor(out=dst[:, 2:512], in0=banks[bi][:, 0:510], scalar=w0, in1=dst[:, 2:512],
                                                                                   op0=ALU.mult, op1=ALU.add),
                       reads=[bk(bi), dkey, 'cw'], writes=[dkey])
                    if T > 0:
                        op('pool', lambda e, dst=dst, cj=cj: e.tensor_tensor(out=dst[:, 0:2], in0=dst[:, 0:2], in1=HC[:, cj, :], op=ALU.add),
                           reads=['HC', dkey], writes=[dkey])
                    if half == 0:
                        op('act', lambda e: e.activation(out=sg[us], in_=cg[us], func=AF.Silu), reads=[('cg', us)], writes=[('sg', us)])
                op('pool', lambda e: e.tensor_tensor(out=gT[:, j, :], in0=sg[us], in1=cv[us], op=ALU.mult),
                   reads=[('sg', us), ('cv', us)], writes=[('gT', j)])

            def pro_A(T, i):
                tb = T * 4 + i
                par, s = T % 2, i % 2
                x1 = x1b[par]
                okey = ('o', tb // QPB)
                dma('sp', ('xin', par, i), lambda e, b=b: e.dma_start(out=x1[:, i, :], in_=x_d[b, tb * 128:(tb + 1) * 128, :]),
                    writes=[('x1', par, i)])
                sos = so[i]
                sok = ('so', i)
                op('act', lambda e: e.activation(out=junk4[:, 0:512], in_=o_tm[:, tb, 0:512], func=AF.Square, accum_out=sos[:, 0:1]),
                   reads=[okey], writes=['junk4', sok])
                op('act', lambda e: e.activation(out=junk4[:, 512:1024], in_=o_tm[:, tb, 512:1024], func=AF.Square, accum_out=sos[:, 1:2]),
                   reads=[okey], pwrites=['junk4', sok])
                op('act', lambda e: e.activation(out=sos[:, 2:4], in_=sos[:, 0:2], func=AF.Sqrt, scale=1.0 / 512.0, bias=EPS), reads=[sok], writes=[sok])
                op('dve', lambda e: e.reciprocal(sos[:, 4:6], sos[:, 2:4]), reads=[sok], writes=[sok])
                for half in range(2):
                    op('dve', lambda e, half=half: e.scalar_tensor_tensor(
                        out=xn4a[s][:, half * 512:(half + 1) * 512], in0=o_tm[:, tb, half * 512:(half + 1) * 512], scalar=sos[:, 4 + half:5 + half],
                        in1=gob[:, half * 512:(half + 1) * 512], op0=ALU.mult, op1=ALU.mult),
                       reads=[okey, sok, 'gob'], **({'writes': [('xn4a', s)]} if half == 0 else {'pwrites': [('xn4a', s)]}))
                S_.wtag(('xn4a', s), tb)

            def pro_B(T, i):
                tb = T * 4 + i
                s = i % 2
                pi = 6 + s
                ptr = banks[pi][:].bitcast(BF16)
                S_.rtag(('xn4a', s), tb)

                def tro(e):
                    for kc in range(8):
                        ins = e.transpose(ptr[:, kc * 128:(kc + 1) * 128], xn4a[s][:, kc * 128:(kc + 1) * 128], ident)
                    return ins
                op('pe', tro, reads=[('xn4a', s), 'cbf'], writes=[bk(pi)])
                evac(oT[s], ptr.rearrange("p (a b) -> p a b", a=8), reads=[bk(pi)], writes=[('oT', s)])
                S_.wtag(('oT', s), tb)

            def pro_C(T, i):
                tb = T * 4 + i
                par, s = T % 2, i % 2
                x1 = x1b[par]
                S_.rtag(('oT', s), tb)
                for half in range(2):
                    bi = 4 + half

                    def mmo(e, half=half, bi=bi):
                        for kc in range(8):
                            ins = e.matmul(banks[bi][:], lhsT=oT[s][:, kc, :], rhs=wout[:, kc, half * 512:(half + 1) * 512],
                                           start=(kc == 0), stop=(kc == 7))
                        return ins
                    op('pe', mmo, reads=[('oT', s)] + [('wout', kc) for kc in range(8)], writes=[bk(bi)])
                    op('dve', lambda e, bi=bi, half=half: e.tensor_tensor(
                        out=x1[:, i, half * 512:(half + 1) * 512], in0=banks[bi][:], in1=x1[:, i, half * 512:(half + 1) * 512], op=ALU.add),
                       reads=[bk(bi), ('x1', par, i)], pwrites=[('x1', par, i)])

            def pro_D(T, i):
                tb = T * 4 + i
                par, s = T % 2, i % 2
                x1 = x1b[par]
                rstd, rkey = rms_rstd([(x1[:, i, :], [('x1', par, i)])], 1.0 / D, junk4, 'junk4')
                op('dve', lambda e: e.scalar_tensor_tensor(out=xn4b[s], in0=x1[:, i, :], scalar=rstd, in1=gffn, op0=ALU.mult, op1=ALU.mult),
                   reads=[('x1', par, i), rkey, 'gffn'], writes=[('xn4b', s)])
                S_.wtag(('xn4b', s), tb)

            def pro_E(T, i):
                tb = T * 4 + i
                s = i % 2
                pi = 6 + s
                ptr = banks[pi][:].bitcast(BF16)
                S_.rtag(('xn4b', s), tb)

                def tro(e):
                    for kc in range(8):
                        ins = e.transpose(ptr[:, kc * 128:(kc + 1) * 128], xn4b[s][:, kc * 128:(kc + 1) * 128], ident)
                    return ins
                op('pe', tro, reads=[('xn4b', s), 'cbf'], writes=[bk(pi)])
                evac(h2T[:, :, i * 128:(i + 1) * 128], ptr.rearrange("p (a b) -> p a b", a=8), reads=[bk(pi)], pwrites=['h2T'])

            def epi_A(T, i):
                par = T % 2
                x1 = x1b[par]
                for half in range(2):
                    bi = (i % 2) * 2 + half

                    def mmd(e, half=half, bi=bi):
                        for j in range(NJ):
                            ins = e.matmul(banks[bi][:], lhsT=gT[:, j, i * 128:(i + 1) * 128], rhs=wdn[:, j, half * 512:(half + 1) * 512],
                                           start=(j == 0), stop=(j == NJ - 1))
                        return ins
                    op('pe', mmd, reads=[('gT', j) for j in range(NJ)] + [('wdn', j) for j in range(NJ)], writes=[bk(bi)])
                    op('dve', lambda e, bi=bi, half=half: e.tensor_tensor(
                        out=x1[:, i, half * 512:(half + 1) * 512], in0=banks[bi][:], in1=x1[:, i, half * 512:(half + 1) * 512], op=ALU.add),
                       reads=[bk(bi), ('x1', par, i)], pwrites=[('x1', par, i)])

            def epi_B(T, i):
                tb = T * 4 + i
                par = T % 2
                x1 = x1b[par]
                rstd, rkey = rms_rstd([(x1[:, i, :], [('x1', par, i)])], 1.0 / D, junk4, 'junk4')
                op('dve', lambda e: e.scalar_tensor_tensor(out=x1[:, i, :], in0=x1[:, i, :], scalar=rstd, in1=gfin, op0=ALU.mult, op1=ALU.mult),
                   reads=[rkey, 'gfin'], writes=[('x1', par, i)])
                dma('sp', ('yout', par, i), lambda e, b=b: e.dma_start(out=out_d[b, tb * 128:(tb + 1) * 128, :], in_=x1[:, i, :]),
                    reads=[('x1', par, i)], final=True)

            def pro_epi(Tp, Te):
                stages = []
                if Tp is not None:
                    stages.append((0, lambda i: pro_A(Tp, i)))
                if Te is not None:
                    stages.append((0, lambda i: epi_A(Te, i)))
                    stages.append((1, lambda i: epi_B(Te, i)))
                if Tp is not None:
                    stages.append((1, lambda i: pro_B(Tp, i)))
                    stages.append((2, lambda i: pro_C(Tp, i)))
                    stages.append((2, lambda i: pro_D(Tp, i)))
                    stages.append((3, lambda i: pro_E(Tp, i)))
                stages.sort(key=lambda t: t[0])
                run_pipeline(list(range(4)), stages)

            ffn_dma_upto(1)
            pro_epi(0, None)
            for T in range(NT):
                for j in range(NJ):
                    ffn_dma_upto(T * NJ + j + 2)
                    ffn_mmu(T, j)
                    if j >= 2:
                        ffn_conv(T, j - 2)
                ffn_conv(T, NJ - 2)
                ffn_conv(T, NJ - 1)
                if T + 1 < NT:
                    hw_ = halo[T % 2]
                    hkeys = [('halo', T % 2, cj) for cj in range(2 * NJ)]
                    op('dve', lambda e, hw_=hw_: e.tensor_tensor(out=HC[:, :, 0], in0=hw_[:, :, 1], in1=cw3[:, :, 1], op=ALU.mult), reads=hkeys + ['cw'], writes=['HC'])
                    op('dve', lambda e, hw_=hw_: e.tensor_tensor(out=hct, in0=hw_[:, :, 0], in1=cw3[:, :, 0], op=ALU.mult), reads=hkeys + ['cw'], writes=['hct'])
                    op('dve', lambda e: e.tensor_tensor(out=HC[:, :, 0], in0=HC[:, :, 0], in1=hct, op=ALU.add), reads=['hct', 'HC'], pwrites=['HC'])
                    op('dve', lambda e, hw_=hw_: e.tensor_tensor(out=HC[:, :, 1], in0=hw_[:, :, 1], in1=cw3[:, :, 0], op=ALU.mult), reads=hkeys + ['cw'], pwrites=['HC'])
                pro_epi(T + 1 if T + 1 < NT else None, T)

            S_.barrier()

        S_.emit()
    return nc


def _prep_shared(inputs):
    f32 = np.float32
    w_in = np.asarray(inputs["w_in"][0], f32)
    perm = np.concatenate([np.arange(0, 1536), np.arange(1536, 1920), np.arange(2176, 2208), np.arange(1920, 2176)])
    w_in_p = np.ascontiguousarray(w_in[:, perm])
    w_up = np.asarray(inputs["w_up"][0], f32)
    cols = np.concatenate([np.concatenate([np.arange(j * 128, (j + 1) * 128), DFF + np.arange(j * 128, (j + 1) * 128)]) for j in range(NJ)])
    w_up_p = np.ascontiguousarray(w_up[:, cols])
    conv_w = np.asarray(inputs["conv_w"][0], f32)
    conv_b = np.asarray(inputs["conv_b"][0], f32)
    cw = np.zeros((128, 2 * NJ, 4), f32)
    for j in range(NJ):
        for half in range(2):
            ch = half * DFF + j * 128 + np.arange(128)
            cw[:, j * 2 + half, 0] = conv_w[0, ch]
            cw[:, j * 2 + half, 1] = conv_w[1, ch]
            cw[:, j * 2 + half, 2] = conv_w[2, ch]
            cw[:, j * 2 + half, 3] = conv_b[ch]

    def bc(v):
        return np.ascontiguousarray(np.broadcast_to(np.asarray(v, f32)[None, :], (128, v.shape[0])))
    g_c = np.concatenate([np.asarray(inputs["g_cq"][0], f32), np.asarray(inputs["g_ckv"][0], f32)])
    g_o = np.concatenate([np.asarray(inputs["g_sb_out"][0], f32), np.asarray(inputs["g_mla_out"][0], f32)])
    ii = np.arange(128)
    ident = np.eye(128, dtype=f32)
    tri = (ii[:, None] >= ii[None, :]).astype(f32)
    ones = np.ones((128, 128), f32)
    mstrict = (ii[:, None] < ii[None, :]).astype(f32)
    mincl = (ii[:, None] <= ii[None, :]).astype(f32)
    cbf = np.concatenate([ident, tri, ones, mstrict, mincl, -tri, -ones], axis=1).astype(ml_dtypes.bfloat16)
    half = 16
    inv_freq = (1.0 / (np.float32(10000.0) ** (np.arange(half, dtype=f32) * np.float32(2.0 / 32)))).astype(f32)
    return {
        "w_in": w_in_p,
        "w_uq": np.ascontiguousarray(np.asarray(inputs["w_uq"][0], f32)),
        "w_ukv": np.ascontiguousarray(np.asarray(inputs["w_ukv"][0], f32)),
        "w_out": np.ascontiguousarray(np.asarray(inputs["w_out"][0], f32)),
        "w_up": w_up_p,
        "w_down": np.ascontiguousarray(np.asarray(inputs["w_down"][0], f32)),
        "g_mix": bc(inputs["g_mix"][0]),
        "g_c": bc(g_c),
        "g_o": bc(g_o),
        "g_ffn": bc(inputs["g_ffn"][0]),
        "g_final": bc(inputs["g_final"]),
        "cw": np.ascontiguousarray(cw.reshape(128, 2 * NJ * 4)),
        "cbf": cbf,
        "invf": bc(inv_freq),
    }


_PROG_CACHE = {}


def kernel(**inputs):
    x = np.asarray(inputs["x"], np.float32)
    pos = np.asarray(inputs["positions"], np.int32)
    B, S, _ = x.shape
    ncores = NCORES if B % NCORES == 0 else B
    nseq = B // ncores
    shared = _prep_shared(inputs)
    key = (S, nseq)
    if key not in _PROG_CACHE:
        _PROG_CACHE[key] = build_program(S, nseq)
    nc = _PROG_CACHE[key]
    NB = S // 128
    in_maps = []
    for c in range(ncores):
        m = dict(shared)
        m["x"] = np.ascontiguousarray(x[c * nseq:(c + 1) * nseq])
        p = pos[c * nseq:(c + 1) * nseq].reshape(nseq, NB, 128).transpose(0, 2, 1)
        m["pos"] = np.ascontiguousarray(p)
        in_maps.append(m)
    res = run_bass_kernel_spmd(nc, in_maps, core_ids=list(range(ncores)))
    if DEBUG:
        global LAST_RES
        LAST_RES = res
    out = np.concatenate([np.asarray(r["out"], np.float32) for r in res.results], axis=0)
    return out
```

```python
import contextlib
import math
import numpy as np
import ml_dtypes
import concourse.bass as bass
import concourse.mybir as mybir
from concourse.bass_utils import run_bass_kernel_spmd

F32 = mybir.dt.float32
BF16 = mybir.dt.bfloat16
I32 = mybir.dt.int32
U8 = mybir.dt.uint8
AF = mybir.ActivationFunctionType
ALU = mybir.AluOpType

D = 1024
NCORES = 8
EPS = 1e-6
DFF = 2816
NJ = DFF // 128
SB_SCALE = 64 ** -0.5
MLA_SCALE = 96 ** -0.5
TWO_PI = 2.0 * math.pi
CW1 = 6.28125
CW2 = TWO_PI - CW1
PI_SAFE = 3.1415925
PHASE_LIMIT = 4
P3_MODE = 'all'
OPLIM = None
DEBUG = False
LAST_RES = None


class Sched:
    ENGS = ('pe', 'act', 'dve', 'pool', 'sp')

    def __init__(self, nc, es):
        self.nc = nc
        self.es = es
        self.streams = {e: [] for e in self.ENGS}
        self.cnt = {e: 0 for e in self.ENGS}
        self.esem = {e: es.enter_context(nc.semaphore("sem_" + e)) for e in self.ENGS}
        self.waited = {e: {} for e in self.ENGS}
        self.bufs = {}
        self.dsem = {}
        self.semobj = {}
        self.final_tokens = []
        self.tags = {}

    def _st(self, b):
        return self.bufs.setdefault(b, {'w': [], 'r': []})

    def _deps(self, reads, writes, pwrites):
        toks = []
        for b in reads:
            st = self._st(b)
            toks += st['w']
            if isinstance(b, tuple) and b[0] == 'pb':
                toks += st['r']
        for b in writes:
            st = self._st(b)
            toks += st['w'] + st['r']
        for b in pwrites:
            st = self._st(b)
            toks += st['w'] + st['r']
        return toks

    def _update(self, tok, reads, writes, pwrites):
        for b in reads:
            self.bufs[b]['r'].append(tok)
        for b in writes:
            self.bufs[b]['w'] = [tok]
            self.bufs[b]['r'] = []
        for b in pwrites:
            st = self.bufs[b]
            st['w'].append(tok)
            if len(st['w']) > 64:
                best = {}
                for t in st['w']:
                    k = id(t[0])
                    if k not in best or best[k][1] < t[1]:
                        best[k] = t
                st['w'] = list(best.values())
            st['r'] = []

    def _waits(self, eng, toks):
        need = {}
        for (sem, val, src) in toks:
            if src == eng and eng == 'pe':
                continue
            k = id(sem)
            self.semobj[k] = sem
            if self.waited[eng].get(k, 0) >= val:
                continue
            if need.get(k, 0) < val:
                need[k] = val
        out = []
        for k, v in need.items():
            self.waited[eng][k] = v
            out.append((self.semobj[k], v))
        return out

    enabled = True

    oplimit = None

    def op(self, eng, fn, reads=(), writes=(), pwrites=()):
        if not self.enabled:
            return None
        if self.oplimit is not None:
            if self.oplimit <= 0:
                return None
            self.oplimit -= 1
        toks = self._deps(reads, writes, pwrites)
        waits = self._waits(eng, toks)
        self.cnt[eng] += 1
        tok = (self.esem[eng], self.cnt[eng], eng)
        self.streams[eng].append((waits, fn, self.esem[eng], 1))
        self._update(tok, reads, writes, pwrites)
        return tok

    def dma(self, eng, key, fn, reads=(), writes=(), pwrites=(), final=False):
        if not self.enabled:
            return None
        toks = self._deps(reads, writes, pwrites)
        if key not in self.dsem:
            self.dsem[key] = [self.es.enter_context(self.nc.semaphore("dsem_%d" % len(self.dsem))), 0]
        ent = self.dsem[key]
        if ent[1] > 0:
            toks = toks + [(ent[0], 16 * ent[1], 'dma')]
        waits = self._waits(eng, toks)
        ent[1] += 1
        tok = (ent[0], 16 * ent[1], 'dma')
        self.streams[eng].append((waits, fn, ent[0], 16))
        self._update(tok, reads, writes, pwrites)
        if final:
            self.final_tokens.append(tok)
        return tok

    def wtag(self, key, tag):
        self.tags[key] = tag

    def rtag(self, key, tag):
        assert self.tags.get(key) == tag, ("pipeline slot hazard", key, self.tags.get(key), tag)

    def barrier(self):
        toks = [(self.esem[e], self.cnt[e], e + '_b') for e in self.ENGS if self.cnt[e] > 0]
        toks += [(ent[0], 16 * ent[1], 'dma') for ent in self.dsem.values() if ent[1] > 0]
        for e in self.ENGS:
            waits = self._waits(e, [t for t in toks if not t[2].startswith(e + '_')])
            if waits:
                self.streams[e].append((waits, None, None, 0))
        self.bufs = {}

    def emit(self):
        nc = self.nc
        with nc.Block() as block:
            def replay(name):
                def f(eng):
                    for (waits, fn, sem, inc) in self.streams[name]:
                        for (s, v) in waits:
                            eng.wait_ge(s, v)
                        if fn is None:
                            continue
                        ins = fn(eng)
                        ins.then_inc(sem, inc)
                    if name == 'sp':
                        best = {}
                        for (s, v, _) in self.final_tokens:
                            k = id(s)
                            if k not in best or best[k][1] < v:
                                best[k] = (s, v)
                        for (s, v) in best.values():
                            eng.wait_ge(s, v)
                return f
            if self.streams['pe']:
                block.tensor(replay('pe'))
            if self.streams['act']:
                block.scalar(replay('act'))
            if self.streams['dve']:
                block.vector(replay('dve'))
            if self.streams['pool']:
                block.gpsimd(replay('pool'))
            block.sync(replay('sp'))


def run_pipeline(items, stages):
    n = len(items)
    maxlag = max(l for l, _ in stages)
    for t in range(n + maxlag):
        for lag, fn in stages:
            i = t - lag
            if 0 <= i < n:
                fn(items[i])


class Arena:
    def __init__(self, ap, start, limit):
        self.ap = ap
        self.off = start
        self.limit = limit
        self.n = 0

    def take(self, shape, dt):
        esz = {F32: 4, BF16: 2, I32: 4}[dt]
        free = 1
        for s in shape[1:]:
            free *= s
        nbytes = (free * esz + 63) // 64 * 64
        assert self.off + nbytes <= self.limit, ("arena overflow", self.off, nbytes, self.limit)
        v = self.ap[:, self.off:self.off + free * esz].bitcast(dt)
        self.off += nbytes
        if len(shape) == 3:
            v = v.rearrange("p (a b) -> p a b", a=shape[1])
        elif len(shape) == 4:
            v = v.rearrange("p (a b c) -> p a b c", a=shape[1], b=shape[2])
        if shape[0] < 128:
            v = v[0:shape[0]]
        return v


def build_program(S, NSEQ):
    NB = S // 128
    NT = S // 512
    assert S % 512 == 0
    nc = bass.Bass("TRN2", target_bir_lowering=False)

    def din(name, shape, dt):
        return nc.dram_tensor(name, list(shape), dt, kind="ExternalInput").ap()

    x_d = din("x", [NSEQ, S, D], F32)
    pos_d = din("pos", [NSEQ, 128, NB], I32)
    win_d = din("w_in", [D, 2208], F32)
    wuq_d = din("w_uq", [384, 768], F32)
    wukv_d = din("w_ukv", [256, 1024], F32)
    wout_d = din("w_out", [D, D], F32)
    wup_d = din("w_up", [D, 2 * DFF], F32)
    wdn_d = din("w_down", [DFF, D], F32)
    gmix_d = din("g_mix", [128, D], F32)
    gc_d = din("g_c", [128, 640], F32)
    go_d = din("g_o", [128, D], F32)
    gffn_d = din("g_ffn", [128, D], F32)
    gfin_d = din("g_final", [128, D], F32)
    cw_d = din("cw", [128, 2 * NJ * 4], F32)
    cb_d = din("cbf", [128, 7 * 128], BF16)
    invf_d = din("invf", [128, 16], F32)
    out_d = nc.dram_tensor("out", [NSEQ, S, D], F32, kind="ExternalOutput").ap()
    dbg_d = nc.dram_tensor("dbg", [NSEQ, 128, NB, D], BF16, kind="ExternalOutput").ap() if DEBUG else None

    win_v = win_d.rearrange("(kc p) n -> p kc n", p=128)
    wuq_v = wuq_d.rearrange("(kc p) n -> p kc n", p=128)
    wukv_v = wukv_d.rearrange("(kc p) n -> p kc n", p=128)
    wout_v = wout_d.rearrange("(kc p) n -> p kc n", p=128)
    wup_v = wup_d.rearrange("(kc p) n -> p kc n", p=128)
    wdn_v = wdn_d.rearrange("(j p) n -> p j n", p=128)

    with contextlib.ExitStack() as es:
        S_ = Sched(nc, es)
        op, dma = S_.op, S_.dma

        def sbt(name, shape, dt):
            return es.enter_context(nc.sbuf_tensor(name, list(shape), dt))

        ARENA_BYTES = 172 * 1024
        arena = sbt("arena", [128, ARENA_BYTES], U8)
        o_tm = sbt("o_tm", [128, NB, D], BF16)
        cbf = sbt("cbf_sb", [128, 7 * 128], BF16)
        zer = sbt("zer", [128, 512], BF16)
        invf = sbt("invf_sb", [128, 16], F32)
        cw = sbt("cw_sb", [128, 2 * NJ * 4], F32)
        cscale = sbt("cscale", [128, 2], F32)
        stat = sbt("stat", [128, 64], F32)
        ident = cbf[:, 0:128]
        tri = cbf[:, 128:256]
        ones = cbf[:, 256:384]
        mstrict = cbf[:, 384:512]
        mincl = cbf[:, 512:640]
        ntri = cbf[:, 640:768]
        nones = cbf[:, 768:896]

        banks = [es.enter_context(nc.psum_tensor("pb%d" % i, [128, 512], F32)) for i in range(8)]

        def bk(i):
            return ('pb', i)

        dma('sp', 'c0', lambda e: e.dma_start(out=cbf[:], in_=cb_d[:, :]), writes=['cbf'])
        dma('sp', 'c1', lambda e: e.dma_start(out=invf[:], in_=invf_d[:, :]), writes=['invf'])
        dma('sp', 'c2', lambda e: e.dma_start(out=cw[:], in_=cw_d[:, :]), writes=['cw'])
        op('pool', lambda e: e.memset(zer[:], 0.0), writes=['zer'])
        op('pool', lambda e: e.memset(cscale[:, 0:1], 1.0 / 384.0), pwrites=['cscale'])
        op('pool', lambda e: e.memset(cscale[:, 1:2], 1.0 / 256.0), pwrites=['cscale'])

        evac_flip = [0]

        def evac(out_ap, in_ap, reads, writes=(), pwrites=(), scale=None):
            evac_flip[0] ^= 1
            if evac_flip[0]:
                if scale is None:
                    op('act', lambda e: e.activation(out=out_ap, in_=in_ap, func=AF.Copy), reads=reads, writes=writes, pwrites=pwrites)
                else:
                    op('act', lambda e: e.activation(out=out_ap, in_=in_ap, func=AF.Copy, scale=scale), reads=reads, writes=writes, pwrites=pwrites)
            else:
                if scale is None:
                    op('dve', lambda e: e.tensor_copy(out_ap, in_ap), reads=reads, writes=writes, pwrites=pwrites)
                else:
                    op('dve', lambda e: e.tensor_scalar(out=out_ap, in0=in_ap, scalar1=scale, scalar2=None, op0=ALU.mult), reads=reads, writes=writes, pwrites=pwrites)

        stat_ctr = [0]

        def stat_cols(n):
            c = stat_ctr[0]
            if c + n > 64:
                c = 0
            stat_ctr[0] = c + n
            return c

        def rms_rstd(src_list, inv_n, junk, junk_key):
            c = stat_cols(4)
            key = ('stat', c)
            assert len(src_list) == 1
            ap, rk = src_list[0]
            op('act', lambda e: e.activation(out=junk[:, 0:ap.shape[1]], in_=ap, func=AF.Square, accum_out=stat[:, c:c + 1]),
               reads=rk, writes=[junk_key, key])
            op('act', lambda e: e.activation(out=stat[:, c + 2:c + 3], in_=stat[:, c:c + 1], func=AF.Sqrt, scale=inv_n, bias=EPS),
               reads=[key], writes=[key])
            op('dve', lambda e: e.reciprocal(stat[:, c + 3:c + 4], stat[:, c + 2:c + 3]), reads=[key], writes=[key])
            return stat[:, c + 3:c + 4], key

        R1_END = 90 * 1024

        for b in range(NSEQ):
            A1 = Arena(arena, 0, R1_END)
            QT = A1.take([128, 4, S], BF16)
            KTz_flat = A1.take([128, 8 * S], BF16)
            KTz = KTz_flat.rearrange("p (a b) -> p a b", a=8)
            KT = KTz_flat[:, 0:4 * S].rearrange("p (a b) -> p a b", a=4)
            Vs_flat = A1.take([128, NB * 512], BF16)
            Vs = Vs_flat.rearrange("p (a b) -> p a b", a=NB)
            cT = A1.take([128, 5, S], BF16)
            kr_tm = A1.take([128, NB, 32], F32)
            cos_t = A1.take([128, NB, 16], F32)
            sin_t = A1.take([128, NB, 16], F32)
            kro = A1.take([128, NB, 32], BF16)

            A2 = Arena(arena, R1_END, ARENA_BYTES)
            win = A2.take([128, 8, 2208], BF16)
            gmix = A2.take([128, D], F32)
            gcb = A2.take([128, 640], F32)
            xt = [A2.take([128, D], F32) for _ in range(2)]
            xn = [A2.take([128, D], BF16) for _ in range(2)]
            hT = [A2.take([128, 8, 512], BF16) for _ in range(2)]
            junk = A2.take([128, D], BF16)
            cn = [A2.take([128, 640], BF16) for _ in range(2)]
            posi = A2.take([128, NB], I32)
            posf = A2.take([128, NB], F32)
            ang = A2.take([128, NB, 16], F32)
            rt = [A2.take([128, NB, 16], F32) for _ in range(4)]
            ki = A2.take([128, NB, 16], I32)
            sc = [A2.take([128, 8], F32) for _ in range(2)]

            for kc in range(8):
                dma('pool', ('win', kc), lambda e, kc=kc: e.dma_start(out=win[:, kc, :], in_=win_v[:, kc, :]), writes=[('win', kc)])
            dma('sp', 'gmix', lambda e: e.dma_start(out=gmix, in_=gmix_d[:, :]), writes=['gmix'])
            dma('sp', 'gcb', lambda e: e.dma_start(out=gcb, in_=gc_d[:, :]), writes=['gcb'])
            dma('sp', 'posi', lambda e, b=b: e.dma_start(out=posi, in_=pos_d[b, :, :]), writes=['posi'])

            op('pool', lambda e: e.memset(KTz_flat, 0.0), writes=[('KT', c_) for c_ in range(4)])

            op('dve', lambda e: e.tensor_copy(posf, posi), reads=['posi'], writes=['posf'])
            op('dve', lambda e: e.tensor_tensor(out=ang, in0=invf[:].unsqueeze(1).to_broadcast([128, NB, 16]),
                                                in1=posf.unsqueeze(2).to_broadcast([128, NB, 16]), op=ALU.mult),
               reads=['posf', 'invf'], writes=['ang'])
            for which, shift, dst in ((0, 0.0, sin_t), (1, 0.25, cos_t)):
                t0, t1 = rt[0], rt[1]
                op('dve', lambda e, shift=shift: e.tensor_scalar(out=t0, in0=ang, scalar1=1.0 / TWO_PI, scalar2=shift, op0=ALU.mult, op1=ALU.add),
                   reads=['ang'], writes=['rt0'])
                op('dve', lambda e: e.tensor_copy(ki, t0), reads=['rt0'], writes=['ki'])
                op('dve', lambda e: e.tensor_copy(t0, ki), reads=['ki'], writes=['rt0'])
                op('dve', lambda e: e.scalar_tensor_tensor(out=t1, in0=t0, scalar=-CW1, in1=ang, op0=ALU.mult, op1=ALU.add),
                   reads=['rt0', 'ang'], writes=['rt1'])
                op('dve', lambda e: e.scalar_tensor_tensor(out=t1, in0=t0, scalar=-CW2, in1=t1, op0=ALU.mult, op1=ALU.add),
                   reads=['rt0', 'rt1'], writes=['rt1'])
                if which == 1:
                    op('dve', lambda e: e.tensor_scalar(out=t1, in0=t1, scalar1=math.pi / 2, scalar2=None, op0=ALU.add),
                       reads=['rt1'], writes=['rt1'])
                op('dve', lambda e: e.tensor_scalar(out=t1, in0=t1, scalar1=PI_SAFE, scalar2=-PI_SAFE, op0=ALU.min, op1=ALU.max),
                   reads=['rt1'], writes=['rt1'])
                op('act', lambda e, dst=dst: e.activation(out=dst, in_=t1, func=AF.Sin), reads=['rt1'], writes=[('trig', which)])

            pbi = [0]

            def nextbank(lo=0, hi=4):
                i = lo + pbi[0] % (hi - lo)
                pbi[0] += 1
                return i

            def p1_norm(T, blk):
                tb = T * 4 + blk
                s = tb % 2
                dma('sp', ('xt', s), lambda e, b=b: e.dma_start(out=xt[s], in_=x_d[b, tb * 128:(tb + 1) * 128, :]),
                    writes=[('xt', s)])
                rstd, rkey = rms_rstd([(xt[s], [('xt', s)])], 1.0 / D, junk, 'junk')
                op('dve', lambda e: e.scalar_tensor_tensor(out=xn[s], in0=xt[s], scalar=rstd, in1=gmix, op0=ALU.mult, op1=ALU.mult),
                   reads=[('xt', s), rkey, 'gmix'], writes=[('xn', s)])
                S_.wtag(('xn', s), tb)

            def p1_tr(T, blk):
                tb = T * 4 + blk
                s = tb % 2
                hs = T % 2
                pi = 6 + (tb % 2)
                ptr = banks[pi][:].bitcast(BF16)
                S_.rtag(('xn', s), tb)

                def tr8(e):
                    for kc in range(8):
                        ins = e.transpose(ptr[:, kc * 128:(kc + 1) * 128], xn[s][:, kc * 128:(kc + 1) * 128], ident)
                    return ins
                op('pe', tr8, reads=[('xn', s), 'cbf'], writes=[bk(pi)])
                evac(hT[hs][:, :, blk * 128:(blk + 1) * 128], ptr.rearrange("p (a b) -> p a b", a=8),
                     reads=[bk(pi)], pwrites=[('hT', hs)])

            qkb = [0]

            def p1_qk(T, oc):
                hs = T % 2
                hTs, hkey = hT[hs], ('hT', hs)
                bi = 4 + qkb[0] % 2
                qkb[0] += 1

                def mmqk(e):
                    for kc in range(8):
                        ins = e.matmul(banks[bi][:], lhsT=win[:, kc, oc * 128:(oc + 1) * 128], rhs=hTs[:, kc, :],
                                       start=(kc == 0), stop=(kc == 7))
                    return ins
                op('pe', mmqk, reads=[hkey] + [('win', kc) for kc in range(8)], writes=[bk(bi)])
                if oc < 4:
                    evac(QT[:, oc, T * 512:(T + 1) * 512], banks[bi][:], reads=[bk(bi)], pwrites=[('QT', oc)], scale=SB_SCALE)
                else:
                    c_ = oc - 4
                    evac(KTz[0:64, 2 * c_, T * 512:(T + 1) * 512], banks[bi][0:64, :], reads=[bk(bi)], pwrites=[('KT', c_)])
                    evac(KTz[64:128, 2 * c_ + 1, T * 512:(T + 1) * 512], banks[bi][64:128, :], reads=[bk(bi)], pwrites=[('KT', c_)])

            def p1_v(T, blk):
                tb = T * 4 + blk
                hs = T % 2
                hTs, hkey = hT[hs], ('hT', hs)
                bi = 4 + qkb[0] % 2
                qkb[0] += 1

                def mmv(e):
                    for kc in range(8):
                        ins = e.matmul(banks[bi][:], lhsT=hTs[:, kc, blk * 128:(blk + 1) * 128], rhs=win[:, kc, 1024:1536],
                                       start=(kc == 0), stop=(kc == 7))
                    return ins
                op('pe', mmv, reads=[hkey] + [('win', kc) for kc in range(8)], writes=[bk(bi)])
                evac(Vs[:, tb, :], banks[bi][:], reads=[bk(bi)], writes=[('V', tb)])

            def p1_c_mm(T, blk):
                tb = T * 4 + blk
                hs = T % 2
                hTs, hkey = hT[hs], ('hT', hs)
                ba, bb = (0, 1) if blk % 2 == 0 else (2, 3)

                def mmc(e):
                    for kc in range(8):
                        e.matmul(banks[ba][:, 0:416], lhsT=hTs[:, kc, blk * 128:(blk + 1) * 128], rhs=win[:, kc, 1536:1952],
                                 start=(kc == 0), stop=(kc == 7))
                    for kc in range(8):
                        ins = e.matmul(banks[bb][:, 0:256], lhsT=hTs[:, kc, blk * 128:(blk + 1) * 128], rhs=win[:, kc, 1952:2208],
                                       start=(kc == 0), stop=(kc == 7))
                    return ins
                op('pe', mmc, reads=[hkey] + [('win', kc) for kc in range(8)], writes=[bk(ba), bk(bb)])
                cs = tb % 2
                sck = ('sc', cs)
                scs = sc[cs]
                op('act', lambda e: e.activation(out=junk[:, 0:384], in_=banks[ba][:, 0:384], func=AF.Square, accum_out=scs[:, 0:1]),
                   reads=[bk(ba)], writes=['junk', sck])
                op('act', lambda e: e.activation(out=junk[:, 0:256], in_=banks[bb][:, 0:256], func=AF.Square, accum_out=scs[:, 1:2]),
                   reads=[bk(bb)], writes=['junk'], pwrites=[sck])
                op('act', lambda e: e.activation(out=kr_tm[:, tb, :], in_=banks[ba][:, 384:416], func=AF.Copy),
                   reads=[bk(ba)], pwrites=['kr_tm'])
                op('dve', lambda e: e.tensor_tensor(out=scs[:, 2:4], in0=scs[:, 0:2], in1=cscale[:, 0:2], op=ALU.mult),
                   reads=[sck, 'cscale'], writes=[sck])
                op('act', lambda e: e.activation(out=scs[:, 4:6], in_=scs[:, 2:4], func=AF.Sqrt, bias=EPS), reads=[sck], writes=[sck])
                op('dve', lambda e: e.reciprocal(scs[:, 6:8], scs[:, 4:6]), reads=[sck], writes=[sck])
                op('dve', lambda e: e.scalar_tensor_tensor(out=cn[cs][:, 0:384], in0=banks[ba][:, 0:384], scalar=scs[:, 6:7],
                                                           in1=gcb[:, 0:384], op0=ALU.mult, op1=ALU.mult),
                   reads=[bk(ba), sck, 'gcb'], writes=[('cn', cs)])
                op('dve', lambda e: e.scalar_tensor_tensor(out=cn[cs][:, 384:640], in0=banks[bb][:, 0:256], scalar=scs[:, 7:8],
                                                           in1=gcb[:, 384:640], op0=ALU.mult, op1=ALU.mult),
                   reads=[bk(bb), sck, 'gcb'], pwrites=[('cn', cs)])
                S_.wtag(('cn', cs), tb)

            def p1_c_tr(T, blk):
                tb = T * 4 + blk
                cs = tb % 2
                pi = 6 + (tb % 2)
                ptr = banks[pi][:].bitcast(BF16)
                S_.rtag(('cn', cs), tb)

                def tr5(e):
                    for i in range(5):
                        ins = e.transpose(ptr[:, i * 128:(i + 1) * 128], cn[cs][:, i * 128:(i + 1) * 128], ident)
                    return ins
                op('pe', tr5, reads=[('cn', cs), 'cbf'], writes=[bk(pi)])
                evac(cT[:, :, tb * 128:(tb + 1) * 128], ptr[:, 0:640].rearrange("p (a b) -> p a b", a=5),
                     reads=[bk(pi)], writes=[('cT', tb)])

            p1_norm(0, 0)
            for blk in range(4):
                if blk + 1 < 4:
                    p1_norm(0, blk + 1)
                p1_tr(0, blk)
            for T in range(NT):
                nxt = (T + 1 < NT)
                p1_c_mm(T, 0)
                p1_c_mm(T, 1)
                if nxt:
                    p1_norm(T + 1, 0)
                for oc in range(0, 4):
                    p1_qk(T, oc)
                p1_c_tr(T, 0)
                p1_c_tr(T, 1)
                p1_c_mm(T, 2)
                p1_c_mm(T, 3)
                if nxt:
                    p1_tr(T + 1, 0)
                    p1_norm(T + 1, 1)
                for oc in range(4, 8):
                    p1_qk(T, oc)
                p1_c_tr(T, 2)
                p1_c_tr(T, 3)
                if nxt:
                    p1_tr(T + 1, 1)
                    p1_norm(T + 1, 2)
                for blk in range(4):
                    p1_v(T, blk)
                    if nxt and blk == 1:
                        p1_tr(T + 1, 2)
                        p1_norm(T + 1, 3)
                if nxt:
                    p1_tr(T + 1, 3)
            x1v, x2v = kr_tm[:, :, 0:16], kr_tm[:, :, 16:32]
            op('dve', lambda e: e.tensor_tensor(out=rt[0], in0=x1v, in1=cos_t, op=ALU.mult), reads=['kr_tm', ('trig', 1)], writes=['rt0'])
            op('dve', lambda e: e.tensor_tensor(out=rt[1], in0=x2v, in1=sin_t, op=ALU.mult), reads=['kr_tm', ('trig', 0)], writes=['rt1'])
            op('dve', lambda e: e.tensor_tensor(out=kro[:, :, 0:16], in0=rt[0], in1=rt[1], op=ALU.subtract), reads=['rt0', 'rt1'], pwrites=['kro'])
            op('dve', lambda e: e.tensor_tensor(out=rt[2], in0=x2v, in1=cos_t, op=ALU.mult), reads=['kr_tm', ('trig', 1)], writes=['rt2'])
            op('dve', lambda e: e.tensor_tensor(out=rt[3], in0=x1v, in1=sin_t, op=ALU.mult), reads=['kr_tm', ('trig', 0)], writes=['rt3'])
            op('dve', lambda e: e.tensor_tensor(out=kro[:, :, 16:32], in0=rt[2], in1=rt[3], op=ALU.add), reads=['rt2', 'rt3'], pwrites=['kro'])

            S_.barrier()
            if PHASE_LIMIT < 2:
                S_.enabled = False

            A3 = Arena(arena, R1_END, ARENA_BYTES)
            e_sb = [A3.take([128, 512], F32) for _ in range(3)]
            sp_sb = [A3.take([128, 512], BF16) for _ in range(3)]
            a_sb = [A3.take([128, 512], BF16) for _ in range(3)]
            LKSb = [A3.take([128, S], BF16) for _ in range(2)]
            wuq = A3.take([128, 3, 768], BF16)
            wukv = A3.take([128, 2, 1024], BF16)
            qtm = [A3.take([128, 4, 128], BF16) for _ in range(2)]
            ktm = [A3.take([128, 4, 128], BF16) for _ in range(2)]
            rq = [A3.take([128, 4, 16], F32) for _ in range(4)]
            rec = [A3.take([128, 8], F32) for _ in range(4)]
            mpc = [0]
            qr = A3.take([128, 4, 32], F32)

            for kc in range(3):
                dma('pool', ('wuq', kc), lambda e, kc=kc: e.dma_start(out=wuq[:, kc, :], in_=wuq_v[:, kc, :]), writes=[('wuq', kc)])
            for kc in range(2):
                dma('pool', ('wukv', kc), lambda e, kc=kc: e.dma_start(out=wukv[:, kc, :], in_=wukv_v[:, kc, :]), writes=[('wukv', kc)])

            def pieces():
                lst = []
                for kb in range(NB - 1, -1, -1):
                    for c in range(NT):
                        lo = max(512 * c, kb * 128)
                        hi = 512 * c + 512
                        if hi > lo:
                            lst.append((kb, c, lo, hi))
                return lst

            QPB = 8
            n_ob = (NB + QPB - 1) // QPB
            assert n_ob <= 2
            pcs = pieces()
            sb_items = []
            for h in range(8):
                for idx, (kb, c, lo, hi) in enumerate(pcs):
                    sb_items.append(dict(h=h, kb=kb, c=c, lo=lo, hi=hi, p=len(sb_items), first=(idx == 0), last=(idx == len(pcs) - 1)))

            def sb_zmm(e, bank_ap, h, kb, lo, hi, start, stop):
                return e.matmul(bank_ap, lhsT=KTz[:, h, kb * 128:(kb + 1) * 128], rhs=QT[:, h // 2, lo:hi], start=start, stop=stop)

            def sb_s0(it):
                h, kb, c, lo, hi, p = it['h'], it['kb'], it['c'], it['lo'], it['hi'], it['p']
                n = hi - lo
                hp = h % 2
                ch = h // 2
                if it['first']:
                    op('pool', lambda e: e.memset(LKSb[hp], 0.0), writes=[('LKSb', hp, cc) for cc in range(NT)])
                zb = p % 2
                op('pe', lambda e: sb_zmm(e, banks[zb][:, 0:n], h, kb, lo, hi, True, True),
                   reads=[('QT', ch), ('KT', ch)], writes=[bk(zb)])
                S_.wtag(bk(zb), ('z', p))

            def sb_s1(it):
                kb, lo, hi, p = it['kb'], it['lo'], it['hi'], it['p']
                n = hi - lo
                zb, es_ = p % 2, p % 3
                S_.rtag(bk(zb), ('z', p))
                op('act', lambda e: e.activation(out=e_sb[es_][:, 0:n], in_=banks[zb][:, 0:n], func=AF.Exp),
                   reads=[bk(zb)], writes=[('e', es_)])
                op('act', lambda e: e.activation(out=sp_sb[es_][:, 0:n], in_=e_sb[es_][:, 0:n], func=AF.Ln, bias=1.0),
                   reads=[('e', es_)], writes=[('sp', es_)])
                if lo == kb * 128:
                    op('pool', lambda e: e.tensor_tensor(out=sp_sb[es_][:, 0:128], in0=sp_sb[es_][:, 0:128], in1=mstrict, op=ALU.mult),
                       reads=[('sp', es_), 'cbf'], writes=[('sp', es_)])
                S_.wtag(('sp', es_), p)

            def sb_s2(it):
                h, kb, c, lo, hi, p = it['h'], it['kb'], it['c'], it['lo'], it['hi'], it['p']
                n = hi - lo
                hp = h % 2
                ch = h // 2
                cbk, ss_ = 2 + p % 2, p % 3
                diag = (lo == kb * 128)
                has_lks = (n > 128) or (not diag)
                S_.rtag(('sp', ss_), p)

                def mmc2(e):
                    e.matmul(banks[cbk][:, 0:n], lhsT=ntri, rhs=sp_sb[ss_][:, 0:n], start=True, stop=False)
                    if has_lks:
                        e.matmul(banks[cbk][:, 0:n], lhsT=nones, rhs=LKSb[hp][:, lo:hi], start=False, stop=False)
                    return sb_zmm(e, banks[cbk][:, 0:n], h, kb, lo, hi, False, True)
                op('pe', mmc2, reads=[('sp', ss_), 'cbf', ('QT', ch), ('KT', ch)] + ([('LKSb', hp, c)] if has_lks else []), writes=[bk(cbk)])
                S_.wtag(bk(cbk), ('c', p))
                if kb > 0:
                    op('dve', lambda e: e.tensor_tensor(out=LKSb[hp][:, lo:hi], in0=LKSb[hp][:, lo:hi], in1=sp_sb[ss_][:, 0:n], op=ALU.add),
                       reads=[('sp', ss_)], writes=[('LKSb', hp, c)])

            def sb_s3(it):
                kb, lo, hi, p = it['kb'], it['lo'], it['hi'], it['p']
                n = hi - lo
                cbk, as_ = 2 + p % 2, p % 3
                S_.rtag(bk(cbk), ('c', p))
                op('act', lambda e: e.activation(out=a_sb[as_][:, 0:n], in_=banks[cbk][:, 0:n], func=AF.Exp),
                   reads=[bk(cbk)], writes=[('a', as_)])
                if lo == kb * 128:
                    op('pool', lambda e: e.tensor_tensor(out=a_sb[as_][:, 0:128], in0=a_sb[as_][:, 0:128], in1=mstrict, op=ALU.mult),
                       reads=[('a', as_), 'cbf'], writes=[('a', as_)])
                S_.wtag(('a', as_), p)

            def sb_s4(it):
                h, kb, lo, hi, p = it['h'], it['kb'], it['lo'], it['hi'], it['p']
                as_ = p % 3
                ob0 = 4 if h % 2 == 0 else 6
                obanks = [ob0 + i for i in range(n_ob)]
                S_.rtag(('a', as_), p)
                if it['first']:
                    def zero_o(e):
                        for bi in obanks:
                            ins = e.matmul(banks[bi][:], lhsT=zer[:, 0:128], rhs=zer[:, :], start=True, stop=False, skip_group_check=True)
                        return ins
                    op('pe', zero_o, reads=['zer'], writes=[bk(bi) for bi in obanks])

                def mmav(e):
                    for qb in range(lo // 128, hi // 128):
                        bi = ob0 + qb // QPB
                        sl = (qb % QPB) * 64
                        last = (kb == 0) and (qb % QPB == QPB - 1 or qb == NB - 1)
                        ins = e.matmul(banks[bi][:, sl:sl + 64], lhsT=a_sb[as_][:, qb * 128 - lo:qb * 128 - lo + 128],
                                       rhs=Vs[:, kb, h * 64:(h + 1) * 64], start=False, stop=last, skip_group_check=True)
                    return ins
                obs = sorted(set(ob0 + qb // QPB for qb in range(lo // 128, hi // 128)))
                op('pe', mmav, reads=[('a', as_), ('V', kb)], pwrites=[bk(bi) for bi in obs])
                if it['last']:
                    for i, bi in enumerate(obanks):
                        nq = min(QPB, NB - i * QPB)
                        evac(o_tm[:, i * QPB:i * QPB + nq, h * 64:(h + 1) * 64],
                             banks[bi][:, 0:nq * 64].rearrange("p (a b) -> p a b", a=nq),
                             reads=[bk(bi)], pwrites=[('o', i)])

            run_pipeline(sb_items, [(0, sb_s0), (1, sb_s1), (2, sb_s2), (3, sb_s3), (4, sb_s4)])

            S_.barrier()
            if PHASE_LIMIT < 3:
                S_.enabled = False

            if OPLIM is not None:
                S_.oplimit = OPLIM
            QTm, KTm = QT, KT
            Vm = Vs[:, :, 0:260].rearrange("p a (h d) -> p a h d", h=4)
            MQ = 7
            n_mb = (NB + MQ - 1) // MQ
            for g in range(2):
                op('pool', lambda e: e.memset(Vs_flat, 1.0), writes=[('V', tb) for tb in range(NB)])
                if g == 0:
                    for qs_ in range(2):
                        op('pool', lambda e, qs_=qs_: e.memset(qtm[qs_].rearrange("p a b -> p (a b)"), 0.0), writes=[('qtm', qs_)])
                        op('pool', lambda e, qs_=qs_: e.memset(ktm[qs_].rearrange("p a b -> p (a b)"), 0.0), writes=[('ktm', qs_)])
                def mp_banks(tb):
                    return (0, 1) if tb % 2 == 0 else (2, 3)

                def mp_A(tb, g=g):
                    bq, bkv = mp_banks(tb)

                    def mmq(e):
                        for kc in range(3):
                            ins = e.matmul(banks[bq][:, 0:384], lhsT=cT[:, kc, tb * 128:(tb + 1) * 128], rhs=wuq[:, kc, g * 384:(g + 1) * 384],
                                           start=(kc == 0), stop=(kc == 2))
                        return ins
                    op('pe', mmq, reads=[('cT', tb)] + [('wuq', kc) for kc in range(3)], writes=[bk(bq)])

                    def mmkv(e):
                        for kc in range(2):
                            ins = e.matmul(banks[bkv][:, :], lhsT=cT[:, 3 + kc, tb * 128:(tb + 1) * 128], rhs=wukv[:, kc, g * 512:(g + 1) * 512],
                                           start=(kc == 0), stop=(kc == 1))
                        return ins
                    op('pe', mmkv, reads=[('cT', tb)] + [('wukv', kc) for kc in range(2)], writes=[bk(bkv)])
                    S_.wtag(bk(bq), ('mq', g, tb))

                def mp_B(tb, g=g):
                    qs = tb % 2
                    bq, bkv = mp_banks(tb)
                    S_.rtag(bk(bq), ('mq', g, tb))
                    qv = banks[bq][:, 0:384].rearrange("p (h d) -> p h d", h=4)
                    kvv = banks[bkv][:, :].rearrange("p (h d) -> p h d", h=4)
                    cosb = cos_t[:, tb, :].unsqueeze(1).to_broadcast([128, 4, 16])
                    sinb = sin_t[:, tb, :].unsqueeze(1).to_broadcast([128, 4, 16])
                    qk_ = ('qtm', qs)
                    kk_ = ('ktm', qs)
                    op('act', lambda e: e.activation(out=qr, in_=qv[:, :, 64:96], func=AF.Copy), reads=[bk(bq)], writes=['qr'])
                    op('act', lambda e: e.activation(out=qtm[qs][:, :, 0:64], in_=qv[:, :, 0:64], func=AF.Copy),
                       reads=[bk(bq)], pwrites=[qk_])
                    op('act', lambda e: e.activation(out=ktm[qs][:, :, 0:64], in_=kvv[:, :, 0:64], func=AF.Copy),
                       reads=[bk(bkv)], pwrites=[kk_])
                    op('dve', lambda e: e.tensor_tensor(out=rq[0], in0=qr[:, :, 0:16], in1=cosb, op=ALU.mult), reads=['qr', ('trig', 1)], writes=['rq0'])
                    op('dve', lambda e: e.tensor_tensor(out=rq[1], in0=qr[:, :, 16:32], in1=sinb, op=ALU.mult), reads=['qr', ('trig', 0)], writes=['rq1'])
                    op('dve', lambda e: e.tensor_tensor(out=qtm[qs][:, :, 64:80], in0=rq[0], in1=rq[1], op=ALU.subtract), reads=['rq0', 'rq1'], pwrites=[qk_])
                    op('dve', lambda e: e.tensor_tensor(out=rq[2], in0=qr[:, :, 16:32], in1=cosb, op=ALU.mult), reads=['qr', ('trig', 1)], writes=['rq2'])
                    op('dve', lambda e: e.tensor_tensor(out=rq[3], in0=qr[:, :, 0:16], in1=sinb, op=ALU.mult), reads=['qr', ('trig', 0)], writes=['rq3'])
                    op('dve', lambda e: e.tensor_tensor(out=qtm[qs][:, :, 80:96], in0=rq[2], in1=rq[3], op=ALU.add), reads=['rq2', 'rq3'], pwrites=[qk_])
                    for hl_ in range(4):
                        op('pool', lambda e, hl_=hl_: e.tensor_copy(ktm[qs][:, hl_, 64:96], kro[:, tb, :]),
                           reads=['kro'], pwrites=[kk_])
                    op('dve', lambda e: e.tensor_copy(Vm[:, tb, :, 0:64], kvv[:, :, 64:128]), reads=[bk(bkv)], pwrites=[('V', tb)])
                    S_.wtag(qk_, (g, tb))

                def mp_C(tb, g=g):
                    qs = tb % 2
                    qk_ = ('qtm', qs)
                    kk_ = ('ktm', qs)
                    S_.rtag(qk_, (g, tb))
                    pi = 6 + (tb % 2)
                    ptr = banks[pi][:].bitcast(BF16)

                    def trqk(e):
                        for hl in range(4):
                            e.transpose(ptr[:, hl * 128:(hl + 1) * 128], qtm[qs][:, hl, :], ident)
                        for hl in range(4):
                            ins = e.transpose(ptr[:, (4 + hl) * 128:(5 + hl) * 128], ktm[qs][:, hl, :], ident)
                        return ins
                    op('pe', trqk, reads=[qk_, kk_, 'cbf'], writes=[bk(pi)])
                    evac(QTm[:, :, tb * 128:(tb + 1) * 128], ptr[:, 0:512].rearrange("p (a b) -> p a b", a=4),
                         reads=[bk(pi)], pwrites=[('QT', hl) for hl in range(4)])
                    evac(KTm[:, :, tb * 128:(tb + 1) * 128], ptr[:, 512:1024].rearrange("p (a b) -> p a b", a=4),
                         reads=[bk(pi)], pwrites=[('KT', hl) for hl in range(4)])

                run_pipeline(list(range(NB)), [(0, mp_A), (1, mp_B), (2, mp_C)])
                if P3_MODE == 'proj':
                    S_.enabled = False
                m_items = []
                for hl in range(4):
                    for idx, (kb, c, lo, hi) in enumerate(pcs):
                        m_items.append(dict(hl=hl, kb=kb, c=c, lo=lo, hi=hi, idx=len(m_items), first=(idx == 0), last=(idx == len(pcs) - 1)))

                def ml_s0(it, g=g):
                    hl, kb, lo, hi = it['hl'], it['kb'], it['lo'], it['hi']
                    n = hi - lo
                    p = mpc[0] + it['idx']
                    zb = (0, 1, 5, 6)[p % 4]
                    op('pe', lambda e: e.matmul(banks[zb][:, 0:n], lhsT=KTm[:, hl, kb * 128:(kb + 1) * 128], rhs=QTm[:, hl, lo:hi], start=True, stop=True),
                       reads=[('QT', hl), ('KT', hl)], writes=[bk(zb)])
                    S_.wtag(bk(zb), ('zm', p))

                def ml_s1(it, g=g):
                    kb, lo, hi = it['kb'], it['lo'], it['hi']
                    n = hi - lo
                    p = mpc[0] + it['idx']
                    zb, as_ = (0, 1, 5, 6)[p % 4], p % 3
                    S_.rtag(bk(zb), ('zm', p))
                    op('act', lambda e: e.activation(out=a_sb[as_][:, 0:n], in_=banks[zb][:, 0:n], func=AF.Exp, scale=MLA_SCALE),
                       reads=[bk(zb)], writes=[('a', as_)])
                    if lo == kb * 128:
                        op('pool', lambda e: e.tensor_tensor(out=a_sb[as_][:, 0:128], in0=a_sb[as_][:, 0:128], in1=mincl, op=ALU.mult),
                           reads=[('a', as_), 'cbf'], writes=[('a', as_)])
                    S_.wtag(('a', as_), ('m', p))

                def ml_s2(it, g=g):
                    hl, kb, lo, hi = it['hl'], it['kb'], it['lo'], it['hi']
                    h = g * 4 + hl
                    p = mpc[0] + it['idx']
                    as_ = p % 3
                    ob0 = 2
                    obanks = [ob0 + i for i in range(n_mb)]
                    S_.rtag(('a', as_), ('m', p))
                    if it['first']:
                        def zero_m(e):
                            for bi in obanks:
                                ins = e.matmul(banks[bi][:], lhsT=zer[:, 0:128], rhs=zer[:, :], start=True, stop=False, skip_group_check=True)
                            return ins
                        op('pe', zero_m, reads=['zer'], writes=[bk(bi) for bi in obanks])

                    def mmavm(e):
                        for qb in range(lo // 128, hi // 128):
                            bi = ob0 + qb // MQ
                            sl = (qb % MQ) * 65
                            last = (kb == 0) and (qb % MQ == MQ - 1 or qb == NB - 1)
                            ins = e.matmul(banks[bi][:, sl:sl + 65], lhsT=a_sb[as_][:, qb * 128 - lo:qb * 128 - lo + 128],
                                           rhs=Vm[:, kb, hl, :], start=False, stop=last, skip_group_check=True)
                        return ins
                    obs = sorted(set(ob0 + qb // MQ for qb in range(lo // 128, hi // 128)))
                    op('pe', mmavm, reads=[('a', as_), ('V', kb)], pwrites=[bk(bi) for bi in obs])
                    if it['last']:
                        for i, bi in enumerate(obanks):
                            nq = min(MQ, NB - i * MQ)
                            ov = banks[bi][:, 0:nq * 65].rearrange("p (a b) -> p a b", a=nq)
                            rs = rec[(h * 4 + i) % 4]
                            rk = ('rec', (h * 4 + i) % 4)
                            op('dve', lambda e, ov=ov, nq=nq, rs=rs: e.reciprocal(rs[:, 0:nq], ov[:, :, 64]), reads=[bk(bi)], writes=[rk])
                            qbs = range(i * MQ, i * MQ + nq)
                            okeys = sorted(set(('o', qb // QPB) for qb in qbs))
                            op('dve', lambda e, ov=ov, nq=nq, i=i, h=h, rs=rs: e.tensor_tensor(
                                out=o_tm[:, i * MQ:i * MQ + nq, 512 + h * 64:512 + (h + 1) * 64], in0=ov[:, :, 0:64],
                                in1=rs[:, 0:nq].unsqueeze(2).to_broadcast([128, nq, 64]), op=ALU.mult),
                               reads=[bk(bi), rk], pwrites=okeys)

                run_pipeline(m_items, [(0, ml_s0), (2, ml_s1), (3, ml_s2)])
                mpc[0] += len(m_items)

            S_.barrier()
            if PHASE_LIMIT < 4:
                S_.enabled = False

            A4 = Arena(arena, 0, ARENA_BYTES)
            if DEBUG:
                dma('sp', 'dbg', lambda e, b=b: e.dma_start(out=dbg_d[b], in_=o_tm[:]), reads=[('o', i) for i in range(n_ob)], final=True)
            wout = A4.take([128, 8, D], BF16)
            wdn = A4.take([128, NJ, D], BF16)
            gob = A4.take([128, D], F32)
            gffn = A4.take([128, D], F32)
            gfin = A4.take([128, D], F32)
            xn4a = [A4.take([128, D], BF16) for _ in range(2)]
            xn4b = [A4.take([128, D], BF16) for _ in range(2)]
            junk4 = A4.take([128, D], BF16)
            oT = [A4.take([128, 8, 128], BF16) for _ in range(2)]
            h2T = A4.take([128, 8, 512], BF16)
            x1b = [A4.take([128, 4, D], F32) for _ in range(2)]
            gT = A4.take([128, NJ, 512], BF16)
            wup = [A4.take([128, 8, 256], BF16) for _ in range(3)]
            cg = [A4.take([128, 512], F32) for _ in range(2)]
            cv = [A4.take([128, 512], F32) for _ in range(2)]
            sg = [A4.take([128, 512], BF16) for _ in range(2)]
            halo = [A4.take([128, 2 * NJ, 2], F32) for _ in range(2)]
            so = [A4.take([128, 8], F32) for _ in range(4)]
            HC = A4.take([128, 2 * NJ, 2], F32)
            hct = A4.take([128, 2 * NJ], F32)
            cw3 = cw[:].rearrange("p (c k) -> p c k", k=4)

            for kc in range(8):
                dma('pool', ('wout', kc), lambda e, kc=kc: e.dma_start(out=wout[:, kc, :], in_=wout_v[:, kc, :]), writes=[('wout', kc)])
            dma('sp', 'gob', lambda e: e.dma_start(out=gob, in_=go_d[:, :]), writes=['gob'])
            dma('sp', 'gffn', lambda e: e.dma_start(out=gffn, in_=gffn_d[:, :]), writes=['gffn'])
            dma('sp', 'gfin', lambda e: e.dma_start(out=gfin, in_=gfin_d[:, :]), writes=['gfin'])
            wu_ptr = [0]

            def ffn_dma_upto(g_hi):
                while wu_ptr[0] <= g_hi and wu_ptr[0] < NT * NJ:
                    g_ = wu_ptr[0]
                    wu_ptr[0] += 1
                    ws, j = g_ % 3, g_ % NJ
                    dma('pool', ('wup', ws), lambda e, ws=ws, j=j: e.dma_start(out=wup[ws], in_=wup_v[:, :, j * 256:(j + 1) * 256]), writes=[('wup', ws)])
                    S_.wtag(('wup', ws), g_)

            def ffn_mmu(T, j):
                g_ = T * NJ + j
                ws = g_ % 3
                bg, bv = ((0, 1), (2, 3), (4, 5))[j % 3]
                S_.rtag(('wup', ws), g_)

                def mmu(e):
                    for kc in range(8):
                        e.matmul(banks[bg][:], lhsT=wup[ws][:, kc, 0:128], rhs=h2T[:, kc, :], start=(kc == 0), stop=(kc == 7))
                    for kc in range(8):
                        ins = e.matmul(banks[bv][:], lhsT=wup[ws][:, kc, 128:256], rhs=h2T[:, kc, :], start=(kc == 0), stop=(kc == 7))
                    return ins
                op('pe', mmu, reads=[('wup', ws), 'h2T'], writes=[bk(bg), bk(bv)])
                S_.wtag(bk(bg), ('u', g_))

            def ffn_conv(T, j):
                g_ = T * NJ + j
                bg, bv = ((0, 1), (2, 3), (4, 5))[j % 3]
                S_.rtag(bk(bg), ('u', g_))
                us = j % 2
                hw, hr = halo[T % 2], halo[(T + 1) % 2]
                halves = ((0, bg, cg[us], ('cg', us)), (1, bv, cv[us], ('cv', us)))
                for (half, bi, dst, dkey) in halves:
                    cj = j * 2 + half
                    w2 = cw[:, cj * 4 + 2:cj * 4 + 3]
                    bb_ = cw[:, cj * 4 + 3:cj * 4 + 4]
                    op('act', lambda e, bi=bi, dst=dst, w2=w2, bb_=bb_: e.activation(out=dst, in_=banks[bi][:], func=AF.Identity, scale=w2, bias=bb_),
                       reads=[bk(bi), 'cw'], writes=[dkey])
                    if T < NT - 1:
                        op('act', lambda e, bi=bi, cj=cj: e.activation(out=hw[:, cj, :], in_=banks[bi][:, 510:512], func=AF.Copy),
                           reads=[bk(bi)], writes=[('halo', T % 2, cj)])
                for (half, bi, dst, dkey) in halves:
                    cj = j * 2 + half
                    w0 = cw[:, cj * 4 + 0:cj * 4 + 1]
                    w1 = cw[:, cj * 4 + 1:cj * 4 + 2]
                    hk = ('halo', (T + 1) % 2, cj)
                    op('dve', lambda e, bi=bi, dst=dst, w1=w1: e.scalar_tensor_tensor(out=dst[:, 1:512], in0=banks[bi][:, 0:511], scalar=w1, in1=dst[:, 1:512],
                                                                                   op0=ALU.mult, op1=ALU.add),
                       reads=[bk(bi), dkey, 'cw'], writes=[dkey])
                    op('dve', lambda e, bi=bi, dst=dst, w0=w0: e.scalar_tensor_tensor(out=dst[:, 2:512], in0=banks[bi][:, 0:510], scalar=w0, in1=dst[:, 2:512],
                                                                                   op0=ALU.mult, op1=ALU.add),
                       reads=[bk(bi), dkey, 'cw'], writes=[dkey])
                    if T > 0:
                        op('pool', lambda e, dst=dst, cj=cj: e.tensor_tensor(out=dst[:, 0:2], in0=dst[:, 0:2], in1=HC[:, cj, :], op=ALU.add),
                           reads=['HC', dkey], writes=[dkey])
                    if half == 0:
                        op('act', lambda e: e.activation(out=sg[us], in_=cg[us], func=AF.Silu), reads=[('cg', us)], writes=[('sg', us)])
                op('pool', lambda e: e.tensor_tensor(out=gT[:, j, :], in0=sg[us], in1=cv[us], op=ALU.mult),
                   reads=[('sg', us), ('cv', us)], writes=[('gT', j)])

            def pro_A(T, i):
                tb = T * 4 + i
                par, s = T % 2, i % 2
                x1 = x1b[par]
                okey = ('o', tb // QPB)
                dma('sp', ('xin', par, i), lambda e, b=b: e.dma_start(out=x1[:, i, :], in_=x_d[b, tb * 128:(tb + 1) * 128, :]),
                    writes=[('x1', par, i)])
                sos = so[i]
                sok = ('so', i)
                op('act', lambda e: e.activation(out=junk4[:, 0:512], in_=o_tm[:, tb, 0:512], func=AF.Square, accum_out=sos[:, 0:1]),
                   reads=[okey], writes=['junk4', sok])
                op('act', lambda e: e.activation(out=junk4[:, 512:1024], in_=o_tm[:, tb, 512:1024], func=AF.Square, accum_out=sos[:, 1:2]),
                   reads=[okey], pwrites=['junk4', sok])
                op('act', lambda e: e.activation(out=sos[:, 2:4], in_=sos[:, 0:2], func=AF.Sqrt, scale=1.0 / 512.0, bias=EPS), reads=[sok], writes=[sok])
                op('dve', lambda e: e.reciprocal(sos[:, 4:6], sos[:, 2:4]), reads=[sok], writes=[sok])
                for half in range(2):
                    op('dve', lambda e, half=half: e.scalar_tensor_tensor(
                        out=xn4a[s][:, half * 512:(half + 1) * 512], in0=o_tm[:, tb, half * 512:(half + 1) * 512], scalar=sos[:, 4 + half:5 + half],
                        in1=gob[:, half * 512:(half + 1) * 512], op0=ALU.mult, op1=ALU.mult),
                       reads=[okey, sok, 'gob'], **({'writes': [('xn4a', s)]} if half == 0 else {'pwrites': [('xn4a', s)]}))
                S_.wtag(('xn4a', s), tb)

            def pro_B(T, i):
                tb = T * 4 + i
                s = i % 2
                pi = 6 + s
                ptr = banks[pi][:].bitcast(BF16)
                S_.rtag(('xn4a', s), tb)

                def tro(e):
                    for kc in range(8):
                        ins = e.transpose(ptr[:, kc * 128:(kc + 1) * 128], xn4a[s][:, kc * 128:(kc + 1) * 128], ident)
                    return ins
                op('pe', tro, reads=[('xn4a', s), 'cbf'], writes=[bk(pi)])
                evac(oT[s], ptr.rearrange("p (a b) -> p a b", a=8), reads=[bk(pi)], writes=[('oT', s)])
                S_.wtag(('oT', s), tb)

            def pro_C(T, i):
                tb = T * 4 + i
                par, s = T % 2, i % 2
                x1 = x1b[par]
                S_.rtag(('oT', s), tb)
                for half in range(2):
                    bi = 4 + half

                    def mmo(e, half=half, bi=bi):
                        for kc in range(8):
                            ins = e.matmul(banks[bi][:], lhsT=oT[s][:, kc, :], rhs=wout[:, kc, half * 512:(half + 1) * 512],
                                           start=(kc == 0), stop=(kc == 7))
                        return ins
                    op('pe', mmo, reads=[('oT', s)] + [('wout', kc) for kc in range(8)], writes=[bk(bi)])
                    op('dve', lambda e, bi=bi, half=half: e.tensor_tensor(
                        out=x1[:, i, half * 512:(half + 1) * 512], in0=banks[bi][:], in1=x1[:, i, half * 512:(half + 1) * 512], op=ALU.add),
                       reads=[bk(bi), ('x1', par, i)], pwrites=[('x1', par, i)])

            def pro_D(T, i):
                tb = T * 4 + i
                par, s = T % 2, i % 2
                x1 = x1b[par]
                rstd, rkey = rms_rstd([(x1[:, i, :], [('x1', par, i)])], 1.0 / D, junk4, 'junk4')
                op('dve', lambda e: e.scalar_tensor_tensor(out=xn4b[s], in0=x1[:, i, :], scalar=rstd, in1=gffn, op0=ALU.mult, op1=ALU.mult),
                   reads=[('x1', par, i), rkey, 'gffn'], writes=[('xn4b', s)])
                S_.wtag(('xn4b', s), tb)

            def pro_E(T, i):
                tb = T * 4 + i
                s = i % 2
                pi = 6 + s
                ptr = banks[pi][:].bitcast(BF16)
                S_.rtag(('xn4b', s), tb)

                def tro(e):
                    for kc in range(8):
                        ins = e.transpose(ptr[:, kc * 128:(kc + 1) * 128], xn4b[s][:, kc * 128:(kc + 1) * 128], ident)
                    return ins
                op('pe', tro, reads=[('xn4b', s), 'cbf'], writes=[bk(pi)])
                evac(h2T[:, :, i * 128:(i + 1) * 128], ptr.rearrange("p (a b) -> p a b", a=8), reads=[bk(pi)], pwrites=['h2T'])

            def epi_A(T, i):
                par = T % 2
                x1 = x1b[par]
                for half in range(2):
                    bi = (i % 2) * 2 + half

                    def mmd(e, half=half, bi=bi):
                        for j in range(NJ):
                            ins = e.matmul(banks[bi][:], lhsT=gT[:, j, i * 128:(i + 1) * 128], rhs=wdn[:, j, half * 512:(half + 1) * 512],
                                           start=(j == 0), stop=(j == NJ - 1))
                        return ins
                    op('pe', mmd, reads=[('gT', j) for j in range(NJ)] + [('wdn', j) for j in range(NJ)], writes=[bk(bi)])
                    op('dve', lambda e, bi=bi, half=half: e.tensor_tensor(
                        out=x1[:, i, half * 512:(half + 1) * 512], in0=banks[bi][:], in1=x1[:, i, half * 512:(half + 1) * 512], op=ALU.add),
                       reads=[bk(bi), ('x1', par, i)], pwrites=[('x1', par, i)])

            def epi_B(T, i):
                tb = T * 4 + i
                par = T % 2
                x1 = x1b[par]
                rstd, rkey = rms_rstd([(x1[:, i, :], [('x1', par, i)])], 1.0 / D, junk4, 'junk4')
                op('dve', lambda e: e.scalar_tensor_tensor(out=x1[:, i, :], in0=x1[:, i, :], scalar=rstd, in1=gfin, op0=ALU.mult, op1=ALU.mult),
                   reads=[rkey, 'gfin'], writes=[('x1', par, i)])
                dma('sp', ('yout', par, i), lambda e, b=b: e.dma_start(out=out_d[b, tb * 128:(tb + 1) * 128, :], in_=x1[:, i, :]),
                    reads=[('x1', par, i)], final=True)

            def pro_epi(Tp, Te):
                stages = []
                if Tp is not None:
                    stages.append((0, lambda i: pro_A(Tp, i)))
                if Te is not None:
                    stages.append((0, lambda i: epi_A(Te, i)))
                    stages.append((1, lambda i: epi_B(Te, i)))
                if Tp is not None:
                    stages.append((1, lambda i: pro_B(Tp, i)))
                    stages.append((2, lambda i: pro_C(Tp, i)))
                    stages.append((2, lambda i: pro_D(Tp, i)))
                    stages.append((3, lambda i: pro_E(Tp, i)))
                stages.sort(key=lambda t: t[0])
                run_pipeline(list(range(4)), stages)

            ffn_dma_upto(1)
            for j0 in range(0, NJ, 2):
                dma('pool', ('wdn', j0), lambda e, j0=j0: e.dma_start(out=wdn[:, j0:j0 + 2, :], in_=wdn_v[:, j0:j0 + 2, :]), writes=[('wdn', j0), ('wdn', j0 + 1)])
            pro_epi(0, None)
            for T in range(NT):
                for j in range(NJ):
                    ffn_dma_upto(T * NJ + j + 2)
                    ffn_mmu(T, j)
                    if j >= 2:
                        ffn_conv(T, j - 2)
                ffn_conv(T, NJ - 2)
                ffn_conv(T, NJ - 1)
                if T + 1 < NT:
                    hw_ = halo[T % 2]
                    hkeys = [('halo', T % 2, cj) for cj in range(2 * NJ)]
                    op('dve', lambda e, hw_=hw_: e.tensor_tensor(out=HC[:, :, 0], in0=hw_[:, :, 1], in1=cw3[:, :, 1], op=ALU.mult), reads=hkeys + ['cw'], writes=['HC'])
                    op('dve', lambda e, hw_=hw_: e.tensor_tensor(out=hct, in0=hw_[:, :, 0], in1=cw3[:, :, 0], op=ALU.mult), reads=hkeys + ['cw'], writes=['hct'])
                    op('dve', lambda e: e.tensor_tensor(out=HC[:, :, 0], in0=HC[:, :, 0], in1=hct, op=ALU.add), reads=['hct', 'HC'], pwrites=['HC'])
                    op('dve', lambda e, hw_=hw_: e.tensor_tensor(out=HC[:, :, 1], in0=hw_[:, :, 1], in1=cw3[:, :, 0], op=ALU.mult), reads=hkeys + ['cw'], pwrites=['HC'])
                pro_epi(T + 1 if T + 1 < NT else None, T)

            S_.barrier()

        S_.emit()
    return nc


def _prep_shared(inputs):
    f32 = np.float32
    w_in = np.asarray(inputs["w_in"][0], f32)
    perm = np.concatenate([np.arange(0, 1536), np.arange(1536, 1920), np.arange(2176, 2208), np.arange(1920, 2176)])
    w_in_p = np.ascontiguousarray(w_in[:, perm])
    w_up = np.asarray(inputs["w_up"][0], f32)
    cols = np.concatenate([np.concatenate([np.arange(j * 128, (j + 1) * 128), DFF + np.arange(j * 128, (j + 1) * 128)]) for j in range(NJ)])
    w_up_p = np.ascontiguousarray(w_up[:, cols])
    conv_w = np.asarray(inputs["conv_w"][0], f32)
    conv_b = np.asarray(inputs["conv_b"][0], f32)
    cw = np.zeros((128, 2 * NJ, 4), f32)
    for j in range(NJ):
        for half in range(2):
            ch = half * DFF + j * 128 + np.arange(128)
            cw[:, j * 2 + half, 0] = conv_w[0, ch]
            cw[:, j * 2 + half, 1] = conv_w[1, ch]
            cw[:, j * 2 + half, 2] = conv_w[2, ch]
            cw[:, j * 2 + half, 3] = conv_b[ch]

    def bc(v):
        return np.ascontiguousarray(np.broadcast_to(np.asarray(v, f32)[None, :], (128, v.shape[0])))
    g_c = np.concatenate([np.asarray(inputs["g_cq"][0], f32), np.asarray(inputs["g_ckv"][0], f32)])
    g_o = np.concatenate([np.asarray(inputs["g_sb_out"][0], f32), np.asarray(inputs["g_mla_out"][0], f32)])
    ii = np.arange(128)
    ident = np.eye(128, dtype=f32)
    tri = (ii[:, None] >= ii[None, :]).astype(f32)
    ones = np.ones((128, 128), f32)
    mstrict = (ii[:, None] < ii[None, :]).astype(f32)
    mincl = (ii[:, None] <= ii[None, :]).astype(f32)
    cbf = np.concatenate([ident, tri, ones, mstrict, mincl, -tri, -ones], axis=1).astype(ml_dtypes.bfloat16)
    half = 16
    inv_freq = (1.0 / (np.float32(10000.0) ** (np.arange(half, dtype=f32) * np.float32(2.0 / 32)))).astype(f32)
    return {
        "w_in": w_in_p,
        "w_uq": np.ascontiguousarray(np.asarray(inputs["w_uq"][0], f32)),
        "w_ukv": np.ascontiguousarray(np.asarray(inputs["w_ukv"][0], f32)),
        "w_out": np.ascontiguousarray(np.asarray(inputs["w_out"][0], f32)),
        "w_up": w_up_p,
        "w_down": np.ascontiguousarray(np.asarray(inputs["w_down"][0], f32)),
        "g_mix": bc(inputs["g_mix"][0]),
        "g_c": bc(g_c),
        "g_o": bc(g_o),
        "g_ffn": bc(inputs["g_ffn"][0]),
        "g_final": bc(inputs["g_final"]),
        "cw": np.ascontiguousarray(cw.reshape(128, 2 * NJ * 4)),
        "cbf": cbf,
        "invf": bc(inv_freq),
    }


_PROG_CACHE = {}


def kernel(**inputs):
    x = np.asarray(inputs["x"], np.float32)
    pos = np.asarray(inputs["positions"], np.int32)
    B, S, _ = x.shape
    ncores = NCORES if B % NCORES == 0 else B
    nseq = B // ncores
    shared = _prep_shared(inputs)
    key = (S, nseq)
    if key not in _PROG_CACHE:
        _PROG_CACHE[key] = build_program(S, nseq)
    nc = _PROG_CACHE[key]
    NB = S // 128
    in_maps = []
    for c in range(ncores):
        m = dict(shared)
        m["x"] = np.ascontiguousarray(x[c * nseq:(c + 1) * nseq])
        p = pos[c * nseq:(c + 1) * nseq].reshape(nseq, NB, 128).transpose(0, 2, 1)
        m["pos"] = np.ascontiguousarray(p)
        in_maps.append(m)
    res = run_bass_kernel_spmd(nc, in_maps, core_ids=list(range(ncores)))
    if DEBUG:
        global LAST_RES
        LAST_RES = res
    out = np.concatenate([np.asarray(r["out"], np.float32) for r in res.results], axis=0)
    return out
```
